# Optimizing a Trainium2 kernel written in Bass

```python
import jax, jax.numpy as jnp
from jax import lax
import numpy as np

D_MODEL = 4096
BATCH = 4
SEQ = 4096
DEPTH = 2
DEC_BATCH = 2
DEC_SEQ = 4096
PAST_LEN = 128

GRID_W = 64
PLE_DIM = 256
NA_HEADS = 16
NA_HEAD_DIM = 128
NA_WIDTH = NA_HEADS * NA_HEAD_DIM
NA_WIN_ROWS = 8
NA_WIN_COLS = 16
RET_HEADS = 8
RET_QK_DIM = 128
RET_V_DIM = 256
RET_QK_WIDTH = RET_HEADS * RET_QK_DIM
RET_V_WIDTH = RET_HEADS * RET_V_DIM
RET_CHUNK = 128
ROPE_BASE = 10000.0
EPS = 1e-6
SPLITS = (NA_WIDTH, NA_WIDTH, NA_WIDTH, NA_WIDTH,
          RET_QK_WIDTH, RET_QK_WIDTH, RET_V_WIDTH, RET_V_WIDTH,
          D_MODEL, D_MODEL)
IN_WIDTH = sum(SPLITS)

kernel_name = 'hybrid_natten_retention_encoder'


def rmsnorm(x, g):
    xf = x.astype(jnp.float32)
    y = xf * lax.rsqrt(jnp.mean(xf * xf, axis=-1, keepdims=True) + EPS)
    return (y * g.astype(jnp.float32)).astype(x.dtype)


def rotary(t, positions):
    half = t.shape[-1] // 2
    inv = ROPE_BASE ** (-jnp.arange(half, dtype=jnp.float32) / half)
    ang = positions.astype(jnp.float32)[:, None] * inv[None, :]
    cos = jnp.cos(ang)[None, :, None, :]
    sin = jnp.sin(ang)[None, :, None, :]
    t1 = t[..., :half].astype(jnp.float32)
    t2 = t[..., half:].astype(jnp.float32)
    return jnp.concatenate([t1 * cos - t2 * sin, t1 * sin + t2 * cos], axis=-1).astype(t.dtype)


def neighborhood_attention(q, k, v, rpb):
    B, S, _ = q.shape
    rows = S // GRID_W
    wr = min(NA_WIN_ROWS, rows)
    grid = lambda t: t.reshape(B, rows, GRID_W, NA_HEADS, NA_HEAD_DIM)
    qg, kg, vg = grid(q), grid(k), grid(v)
    cols = jnp.arange(GRID_W)
    col_start = jnp.clip(cols - NA_WIN_COLS // 2, 0, GRID_W - NA_WIN_COLS)
    col_valid = (cols[None, :] >= col_start[:, None]) & (cols[None, :] < col_start[:, None] + NA_WIN_COLS)
    col_idx = jnp.clip(cols[None, :] - cols[:, None] + NA_WIN_COLS - 1, 0, 2 * NA_WIN_COLS - 2)
    scale = NA_HEAD_DIM ** -0.5

    def one_row(r):
        rs = jnp.clip(r - wr // 2, 0, rows - wr)
        q_r = lax.dynamic_index_in_dim(qg, r, axis=1, keepdims=False)
        k_r = lax.dynamic_slice_in_dim(kg, rs, wr, axis=1)
        v_r = lax.dynamic_slice_in_dim(vg, rs, wr, axis=1)
        s = jnp.einsum('bqhd,bwkhd->bhqwk', q_r, k_r, preferred_element_type=jnp.float32) * scale
        row_idx = rs + jnp.arange(wr) - r + NA_WIN_ROWS - 1
        bias = rpb[:, row_idx][:, :, col_idx].transpose(0, 2, 1, 3)
        s = s + bias.astype(jnp.float32)[None]
        s = jnp.where(col_valid[None, None, :, None, :], s, -jnp.inf)
        p = jax.nn.softmax(s, axis=(-2, -1))
        return jnp.einsum('bhqwk,bwkhd->bqhd', p.astype(v_r.dtype), v_r)

    out = lax.map(one_row, jnp.arange(rows))
    return out.transpose(1, 0, 2, 3, 4).reshape(B, S, NA_WIDTH)


def chunkwise_retention(q, k, v, log_decay):
    B, H, N, C, dk = q.shape
    dv = v.shape[-1]
    pos = jnp.arange(C, dtype=jnp.float32)
    ld = log_decay[:, None]
    diff = pos[:, None] - pos[None, :]
    decay_mat = jnp.where(diff[None] >= 0, jnp.exp(ld[:, :, None] * jnp.maximum(diff, 0.0)[None]), 0.0)
    scores = jnp.einsum('bhncd,bhnsd->bhncs', q, k) * decay_mat[None, :, None]
    inner = jnp.einsum('bhncs,bhnse->bhnce', scores, v)
    q_decay = jnp.exp(ld * (pos[None, :] + 1.0))
    k_decay = jnp.exp(ld * (C - 1.0 - pos[None, :]))
    chunk_decay = jnp.exp(log_decay * C)
    kv = jnp.einsum('bhncd,hc,bhnce->bhnde', k, k_decay, v)

    def step(state, kv_n):
        return state * chunk_decay[None, :, None, None] + kv_n, state

    init = jnp.zeros((B, H, dk, dv), jnp.float32)
    _, prev = lax.scan(step, init, jnp.moveaxis(kv, 2, 0))
    prev = jnp.moveaxis(prev, 0, 2)
    cross = jnp.einsum('bhncd,bhnde->bhnce', q * q_decay[None, :, None, :, None], prev)
    return inner + cross


def retention_branch(q, k, v, log_decay_fwd, log_decay_bwd, gn_gain):
    B, S, _ = q.shape
    n_chunks = S // RET_CHUNK
    pos = jnp.arange(S)
    q = rotary(q.reshape(B, S, RET_HEADS, RET_QK_DIM), pos)
    k = rotary(k.reshape(B, S, RET_HEADS, RET_QK_DIM), pos) * (RET_QK_DIM ** -0.5)
    v4 = v.reshape(B, S, RET_HEADS, RET_V_DIM)
    chunk = lambda t: t.astype(jnp.float32).reshape(B, n_chunks, RET_CHUNK, RET_HEADS, -1).transpose(0, 3, 1, 2, 4)
    unchunk = lambda t: t.transpose(0, 2, 3, 1, 4).reshape(B, S, RET_HEADS, RET_V_DIM)
    flip = lambda t: jnp.flip(t, axis=1)
    ld_f = -jnp.abs(log_decay_fwd.astype(jnp.float32))
    ld_b = -jnp.abs(log_decay_bwd.astype(jnp.float32))
    fwd = unchunk(chunkwise_retention(chunk(q), chunk(k), chunk(v4), ld_f))
    bwd = flip(unchunk(chunkwise_retention(chunk(flip(q)), chunk(flip(k)), chunk(flip(v4)), ld_b)))
    o = fwd + bwd
    o = o * lax.rsqrt(jnp.mean(o * o, axis=-1, keepdims=True) + EPS)
    o = o * gn_gain.astype(jnp.float32).reshape(RET_HEADS, RET_V_DIM)
    return o.reshape(B, S, RET_V_WIDTH).astype(v.dtype)


def encoder_layer(x, p_i, w_in, ln_pre, ln_post, na_rpb, ret_ld_f, ret_ld_b, ret_gn,
                  w_proj_a, w_proj_b, w_out, w_ple, w_ple_gate):
    xn = rmsnorm(x, ln_pre)
    proj = xn @ w_in
    offsets = [int(o) for o in np.cumsum(SPLITS)[:-1]]
    na_q, na_k, na_v, na_g, r_q, r_k, r_v, r_g, gate_a, gate_b = jnp.split(proj, offsets, axis=-1)
    a = neighborhood_attention(na_q, na_k, na_v, na_rpb) * jax.nn.silu(na_g)
    b = retention_branch(r_q, r_k, r_v, ret_ld_f, ret_ld_b, ret_gn) * jax.nn.silu(r_g)
    merged = jax.nn.sigmoid(gate_a) * (a @ w_proj_a) + jax.nn.sigmoid(gate_b) * (b @ w_proj_b)
    x = x + rmsnorm(merged @ w_out, ln_post)
    x = x + jax.nn.sigmoid(x @ w_ple_gate) * (p_i @ w_ple)
    return x


def trunk(x, p, w_in, ln_pre, ln_post, na_rpb, ret_log_decay_fwd, ret_log_decay_bwd, ret_gn_gain,
          w_proj_a, w_proj_b, w_out, w_ple, w_ple_gate):
    for i in range(DEPTH):
        x = encoder_layer(x, p[i], w_in[i], ln_pre[i], ln_post[i], na_rpb[i],
                          ret_log_decay_fwd[i], ret_log_decay_bwd[i], ret_gn_gain[i],
                          w_proj_a[i], w_proj_b[i], w_out[i], w_ple[i], w_ple_gate[i])
    return x


def setup_inputs(seed: int = 0) -> dict:
    key = jax.random.key(seed)
    ks = jax.random.split(key, 17)
    f32 = jnp.float32
    nrm = lambda k, shape, s: jax.random.normal(k, shape, f32) * s
    base_ld = jnp.asarray(np.log(1.0 - 2.0 ** (-5.0 - np.arange(RET_HEADS))), f32)
    return {
        'x_prompt': nrm(ks[0], (BATCH, SEQ, D_MODEL), 1.0),
        'x_sample': nrm(ks[1], (DEC_BATCH, DEC_SEQ, D_MODEL), 1.0),
        'p_prompt': nrm(ks[2], (DEPTH, BATCH, SEQ, PLE_DIM), 1.0),
        'p_sample': nrm(ks[3], (DEPTH, DEC_BATCH, DEC_SEQ, PLE_DIM), 1.0),
        'w_in': nrm(ks[4], (DEPTH, D_MODEL, IN_WIDTH), D_MODEL ** -0.5),
        'ln_pre': 1.0 + nrm(ks[5], (DEPTH, D_MODEL), 0.02),
        'ln_post': 1.0 + nrm(ks[6], (DEPTH, D_MODEL), 0.02),
        'na_rpb': nrm(ks[7], (DEPTH, NA_HEADS, 2 * NA_WIN_ROWS - 1, 2 * NA_WIN_COLS - 1), 0.1),
        'ret_log_decay_fwd': base_ld[None] * (1.0 + nrm(ks[8], (DEPTH, RET_HEADS), 0.05)),
        'ret_log_decay_bwd': base_ld[None] * (1.0 + nrm(ks[9], (DEPTH, RET_HEADS), 0.05)),
        'ret_gn_gain': 1.0 + nrm(ks[10], (DEPTH, RET_V_WIDTH), 0.02),
        'w_proj_a': nrm(ks[11], (DEPTH, NA_WIDTH, D_MODEL), NA_WIDTH ** -0.5),
        'w_proj_b': nrm(ks[12], (DEPTH, RET_V_WIDTH, D_MODEL), RET_V_WIDTH ** -0.5),
        'w_out': nrm(ks[13], (DEPTH, D_MODEL, D_MODEL), D_MODEL ** -0.5),
        'w_ple': nrm(ks[14], (DEPTH, PLE_DIM, D_MODEL), PLE_DIM ** -0.5),
        'w_ple_gate': nrm(ks[15], (DEPTH, D_MODEL, D_MODEL), D_MODEL ** -0.5),
    }


def reference(x_prompt, x_sample, p_prompt, p_sample, w_in, ln_pre, ln_post, na_rpb,
              ret_log_decay_fwd, ret_log_decay_bwd, ret_gn_gain, w_proj_a, w_proj_b,
              w_out, w_ple, w_ple_gate):
    y_prompt = trunk(x_prompt, p_prompt, w_in, ln_pre, ln_post, na_rpb, ret_log_decay_fwd,
                     ret_log_decay_bwd, ret_gn_gain, w_proj_a, w_proj_b, w_out, w_ple, w_ple_gate)
    y_sample = trunk(x_sample, p_sample, w_in, ln_pre, ln_post, na_rpb, ret_log_decay_fwd,
                     ret_log_decay_bwd, ret_gn_gain, w_proj_a, w_proj_b, w_out, w_ple, w_ple_gate)
    return (y_prompt, y_sample)
```

```python
import math
import os
CSTOP = int(os.environ.get('CSTOP', '9'))
from contextlib import ExitStack
import numpy as np
import ml_dtypes
import concourse.bass as bass
import concourse.mybir as mybir
from concourse.bass_utils import run_bass_kernel_spmd

F32 = mybir.dt.float32
BF16 = mybir.dt.bfloat16
U8 = mybir.dt.uint8
AF = mybir.ActivationFunctionType
ALU = mybir.AluOpType

S = 4096
D = 4096
NL = 2
INW = 22528
EPS = 1e-6
NA_SCALE = 128.0 ** -0.5
LN_RSCALE = -0.5 * math.log(128.0)

class Buf:
    def __init__(self, name, lo, hi):
        self.name, self.lo, self.hi = name, lo, hi
        self.init_r = []
        self.dead = False


class Sched:
    K = 20

    def __init__(self, nc, es):
        self.nc = nc
        self.engs = {"pe": nc.tensor, "act": nc.scalar, "dve": nc.vector, "pool": nc.gpsimd, "sp": nc.sync}
        self.sem = {}
        self.semobj = {}
        for e in ["pe", "act", "dve", "pool"]:
            self.semobj["c_" + e] = es.enter_context(nc.semaphore("c_" + e))
            self.sem[e] = "c_" + e
        self.rings = {}
        for q in ["sp", "pool"]:
            self.rings[q] = []
            for i in range(self.K):
                nm = "d_%s%d" % (q, i)
                self.semobj[nm] = es.enter_context(nc.semaphore(nm))
                self.rings[q].append(nm)
        self.cnt = {e: 0 for e in self.sem}
        self.dman = {"sp": 0, "pool": 0}
        self.prog = {e: [] for e in self.engs}
        self.waited = {e: {} for e in self.engs}
        self.res = {}
        self.bufs = []
        self.grave = []
        self.nops = 0

    def newbuf(self, name, lo, size):
        b = Buf(name, lo, lo + size)
        for o in self.bufs:
            if not o.dead and o.lo < b.hi and b.lo < o.hi:
                fin = {}
                def add(ev):
                    if ev is not None and fin.get(ev[0], 0) < ev[1]:
                        fin[ev[0]] = ev[1]
                dk = []
                for k, st in self.res.items():
                    if k[0] is o:
                        add(st["w"])
                        for ev in st["r"]:
                            add(ev)
                        dk.append(k)
                for k in dk:
                    del self.res[k]
                for ev in o.init_r:
                    add(ev)
                o.final = list(fin.items())
                o.dead = True
                self.grave.append(o)
        self.bufs = [o for o in self.bufs if not o.dead]
        fin = {}
        for g in self.grave:
            if g.lo < b.hi and b.lo < g.hi:
                for s, v in g.final:
                    if fin.get(s, 0) < v:
                        fin[s] = v
        b.init_r = list(fin.items())
        self.bufs.append(b)
        return b

    def _st(self, key):
        st = self.res.get(key)
        if st is None:
            b = key[0]
            init = list(b.init_r) if isinstance(b, Buf) else []
            st = {"w": None, "r": init}
            self.res[key] = st
        if isinstance(key[0], Buf):
            assert not key[0].dead, key[0].name
        return st

    def _deps(self, eng, reads, writes):
        deps = {}
        def add(ev):
            if ev is None:
                return
            s, v = ev
            if deps.get(s, 0) < v:
                deps[s] = v
        for k in reads:
            add(self._st(k)["w"])
        for k in writes:
            st = self._st(k)
            add(st["w"])
            for ev in st["r"]:
                add(ev)
        own = self.sem.get(eng)
        wd = self.waited[eng]
        for s, v in deps.items():
            if eng == "pe" and s == own:
                continue
            if wd.get(s, 0) >= v:
                continue
            wd[s] = v
            self.prog[eng].append(("w", s, v))

    def _record(self, ev, reads, writes):
        for k in reads:
            st = self._st(k)
            rl = st["r"]
            for i, (s, v) in enumerate(rl):
                if s == ev[0]:
                    if v < ev[1]:
                        rl[i] = ev
                    break
            else:
                rl.append(ev)
        for k in writes:
            st = self._st(k)
            st["w"] = ev
            st["r"] = []

    def op(self, eng, fn, reads=(), writes=(), signal=True):
        self.nops += 1
        if any(k[0] == "ps" for k in reads):
            writes = list(writes) + [k for k in reads if k[0] == "ps"]
            reads = [k for k in reads if k[0] != "ps"]
        self._deps(eng, reads, writes)
        s = self.sem[eng]
        if signal:
            self.cnt[eng] += 1
            ev = (s, self.cnt[eng])
            self.prog[eng].append(("o", fn, s))
        else:
            ev = (s, self.cnt[eng] + 1)
            self.prog[eng].append(("o", fn, None))
        self._record(ev, reads, writes)

    def dma(self, q, out, in_, reads=(), writes=()):
        self.nops += 1
        n = self.dman[q]
        self.dman[q] += 1
        s = self.rings[q][n % self.K]
        prev = 16 * (n // self.K)
        wd = self.waited[q]
        if prev > 0 and wd.get(s, 0) < prev:
            wd[s] = prev
            self.prog[q].append(("w", s, prev))
        self._deps(q, reads, writes)
        self.prog[q].append(("d", out, in_, s))
        ev = (s, prev + 16)
        self._record(ev, reads, writes)

    def finish(self):
        for q in ["sp", "pool"]:
            n = self.dman[q]
            for i in range(min(n, self.K)):
                cnt = (n - 1 - i) // self.K + 1
                self.prog[q].append(("w", self.rings[q][i], 16 * cnt))
        n = self.dman["pool"]
        for i in range(min(n, self.K)):
            cnt = (n - 1 - i) // self.K + 1
            self.prog["sp"].append(("w", self.rings["pool"][i], 16 * cnt))

    def emit(self, block):
        so = self.semobj

        def replay(name):
            def run(e):
                for it in self.prog[name]:
                    if it[0] == "w":
                        e.wait_ge(so[it[1]], it[2])
                    elif it[0] == "o":
                        ins = it[1](e)
                        if it[2] is not None:
                            ins.then_inc(so[it[2]], 1)
                    else:
                        e.dma_start(out=it[1], in_=it[2]).then_inc(so[it[3]], 16)
            return run
        block.tensor(replay("pe"))
        block.scalar(replay("act"))
        block.vector(replay("dve"))
        block.gpsimd(replay("pool"))
        block.sync(replay("sp"))


W_SPECS = [
    ("w_in", 4096, INW), ("w_pa", 2048, 4096), ("w_pb", 2048, 4096),
    ("w_out", 4096, 4096), ("w_pg", 4096, 4096), ("w_ple", 256, 4096),
]

CST_W = 2 + 6 * 128


def build_program(debug=False, phases="0ABC", nlayers=NL):
    nc = bass.Bass("TRN2", target_bir_lowering=False)
    es = ExitStack()
    I = {}

    def din(name, shape, dt=F32):
        I[name] = nc.dram_tensor(name, list(shape), dt, kind="ExternalInput").ap()
        return I[name]

    x_in = din("x", [S, D])
    p_in = din("p", [NL, S, 256])
    wsrc = {}
    for nm, K, N in W_SPECS:
        wsrc[nm] = din(nm, [NL, K, N])
    lnpreT = din("ln_preT", [NL, 128, 32])
    lnpost = din("ln_post", [NL, D])
    gn = din("gn", [NL, 2048])
    ldf = din("ldf", [NL, 8])
    ldb = din("ldb", [NL, 8])
    bias_tab = din("bias_tab", [NL, 16, 128, 1024])
    mask_tab = din("mask_tab", [128, 1024])
    cst_in = din("cst", [128, CST_W])
    rope_in = din("rope", [2, S, 64])
    ident_in = din("ident", [128, 128], BF16)
    ones_in = din("ones", [128, 128], BF16)
    y_out = nc.dram_tensor("y", [S, D], F32, kind="ExternalOutput").ap()

    def dscr(name, shape, dt=BF16):
        kind = "ExternalOutput" if (debug and name in ("sF", "nav", "rq", "rk", "rv", "aT", "bT", "xmid")) else "Internal"
        return nc.dram_tensor(name, list(shape), dt, kind=kind).ap()

    wb = {}
    for nm, K, N in W_SPECS:
        wb[nm] = [dscr("wb_%s%d" % (nm, l_), [N // 512, 128, (K // 128) * 512]) for l_ in range(NL)]
    sF = dscr("sF", [16384, S])
    nav = dscr("nav", [S, 2048])
    rq = dscr("rq", [S, 1024])
    rk = dscr("rk", [S, 1024])
    rv = dscr("rv", [S, 2048])
    aT = dscr("aT", [2048, S])
    bT = dscr("bT", [2048, S])
    xmid = dscr("xmid", [S, D], F32)

    ARENA = 200 * 1024
    arena = es.enter_context(nc.sbuf_tensor("arena", [128, ARENA], U8))
    banks = [es.enter_context(nc.psum_tensor("psb%d" % i, [128, 512], F32)) for i in range(8)]
    sc = Sched(nc, es)

    def view(b, dt, shape=None, off=0, n=None):
        esz = 4 if dt == F32 else 2
        lo = b.lo + off * esz
        hi = b.hi if n is None else lo + n * esz
        assert hi <= b.hi
        v = arena[:, lo:hi].bitcast(dt)
        if shape is not None:
            if len(shape) == 2:
                v = v.rearrange("p (a b) -> p a b", a=shape[0], b=shape[1])
            else:
                v = v.rearrange("p (a b c) -> p a b c", a=shape[0], b=shape[1], c=shape[2])
        return v

    def psf(b):
        return banks[b]

    def psb(b):
        return banks[b][:].bitcast(BF16)

    top = [ARENA]

    def palloc(name, size):
        top[0] -= size
        return sc.newbuf(name, top[0], size)

    B_ident = palloc("ident", 256)
    B_ones = palloc("ones", 256)
    B_cst = palloc("cst", CST_W * 4 + 8)
    B_mask = palloc("mask", 4096)
    B_small = palloc("small", 2048)
    ident = view(B_ident, BF16)
    ones = view(B_ones, BF16)
    cst = view(B_cst, F32, n=CST_W)
    maskv = view(B_mask, F32)
    smallv = view(B_small, F32)

    sc.dma("sp", ident, ident_in, writes=[(B_ident, 0)])
    sc.dma("sp", ones, ones_in, writes=[(B_ones, 0)])
    sc.dma("sp", cst, cst_in, writes=[(B_cst, 0)])
    sc.dma("sp", maskv, mask_tab, writes=[(B_mask, 0)])
    KI, KO, KC_, KM = (B_ident, 0), (B_ones, 0), (B_cst, 0), (B_mask, 0)
    colA, colB = cst[:, 0:1], cst[:, 1:2]
    rowA, rowB = cst[:, 2:130], cst[:, 130:258]
    M1, M2 = cst[:, 258:386], cst[:, 386:514]
    mge, mle = cst[:, 514:642], cst[:, 642:770]

    _sm = [0]

    def small(n):
        o = _sm[0]
        _sm[0] += n
        assert _sm[0] <= 512
        return o

    def phase0(layers):
        base = 0
        cin = [sc.newbuf("cin%d" % i, base + i * 16384, 16384) for i in range(3)]
        cout = [sc.newbuf("cout%d" % i, base + 3 * 16384 + i * 8192, 8192) for i in range(3)]
        it = 0
        for l in layers:
            for nm, K, N in W_SPECS:
                KC = K // 128
                G = min(8, KC)
                src = wsrc[nm][l].rearrange("(kc p) n -> p kc n", p=128)
                for cb in range(N // 512):
                    for kg in range(KC // G):
                        bi, bo = cin[it % 3], cout[it % 3]
                        vin = view(bi, F32, shape=(G, 512), n=G * 512)
                        vout = view(bo, BF16, shape=(G, 512), n=G * 512)
                        sc.dma("sp", vin, src[:, kg * G:(kg + 1) * G, cb * 512:(cb + 1) * 512], writes=[(bi, 0)])
                        if it % 2 == 0:
                            sc.op("dve", lambda e, o=vout, i=vin: e.tensor_copy(out=o, in_=i), reads=[(bi, 0)], writes=[(bo, 0)])
                        else:
                            sc.op("act", lambda e, o=vout, i=vin: e.copy(out=o, in_=i), reads=[(bi, 0)], writes=[(bo, 0)])
                        dst = wb[nm][l][cb][:, kg * G * 512:(kg + 1) * G * 512]
                        sc.dma("pool", dst, view(bo, BF16, n=G * 512), reads=[(bo, 0)], writes=[("wb", nm, l, cb, kg)])
                        it += 1

    def wkeys(nm, l, cb):
        K = dict((a, b) for a, b, c in W_SPECS)[nm]
        KC = K // 128
        G = min(8, KC)
        return [("wb", nm, l, cb, kg) for kg in range(KC // G)]

    def cb_info(cb):
        if cb < 4:
            return ("F", 0 + cb * 512, None)
        if cb < 8:
            return ("F", 2048 + (cb - 4) * 512, None)
        if cb < 12:
            return ("T", nav, (cb - 8) * 512, "copy")
        if cb < 16:
            return ("F", 4096 + (cb - 12) * 512, AF.Silu)
        if cb < 18:
            return ("T", rq, (cb - 16) * 512, "rot")
        if cb < 20:
            return ("T", rk, (cb - 18) * 512, "rot")
        if cb < 24:
            return ("T", rv, (cb - 20) * 512, "copy")
        if cb < 28:
            return ("F", 6144 + (cb - 24) * 512, AF.Silu)
        if cb < 36:
            return ("F", 8192 + (cb - 28) * 512, AF.Sigmoid)
        return ("F", 12288 + (cb - 36) * 512, AF.Sigmoid)

    def phaseA(l, xsrc, xsrc_keys):
        o = 0
        Wb = [sc.newbuf("A_w%d" % i, o + i * 32768, 32768) for i in range(2)]; o += 65536
        Bx = sc.newbuf("A_xnT", o, 32768); o += 32768
        Bxl = [sc.newbuf("A_xld%d" % i, o + i * 16384, 16384) for i in range(2)]; o += 32768
        Bxs = sc.newbuf("A_xs", o, 8192); o += 8192
        Brope = sc.newbuf("A_rope", o, 16384); o += 16384
        Bst = [sc.newbuf("A_st%d" % i, o + i * 4096, 4096) for i in range(3)]; o += 12288
        Btmp = [sc.newbuf("A_tmp%d" % i, o + i * 1024, 1024) for i in range(2)]; o += 2048
        Bg = sc.newbuf("A_g", o, 128); o += 128
        assert o <= top[0], (o, top[0])
        xnT = view(Bx, BF16, shape=(32, 512))
        ropev = view(Brope, F32, shape=(2, 32, 64))
        gT = view(Bg, F32)
        sc.dma("sp", gT, lnpreT[l], writes=[(Bg, 0)])
        sc.dma("sp", ropev[:, 0], rope_in[0].rearrange("(t p) d -> p t d", p=128), writes=[(Brope, 0)])
        sc.dma("sp", ropev[:, 1], rope_in[1].rearrange("(t p) d -> p t d", p=128), writes=[(Brope, 1)])
        ss_o = small(8)
        psrot = [2, 3, 4, 5, 6, 7]
        pi = [0]
        sti = [0]
        evi = [0]
        ldi = [0]
        for tt in range(8):
            for j in range(4):
                tok0 = tt * 512 + j * 128
                bl = Bxl[ldi[0] % 2]; ldi[0] += 1
                xl = view(bl, F32)
                xs = view(Bxs, BF16)
                sc.dma("sp", xl, xsrc[tok0:tok0 + 128, :], reads=xsrc_keys(tok0), writes=[(bl, 0)])
                ssv = smallv[:, ss_o:ss_o + 1]; rsv = smallv[:, ss_o + 1:ss_o + 2]; rstd = smallv[:, ss_o + 2:ss_o + 3]
                sc.op("act", lambda e, o_=xs, i=xl, a=ssv: e.activation(out=o_, in_=i, func=AF.Square, accum_out=a),
                      reads=[(bl, 0)], writes=[(Bxs, 0), (B_small, "ss")])
                sc.op("act", lambda e, o_=rsv, i=ssv: e.activation(out=o_, in_=i, func=AF.Sqrt, scale=1.0 / D, bias=EPS),
                      reads=[(B_small, "ss")], writes=[(B_small, "rs")])
                sc.op("dve", lambda e, o_=rstd, i=rsv: e.reciprocal(out=o_, in_=i), reads=[(B_small, "rs")], writes=[(B_small, "rstd")])
                sc.op("act", lambda e, o_=xs, i=xl, s_=rstd: e.activation(out=o_, in_=i, func=AF.Copy, scale=s_),
                      reads=[(bl, 0), (B_small, "rstd")], writes=[(Bxs, 0)])
                for g in range(8):
                    pb = g % 2
                    pv = psb(pb)
                    for q in range(4):
                        kc = g * 4 + q
                        sc.op("pe", lambda e, o_=pv[:, q * 128:(q + 1) * 128], i=xs[:, kc * 128:(kc + 1) * 128]: e.transpose(out=o_, in_=i, identity=ident),
                              reads=[(Bxs, 0), KI], writes=[("ps", pb)], signal=(q == 3))
                    for q in range(4):
                        kc = g * 4 + q
                        dst = xnT[:, kc, j * 128:(j + 1) * 128]
                        srcp = pv[:, q * 128:(q + 1) * 128]
                        if g % 2 == 0:
                            sc.op("dve", lambda e, o_=dst, i=srcp, s_=gT[:, kc:kc + 1]: e.tensor_scalar(out=o_, in0=i, scalar1=s_, scalar2=None, op0=ALU.mult),
                                  reads=[("ps", pb), (Bg, 0)], writes=[(Bx, j)])
                        else:
                            sc.op("act", lambda e, o_=dst, i=srcp, s_=gT[:, kc:kc + 1]: e.activation(out=o_, in_=i, func=AF.Copy, scale=s_),
                                  reads=[("ps", pb), (Bg, 0)], writes=[(Bx, j)])
            xkeys = [(Bx, j) for j in range(4)]
            for cb in range(INW // 512):
                info = cb_info(cb)
                wslot = Wb[cb % 2]
                wv = view(wslot, BF16, shape=(32, 512))
                sc.dma("sp", view(wslot, BF16), wb["w_in"][l][cb], reads=wkeys("w_in", l, cb), writes=[(wslot, 0)])
                bst = Bst[sti[0] % 3]; sti[0] += 1
                stv = view(bst, BF16, shape=(4, 512))
                for sub in range(4):
                    pb = psrot[pi[0] % 6]; pi[0] += 1
                    ps = psf(pb)
                    for kc in range(32):
                        if info[0] == "F":
                            lhsT, rhs = wv[:, kc, sub * 128:(sub + 1) * 128], xnT[:, kc, :]
                        else:
                            lhsT, rhs = xnT[:, kc, sub * 128:(sub + 1) * 128], wv[:, kc, :]
                        sc.op("pe", lambda e, o_=ps[:], a=lhsT, b=rhs, st=(kc == 0), sp_=(kc == 31): e.matmul(o_, a, b, start=st, stop=sp_),
                              reads=[(wslot, 0)] + xkeys, writes=[("ps", pb)], signal=(kc == 31))
                    dst = stv[:, sub, :]
                    if info[0] == "F" or info[3] == "copy":
                        fn = info[2] if info[0] == "F" else None
                        if fn is None:
                            if evi[0] % 2 == 0:
                                sc.op("dve", lambda e, o_=dst, i=ps[:]: e.tensor_copy(out=o_, in_=i), reads=[("ps", pb)], writes=[(bst, sub)])
                            else:
                                sc.op("act", lambda e, o_=dst, i=ps[:]: e.copy(out=o_, in_=i), reads=[("ps", pb)], writes=[(bst, sub)])
                            evi[0] += 1
                        else:
                            sc.op("act", lambda e, o_=dst, i=ps[:], f=fn: e.activation(out=o_, in_=i, func=f), reads=[("ps", pb)], writes=[(bst, sub)])
                    else:
                        ti = tt * 4 + sub
                        p3 = ps[:].rearrange("p (h d) -> p h d", h=4, d=128)
                        d3 = dst.rearrange("p (h d) -> p h d", h=4, d=128)
                        cosb = ropev[:, 0, ti, :].unsqueeze(1).to_broadcast([128, 4, 64])
                        sinb = ropev[:, 1, ti, :].unsqueeze(1).to_broadcast([128, 4, 64])
                        tA = view(Btmp[0], F32, shape=(4, 64)); tB = view(Btmp[1], F32, shape=(4, 64))
                        t1, t2 = p3[:, :, 0:64], p3[:, :, 64:128]
                        rk_ = [(Brope, 0), (Brope, 1)]
                        sc.op("dve", lambda e, o_=tA, a=t1, b=cosb: e.tensor_tensor(out=o_, in0=a, in1=b, op=ALU.mult), reads=[("ps", pb)] + rk_, writes=[(Btmp[0], 0)])
                        sc.op("dve", lambda e, o_=tB, a=t2, b=sinb: e.tensor_tensor(out=o_, in0=a, in1=b, op=ALU.mult), reads=[("ps", pb)] + rk_, writes=[(Btmp[1], 0)])
                        sc.op("dve", lambda e, o_=d3[:, :, 0:64], a=tA, b=tB: e.tensor_tensor(out=o_, in0=a, in1=b, op=ALU.subtract),
                              reads=[(Btmp[0], 0), (Btmp[1], 0)], writes=[(bst, sub)])
                        sc.op("dve", lambda e, o_=tA, a=t1, b=sinb: e.tensor_tensor(out=o_, in0=a, in1=b, op=ALU.mult), reads=[("ps", pb)] + rk_, writes=[(Btmp[0], 0)])
                        sc.op("dve", lambda e, o_=tB, a=t2, b=cosb: e.tensor_tensor(out=o_, in0=a, in1=b, op=ALU.mult), reads=[("ps", pb)] + rk_, writes=[(Btmp[1], 0)])
                        sc.op("dve", lambda e, o_=d3[:, :, 64:128], a=tA, b=tB: e.tensor_tensor(out=o_, in0=a, in1=b, op=ALU.add),
                              reads=[(Btmp[0], 0), (Btmp[1], 0)], writes=[(bst, sub)])
                skeys = [(bst, s_) for s_ in range(4)]
                if info[0] == "F":
                    r0 = info[1]
                    dstd = sF[r0:r0 + 512, tt * 512:(tt + 1) * 512].rearrange("(s p) t -> p s t", p=128)
                    wk = [("sF", r0 // 128 + s_, tt) for s_ in range(4)]
                else:
                    c0 = info[2]
                    dstd = info[1][tt * 512:(tt + 1) * 512, c0:c0 + 512].rearrange("(j p) c -> p j c", p=128)
                    wk = [("TM", cb, tt)]
                sc.dma("pool", dstd, stv, reads=skeys, writes=wk)

    def sF_keys(rowblk):
        return [("sF", rowblk, tt) for tt in range(8)]

    def tm_keys(cbs):
        return [("TM", cb, tt) for cb in cbs for tt in range(8)]

    def phaseB1(l):
        o = 0
        sets = []
        for i in range(2):
            d = {}
            for nm, sz in [("kT", 8192), ("qT", 8192), ("v", 8192), ("vsh", 8192), ("gT", 8192), ("aT", 8192), ("E", 4096)]:
                d[nm] = sc.newbuf("B1_%s%d" % (nm, i), o, sz); o += sz
            sets.append(d)
        Bpe = [sc.newbuf("B1_pexp%d" % i, o + i * 1024, 1024) for i in range(2)]; o += 2048
        Bp = [sc.newbuf("B1_p%d" % i, o + i * 512, 512) for i in range(2)]; o += 1024
        Brc = [sc.newbuf("B1_rc%d" % i, o + i * 256, 256) for i in range(2)]; o += 512
        assert o <= top[0]
        it = [0]
        for h in range(16):
            d = sets[h % 2]
            kT = view(d["kT"], BF16); qT = view(d["qT"], BF16); gT = view(d["gT"], BF16); aTh = view(d["aT"], BF16)
            v = view(d["v"], BF16, shape=(32, 128)); vsh = view(d["vsh"], BF16, shape=(32, 128))
            E = view(d["E"], F32)
            E3 = view(d["E"], F32, shape=(2, 512))
            sc.dma("sp", qT, sF[h * 128:(h + 1) * 128, :], reads=sF_keys(h), writes=[(d["qT"], 0)])
            sc.dma("sp", kT, sF[2048 + h * 128:2048 + (h + 1) * 128, :], reads=sF_keys(16 + h), writes=[(d["kT"], 0)])
            sc.dma("sp", gT, sF[4096 + h * 128:4096 + (h + 1) * 128, :], reads=sF_keys(32 + h), writes=[(d["gT"], 0)])
            vk = tm_keys([8 + h // 4])
            sc.dma("sp", v, nav[:, h * 128:(h + 1) * 128].rearrange("(c p) d -> p c d", p=128), reads=vk, writes=[(d["v"], 0)])
            sc.dma("sp", vsh[:, 0:31, :], nav[64:64 + 31 * 128, h * 128:(h + 1) * 128].rearrange("(c p) d -> p c d", p=128), reads=vk, writes=[(d["vsh"], 0)])
            sc.dma("sp", E, bias_tab[l, h], writes=[(d["E"], 0)])
            sc.op("act", lambda e, o_=E: e.activation(out=o_, in_=o_, func=AF.Exp), reads=[(d["E"], 0)], writes=[(d["E"], 0)])
            sc.op("dve", lambda e, o_=E: e.tensor_tensor(out=o_, in0=o_, in1=maskv, op=ALU.mult), reads=[(d["E"], 0), KM], writes=[(d["E"], 0)])
            for r in range(64):
                rs = min(max(r - 4, 0), 56)
                base = rs - r + 7
                tab, i0 = base % 2, base // 2
                k = it[0]; it[0] += 1
                sb_, ob_ = k % 2, 2 + k % 2
                Sps, Ops = psf(sb_), psf(ob_)
                for m in range(4):
                    kr = rs + 2 * m
                    sc.op("pe", lambda e, o_=Sps[:, m * 64:(m + 1) * 64], a=kT[:, kr * 64:kr * 64 + 128], b=qT[:, r * 64:(r + 1) * 64]:
                          e.matmul(o_, a, b, start=True, stop=True),
                          reads=[(d["kT"], 0), (d["qT"], 0)], writes=[("ps", sb_)], signal=(m == 3))
                pe_ = view(Bpe[k % 2], F32)
                pp = view(Bp[k % 2], BF16)
                sc.op("act", lambda e, o_=pe_, i=Sps[:, 0:256]: e.activation(out=o_, in_=i, func=AF.Exp, scale=NA_SCALE),
                      reads=[("ps", sb_)], writes=[(Bpe[k % 2], 0)])
                sc.op("dve", lambda e, o_=pp, a=pe_, b=E3[:, tab, i0 * 64:(i0 + 4) * 64]: e.tensor_tensor(out=o_, in0=a, in1=b, op=ALU.mult),
                      reads=[(Bpe[k % 2], 0), (d["E"], 0)], writes=[(Bp[k % 2], 0)])
                for m in range(4):
                    kr = rs + 2 * m
                    vc = v[:, kr // 2, :] if kr % 2 == 0 else vsh[:, (kr - 1) // 2, :]
                    sc.op("pe", lambda e, o_=Ops[:, 0:64], a=vc, b=pp[:, m * 64:(m + 1) * 64], st=(m == 0), sp_=(m == 3): e.matmul(o_, a, b, start=st, stop=sp_),
                          reads=[(d["v"], 0), (d["vsh"], 0), (Bp[k % 2], 0)], writes=[("ps", ob_)], signal=False)
                for m in range(4):
                    sc.op("pe", lambda e, o_=Ops[:, 64:128], b=pp[:, m * 64:(m + 1) * 64], st=(m == 0), sp_=(m == 3): e.matmul(o_, ones, b, start=st, stop=sp_),
                          reads=[KO, (Bp[k % 2], 0)], writes=[("ps", ob_)], signal=(m == 3))
                rc = view(Brc[k % 2], F32)
                sc.op("dve", lambda e, o_=rc, i=Ops[:, 64:128]: e.reciprocal(out=o_, in_=i), reads=[("ps", ob_)], writes=[(Brc[k % 2], 0)])
                sc.op("dve", lambda e, o_=rc, a=rc, b=gT[:, r * 64:(r + 1) * 64]: e.tensor_tensor(out=o_, in0=a, in1=b, op=ALU.mult),
                      reads=[(Brc[k % 2], 0), (d["gT"], 0)], writes=[(Brc[k % 2], 0)])
                sc.op("dve", lambda e, o_=aTh[:, r * 64:(r + 1) * 64], a=Ops[:, 0:64], b=rc: e.tensor_tensor(out=o_, in0=a, in1=b, op=ALU.mult),
                      reads=[("ps", ob_), (Brc[k % 2], 0)], writes=[(d["aT"], 0)])
            sc.dma("pool", aT[h * 128:(h + 1) * 128, :], aTh, reads=[(d["aT"], 0)], writes=[("aT", h)])

    def phaseB2(l):
        o = 0
        def nb(nm, sz):
            nonlocal o
            b = sc.newbuf("B2_" + nm, o, sz); o += sz
            return b
        Brq, Brk, Brv, Brg, Bbt = nb("rq", 8192), nb("rk", 8192), nb("rv", 16384), nb("rg", 16384), nb("bt", 16384)
        BqT, BkT, BKf, BKb, BQf, BQb = [nb(n_, 8192) for n_ in ("qT", "kT", "Kf", "Kb", "Qf", "Qb")]
        BSf, BSb = nb("Sf", 16384), nb("Sb", 16384)
        Bqdf, Bqdb, BDT, Bgn = nb("qdf", 4096), nb("qdb", 4096), nb("DT", 4096), nb("gn", 8192)
        Btb = nb("tb", 512)
        Bstate = [nb("st%d" % i, 1024) for i in range(2)]
        Bt1, Bt2 = nb("t1", 512), nb("t2", 512)
        By = [nb("y%d" % i, 512) for i in range(2)]
        Bpt = [nb("pt%d" % i, 256) for i in range(2)]
        Bjk = nb("junk", 512)
        assert o <= top[0], (o, top[0])
        tb = view(Btb, F32)
        nldf, nldb, kdf, kdb, gcf, gcb = [tb[:, i * 8:(i + 1) * 8] for i in range(6)]
        qdf = view(Bqdf, F32, shape=(8, 128)); qdb = view(Bqdb, F32, shape=(8, 128)); DT = view(BDT, F32, shape=(8, 128))
        gnv = view(Bgn, F32)
        t1 = view(Bt1, F32); t2 = view(Bt2, F32)
        sc.dma("sp", nldf, ldf[l:l + 1, :].partition_broadcast(128), writes=[(Btb, "f")])
        sc.dma("sp", nldb, ldb[l:l + 1, :].partition_broadcast(128), writes=[(Btb, "b")])
        sc.dma("sp", gnv, gn[l:l + 1, :].partition_broadcast(128), writes=[(Bgn, 0)])
        for nm, ap_ in (("f", nldf), ("b", nldb)):
            sc.op("act", lambda e, o_=ap_: e.activation(out=o_, in_=o_, func=AF.Abs), reads=[(Btb, nm)], writes=[(Btb, nm)])
            sc.op("dve", lambda e, o_=ap_: e.tensor_scalar(out=o_, in0=o_, scalar1=-1.0, scalar2=None, op0=ALU.mult),
                  reads=[(Btb, nm)], writes=[(Btb, nm)])
        sc.op("act", lambda e: e.activation(out=kdf, in_=nldf, func=AF.Exp, scale=colA, bias=LN_RSCALE), reads=[(Btb, "f"), KC_], writes=[(Btb, "kdf")])
        sc.op("act", lambda e: e.activation(out=kdb, in_=nldb, func=AF.Exp, scale=colB, bias=LN_RSCALE), reads=[(Btb, "b"), KC_], writes=[(Btb, "kdb")])
        sc.op("act", lambda e: e.activation(out=gcf, in_=nldf, func=AF.Exp, scale=128.0), reads=[(Btb, "f")], writes=[(Btb, "gcf")])
        sc.op("act", lambda e: e.activation(out=gcb, in_=nldb, func=AF.Exp, scale=128.0), reads=[(Btb, "b")], writes=[(Btb, "gcb")])
        for h in range(8):
            sc.op("act", lambda e, o_=qdf[:, h, :], s_=nldf[:, h:h + 1]: e.activation(out=o_, in_=rowA, func=AF.Exp, scale=s_), reads=[(Btb, "f"), KC_], writes=[(Bqdf, h)])
            sc.op("act", lambda e, o_=qdb[:, h, :], s_=nldb[:, h:h + 1]: e.activation(out=o_, in_=rowB, func=AF.Exp, scale=s_), reads=[(Btb, "b"), KC_], writes=[(Bqdb, h)])
            sc.op("act", lambda e, s_=nldf[:, h:h + 1]: e.activation(out=t1, in_=M1, func=AF.Exp, scale=s_, bias=LN_RSCALE), reads=[(Btb, "f"), KC_], writes=[(Bt1, 0)])
            sc.op("dve", lambda e: e.tensor_tensor(out=t1, in0=t1, in1=mge, op=ALU.mult), reads=[(Bt1, 0), KC_], writes=[(Bt1, 0)])
            sc.op("act", lambda e, s_=nldb[:, h:h + 1]: e.activation(out=t2, in_=M2, func=AF.Exp, scale=s_, bias=LN_RSCALE), reads=[(Btb, "b"), KC_], writes=[(Bt2, 0)])
            sc.op("dve", lambda e: e.tensor_tensor(out=t2, in0=t2, in1=mle, op=ALU.mult), reads=[(Bt2, 0), KC_], writes=[(Bt2, 0)])
            sc.op("dve", lambda e, o_=DT[:, h, :]: e.tensor_tensor(out=o_, in0=t1, in1=t2, op=ALU.add), reads=[(Bt1, 0), (Bt2, 0)], writes=[(BDT, h)])
        pk = [0]
        for h in range(8):
            rqv = view(Brq, BF16, shape=(32, 128)); rkv = view(Brk, BF16, shape=(32, 128)); rvv = view(Brv, BF16, shape=(32, 256))
            rgv = view(Brg, BF16, shape=(2, S)); btv = view(Bbt, BF16, shape=(2, S))
            qT = view(BqT, BF16); kT = view(BkT, BF16)
            Kf = view(BKf, BF16, shape=(32, 128)); Kb = view(BKb, BF16, shape=(32, 128))
            Qf = view(BQf, BF16, shape=(32, 128)); Qb = view(BQb, BF16, shape=(32, 128))
            Sf = view(BSf, BF16, shape=(32, 256)); Sb = view(BSb, BF16, shape=(32, 256))
            sc.dma("sp", rqv, rq[:, h * 128:(h + 1) * 128].rearrange("(c p) d -> p c d", p=128), reads=tm_keys([16 + h // 4]), writes=[(Brq, 0)])
            sc.dma("sp", rkv, rk[:, h * 128:(h + 1) * 128].rearrange("(c p) d -> p c d", p=128), reads=tm_keys([18 + h // 4]), writes=[(Brk, 0)])
            sc.dma("sp", rvv, rv[:, h * 256:(h + 1) * 256].rearrange("(c p) d -> p c d", p=128), reads=tm_keys([20 + h // 2]), writes=[(Brv, 0)])
            sc.dma("sp", rgv, sF[6144 + h * 256:6144 + (h + 1) * 256, :].rearrange("(j p) t -> p j t", p=128),
                   reads=sF_keys(48 + 2 * h) + sF_keys(49 + 2 * h), writes=[(Brg, 0)])
            for (src3, srck, dstv, dstk) in ((rqv, Brq, qT, BqT), (rkv, Brk, kT, BkT)):
                for g in range(8):
                    pb = pk[0] % 2; pk[0] += 1
                    pv = psb(pb)
                    for q_ in range(4):
                        c = g * 4 + q_
                        sc.op("pe", lambda e, o_=pv[:, q_ * 128:(q_ + 1) * 128], i=src3[:, c, :]: e.transpose(out=o_, in_=i, identity=ident),
                              reads=[(srck, 0), KI], writes=[("ps", pb)], signal=(q_ == 3))
                    if g % 2 == 0:
                        sc.op("dve", lambda e, o_=dstv[:, g * 512:(g + 1) * 512], i=pv[:, 0:512]: e.tensor_copy(out=o_, in_=i), reads=[("ps", pb)], writes=[(dstk, g)])
                    else:
                        sc.op("act", lambda e, o_=dstv[:, g * 512:(g + 1) * 512], i=pv[:, 0:512]: e.copy(out=o_, in_=i), reads=[("ps", pb)], writes=[(dstk, g)])
            qTk = [(BqT, g) for g in range(8)]; kTk = [(BkT, g) for g in range(8)]
            rk2 = view(Brk, BF16)
            sc.op("dve", lambda e, o_=view(BKf, BF16), s_=kdf[:, h:h + 1]: e.tensor_scalar(out=o_, in0=rk2, scalar1=s_, scalar2=None, op0=ALU.mult),
                  reads=[(Brk, 0), (Btb, "kdf")], writes=[(BKf, 0)])
            sc.op("dve", lambda e, o_=view(BKb, BF16), s_=kdb[:, h:h + 1]: e.tensor_scalar(out=o_, in0=rk2, scalar1=s_, scalar2=None, op0=ALU.mult),
                  reads=[(Brk, 0), (Btb, "kdb")], writes=[(BKb, 0)])
            qT3 = view(BqT, BF16, shape=(32, 128))
            sc.op("dve", lambda e, b=qdf[:, h, :].unsqueeze(1).to_broadcast([128, 32, 128]): e.tensor_tensor(out=Qf, in0=qT3, in1=b, op=ALU.mult),
                  reads=qTk + [(Bqdf, h)], writes=[(BQf, 0)])
            sc.op("dve", lambda e, b=qdb[:, h, :].unsqueeze(1).to_broadcast([128, 32, 128]): e.tensor_tensor(out=Qb, in0=qT3, in1=b, op=ALU.mult),
                  reads=qTk + [(Bqdb, h)], writes=[(BQb, 0)])
            for (Kd, Kdk, Sd, Sdk, gc, order) in ((Kf, BKf, Sf, BSf, gcf, list(range(32))), (Kb, BKb, Sb, BSb, gcb, list(range(31, -1, -1)))):
                first = order[0]
                sc.op("dve", lambda e, o_=Sd[:, first, :]: e.memset(o_, 0.0), writes=[(Sdk, first)])
                prev_state = None
                for idx in range(31):
                    n = order[idx]; nxt = order[idx + 1]
                    pb = 4 + pk[0] % 2; pk[0] += 1
                    ps = psf(pb)
                    sc.op("pe", lambda e, o_=ps[:, 0:256], a=Kd[:, n, :], b=rvv[:, n, :]: e.matmul(o_, a, b, start=True, stop=True),
                          reads=[(Kdk, 0), (Brv, 0)], writes=[("ps", pb)])
                    bs = Bstate[idx % 2]
                    stv_ = view(bs, F32)
                    if prev_state is None:
                        sc.op("dve", lambda e, o_=stv_, i=ps[:, 0:256]: e.tensor_copy(out=o_, in_=i), reads=[("ps", pb)], writes=[(bs, 0)])
                    else:
                        pst = view(prev_state, F32)
                        sc.op("dve", lambda e, o_=stv_, a=pst, s_=gc[:, h:h + 1], b=ps[:, 0:256]: e.scalar_tensor_tensor(out=o_, in0=a, scalar=s_, in1=b, op0=ALU.mult, op1=ALU.add),
                              reads=[(prev_state, 0), ("ps", pb), (Btb, "gcf"), (Btb, "gcb")], writes=[(bs, 0)])
                    sc.op("act", lambda e, o_=Sd[:, nxt, :], i=stv_: e.copy(out=o_, in_=i), reads=[(bs, 0)], writes=[(Sdk, nxt)])
                    prev_state = bs
            ss_o = 16
            for n in range(32):
                pb = 2 + n % 2
                ps = psf(pb)
                sc.op("pe", lambda e, o_=ps[:, 0:128], a=kT[:, n * 128:(n + 1) * 128], b=qT[:, n * 128:(n + 1) * 128]: e.matmul(o_, a, b, start=True, stop=True),
                      reads=kTk + qTk, writes=[("ps", pb)])
                pt = view(Bpt[n % 2], BF16)
                sc.op("dve", lambda e, o_=pt, a=ps[:, 0:128], b=DT[:, h, :]: e.tensor_tensor(out=o_, in0=a, in1=b, op=ALU.mult),
                      reads=[("ps", pb), (BDT, h)], writes=[(Bpt[n % 2], 0)])
                ob = 6 + n % 2
                po = psf(ob)
                sc.op("pe", lambda e, o_=po[:, 0:256], a=pt, b=rvv[:, n, :]: e.matmul(o_, a, b, start=True, stop=False),
                      reads=[(Bpt[n % 2], 0), (Brv, 0)], writes=[("ps", ob)], signal=False)
                sc.op("pe", lambda e, o_=po[:, 0:256], a=Qf[:, n, :], b=Sf[:, n, :]: e.matmul(o_, a, b, start=False, stop=False),
                      reads=[(BQf, 0), (BSf, n)], writes=[("ps", ob)], signal=False)
                sc.op("pe", lambda e, o_=po[:, 0:256], a=Qb[:, n, :], b=Sb[:, n, :]: e.matmul(o_, a, b, start=False, stop=True),
                      reads=[(BQb, 0), (BSb, n)], writes=[("ps", ob)])
                ssv = smallv[:, ss_o:ss_o + 1]; rsv = smallv[:, ss_o + 1:ss_o + 2]; rstd = smallv[:, ss_o + 2:ss_o + 3]
                jk = view(Bjk, BF16)
                sc.op("act", lambda e, i=po[:, 0:256]: e.activation(out=jk, in_=i, func=AF.Square, accum_out=ssv), reads=[("ps", ob)], writes=[(Bjk, 0), (B_small, "ss2")])
                sc.op("act", lambda e: e.activation(out=rsv, in_=ssv, func=AF.Sqrt, scale=1.0 / 256, bias=EPS), reads=[(B_small, "ss2")], writes=[(B_small, "rs2")])
                sc.op("dve", lambda e: e.reciprocal(out=rstd, in_=rsv), reads=[(B_small, "rs2")], writes=[(B_small, "rstd2")])
                yb = By[n % 2]
                yv = view(yb, BF16)
                sc.op("dve", lambda e, o_=yv, a=po[:, 0:256], b=gnv[:, h * 256:(h + 1) * 256]: e.scalar_tensor_tensor(out=o_, in0=a, scalar=rstd, in1=b, op0=ALU.mult, op1=ALU.mult),
                      reads=[("ps", ob), (B_small, "rstd2"), (Bgn, 0)], writes=[(yb, 0)])
                tb_ = n % 2
                pv = psb(tb_)
                for j in range(2):
                    sc.op("pe", lambda e, o_=pv[:, j * 128:(j + 1) * 128], i=yv[:, j * 128:(j + 1) * 128]: e.transpose(out=o_, in_=i, identity=ident),
                          reads=[(yb, 0), KI], writes=[("ps", tb_)], signal=(j == 1))
                sc.op("dve", lambda e, o_=btv[:, :, n * 128:(n + 1) * 128], a=pv[:, 0:256].rearrange("p (j t) -> p j t", j=2, t=128), b=rgv[:, :, n * 128:(n + 1) * 128]:
                      e.tensor_tensor(out=o_, in0=a, in1=b, op=ALU.mult),
                      reads=[("ps", tb_), (Brg, 0)], writes=[(Bbt, 0)])
            sc.dma("pool", bT[h * 256:(h + 1) * 256, :].rearrange("(j p) t -> p j t", p=128), btv, reads=[(Bbt, 0)], writes=[("bT", h)])

    def phaseC(l, xsrc, xsrc_keys, ydst, ykey):
        T = 256
        o = 0
        def nb(nm, sz):
            nonlocal o
            b = sc.newbuf("C_" + nm, o, sz); o += sz
            return b
        Wb = [nb("w%d" % i, 32768) for i in range(2)]
        Bmo = nb("mo", 32768)
        Bxin = nb("xin", 16384)
        Blp = nb("lnpost", 16384)
        Bmg = nb("merged", 16384)
        Bab = nb("ab", 16384)
        Bx1b = nb("x1b", 8192)
        Bgt = [nb("gt%d" % i, 4096) for i in range(2)]
        Bm1 = [nb("m1_%d" % i, 1024) for i in range(2)]
        Bm2 = [nb("m2_%d" % i, 1024) for i in range(2)]
        Bsg = [nb("sg%d" % i, 2048) for i in range(2)]
        Bpl = nb("pl", 2048)
        Bplb = nb("plb", 1024)
        BpT = nb("pT", 1024)
        Bjk = nb("junk", 1024)
        assert o <= top[0], (o, top[0])
        lpv = view(Blp, F32)
        for hh in range(2):
            sc.dma("sp", lpv[:, hh * 2048:(hh + 1) * 2048], lnpost[l:l + 1, hh * 2048:(hh + 1) * 2048].partition_broadcast(128), writes=[(Blp, hh)])
        mo = view(Bmo, F32, shape=(2, D))
        mg = view(Bmg, BF16, shape=(32, T))
        wi = [0]
        pi = [0]
        ssq_o = 32
        def wload(nm, cb):
            slot = Wb[wi[0] % 2]; wi[0] += 1
            KC = dict((a, b // 128) for a, b, c in W_SPECS)[nm]
            sc.dma("sp", view(slot, BF16, n=KC * 512), wb[nm][l][cb], reads=wkeys(nm, l, cb), writes=[(slot, 0)])
            return slot, view(slot, BF16, shape=(KC, 512), n=KC * 512)
        for tt in range(S // T):
            t0 = tt * T
            aTv = view(Bab, BF16, shape=(16, T), n=16 * T)
            bTv = view(Bab, BF16, shape=(16, T), off=16 * T, n=16 * T)
            sc.dma("sp", aTv, aT[:, t0:t0 + T].rearrange("(k p) t -> p k t", p=128), reads=[("aT", h) for h in range(16)], writes=[(Bab, 0)])
            sc.dma("sp", bTv, bT[:, t0:t0 + T].rearrange("(k p) t -> p k t", p=128), reads=[("bT", h) for h in range(8)], writes=[(Bab, 1)])
            if CSTOP < 1:
                continue
            for cb in range(8):
                sa, wa = wload("w_pa", cb)
                sb2, wbv = wload("w_pb", cb)
                gslot = Bgt[cb % 2]
                gv = view(gslot, BF16, shape=(2, 4, T))
                r0 = 8192 + cb * 512
                sc.dma("sp", gv[:, 0], sF[r0:r0 + 512, t0:t0 + T].rearrange("(s p) t -> p s t", p=128), reads=[("sF", r0 // 128 + s_, t0 // 512) for s_ in range(4)], writes=[(gslot, 0)])
                r1 = 12288 + cb * 512
                sc.dma("sp", gv[:, 1], sF[r1:r1 + 512, t0:t0 + T].rearrange("(s p) t -> p s t", p=128), reads=[("sF", r1 // 128 + s_, t0 // 512) for s_ in range(4)], writes=[(gslot, 1)])
                for sub in range(4):
                    fb = cb * 4 + sub
                    pa = 2 + pi[0] % 3; pbk = 5 + pi[0] % 3; pi[0] += 1
                    for kc in range(16):
                        sc.op("pe", lambda e, o_=psf(pa)[:, 0:T], a=wa[:, kc, sub * 128:(sub + 1) * 128], b=aTv[:, kc, :], st=(kc == 0), sp_=(kc == 15): e.matmul(o_, a, b, start=st, stop=sp_),
                              reads=[(sa, 0), (Bab, 0)], writes=[("ps", pa)], signal=(kc == 15))
                    for kc in range(16):
                        sc.op("pe", lambda e, o_=psf(pbk)[:, 0:T], a=wbv[:, kc, sub * 128:(sub + 1) * 128], b=bTv[:, kc, :], st=(kc == 0), sp_=(kc == 15): e.matmul(o_, a, b, start=st, stop=sp_),
                              reads=[(sb2, 0), (Bab, 1)], writes=[("ps", pbk)], signal=(kc == 15))
                    m1 = view(Bm1[fb % 2], F32); m2 = view(Bm2[fb % 2], F32)
                    sc.op("dve", lambda e, o_=m1, a=psf(pa)[:, 0:T], b=gv[:, 0, sub, :]: e.tensor_tensor(out=o_, in0=a, in1=b, op=ALU.mult),
                          reads=[("ps", pa), (gslot, 0)], writes=[(Bm1[fb % 2], 0)])
                    sc.op("dve", lambda e, o_=m2, a=psf(pbk)[:, 0:T], b=gv[:, 1, sub, :]: e.tensor_tensor(out=o_, in0=a, in1=b, op=ALU.mult),
                          reads=[("ps", pbk), (gslot, 1)], writes=[(Bm2[fb % 2], 0)])
                    sc.op("dve", lambda e, o_=mg[:, fb, :], a=m1, b=m2: e.tensor_tensor(out=o_, in0=a, in1=b, op=ALU.add),
                          reads=[(Bm1[fb % 2], 0), (Bm2[fb % 2], 0)], writes=[(Bmg, fb)])
            mgk = [(Bmg, fb) for fb in range(32)]
            if CSTOP < 2:
                continue
            for cb in range(8):
                sw, wv = wload("w_out", cb)
                for j in range(2):
                    pb = 2 + pi[0] % 6; pi[0] += 1
                    ps = psf(pb)
                    for kc in range(32):
                        sc.op("pe", lambda e, o_=ps[:], a=mg[:, kc, j * 128:(j + 1) * 128], b=wv[:, kc, :], st=(kc == 0), sp_=(kc == 31): e.matmul(o_, a, b, start=st, stop=sp_),
                              reads=[(sw, 0)] + mgk, writes=[("ps", pb)], signal=(kc == 31))
                    sc.op("dve", lambda e, o_=mo[:, j, cb * 512:(cb + 1) * 512], i=ps[:]: e.tensor_copy(out=o_, in_=i), reads=[("ps", pb)], writes=[(Bmo, (j, cb))])
                    sq = smallv[:, ssq_o + j * 8 + cb:ssq_o + j * 8 + cb + 1]
                    sc.op("act", lambda e, i=mo[:, j, cb * 512:(cb + 1) * 512], a=sq: e.activation(out=view(Bjk, BF16), in_=i, func=AF.Square, accum_out=a),
                          reads=[(Bmo, (j, cb))], writes=[(Bjk, 0), (B_small, ("ssq", j, cb))])
            if CSTOP < 3:
                continue
            for j in range(2):
                tok0 = t0 + j * 128
                tot = smallv[:, 48 + j:49 + j]; rs_ = smallv[:, 50 + j:51 + j]; rstd = smallv[:, 52 + j:53 + j]
                sc.op("dve", lambda e, o_=tot, i=smallv[:, ssq_o + j * 8:ssq_o + j * 8 + 8]: e.tensor_reduce(out=o_, in_=i, axis=mybir.AxisListType.X, op=ALU.add),
                      reads=[(B_small, ("ssq", j, cb)) for cb in range(8)], writes=[(B_small, ("tot", j))])
                sc.op("act", lambda e, o_=rs_, i=tot: e.activation(out=o_, in_=i, func=AF.Sqrt, scale=1.0 / D, bias=EPS), reads=[(B_small, ("tot", j))], writes=[(B_small, ("rs", j))])
                sc.op("dve", lambda e, o_=rstd, i=rs_: e.reciprocal(out=o_, in_=i), reads=[(B_small, ("rs", j))], writes=[(B_small, ("rstd", j))])
                xin = view(Bxin, F32)
                sc.dma("sp", xin, xsrc[tok0:tok0 + 128, :], reads=xsrc_keys(tok0), writes=[(Bxin, 0)])
                mok = [(Bmo, (j, cb)) for cb in range(8)]
                sc.op("dve", lambda e, o_=mo[:, j, :], s_=rstd: e.scalar_tensor_tensor(out=o_, in0=o_, scalar=s_, in1=lpv, op0=ALU.mult, op1=ALU.mult),
                      reads=mok + [(B_small, ("rstd", j)), (Blp, 0), (Blp, 1)], writes=mok)
                sc.op("dve", lambda e, o_=mo[:, j, :], b=xin: e.tensor_tensor(out=o_, in0=o_, in1=b, op=ALU.add), reads=mok + [(Bxin, 0)], writes=mok)
                x1b = view(Bx1b, BF16)
                sc.op("act", lambda e, i=mo[:, j, :]: e.copy(out=x1b, in_=i), reads=mok, writes=[(Bx1b, 0)])
                x1T = view(Bab, BF16, shape=(32, T))
                for g in range(8):
                    pb = g % 2
                    pv = psb(pb)
                    for q in range(4):
                        kc = g * 4 + q
                        sc.op("pe", lambda e, o_=pv[:, q * 128:(q + 1) * 128], i=x1b[:, kc * 128:(kc + 1) * 128]: e.transpose(out=o_, in_=i, identity=ident),
                              reads=[(Bx1b, 0), KI], writes=[("ps", pb)], signal=(q == 3))
                    dst = x1T[:, g * 4:(g + 1) * 4, j * 128:(j + 1) * 128]
                    srcp = pv[:, 0:512].rearrange("p (q t) -> p q t", q=4, t=128)
                    wkk = [(Bab, 0), (Bab, 1)]
                    if g % 2 == 0:
                        sc.op("dve", lambda e, o_=dst, i=srcp: e.tensor_copy(out=o_, in_=i), reads=[("ps", pb)], writes=wkk)
                    else:
                        sc.op("act", lambda e, o_=dst, i=srcp: e.copy(out=o_, in_=i), reads=[("ps", pb)], writes=wkk)
                pl = view(Bpl, F32, n=256)
                plb = view(Bplb, BF16, n=256)
                pT = view(BpT, BF16, shape=(2, T))
                sc.dma("sp", pl, p_in[l, tok0:tok0 + 128, :], writes=[(Bpl, 0)])
                sc.op("act", lambda e: e.copy(out=plb, in_=pl), reads=[(Bpl, 0)], writes=[(Bplb, 0)])
                pv = psb(1)
                for q in range(2):
                    sc.op("pe", lambda e, o_=pv[:, q * 128:(q + 1) * 128], i=plb[:, q * 128:(q + 1) * 128]: e.transpose(out=o_, in_=i, identity=ident),
                          reads=[(Bplb, 0), KI], writes=[("ps", 1)], signal=(q == 1))
                sc.op("dve", lambda e, o_=pT[:, :, j * 128:(j + 1) * 128], i=pv[:, 0:256].rearrange("p (q t) -> p q t", q=2, t=128): e.tensor_copy(out=o_, in_=i),
                      reads=[("ps", 1)], writes=[(BpT, j)])
            x1T = view(Bab, BF16, shape=(32, T))
            x1k = [(Bab, 0), (Bab, 1)]
            if CSTOP < 4:
                continue
            for cb in range(8):
                sg_, wg = wload("w_pg", cb)
                sp2, wp = wload("w_ple", cb)
                for j in range(2):
                    pb = 2 + pi[0] % 6; pi[0] += 1
                    ps = psf(pb)
                    for kc in range(32):
                        sc.op("pe", lambda e, o_=ps[:], a=x1T[:, kc, j * 128:(j + 1) * 128], b=wg[:, kc, :], st=(kc == 0), sp_=(kc == 31): e.matmul(o_, a, b, start=st, stop=sp_),
                              reads=[(sg_, 0)] + x1k, writes=[("ps", pb)], signal=(kc == 31))
                    pb2 = 2 + pi[0] % 6; pi[0] += 1
                    ps2 = psf(pb2)
                    pT = view(BpT, BF16, shape=(2, T))
                    for kc in range(2):
                        sc.op("pe", lambda e, o_=ps2[:], a=pT[:, kc, j * 128:(j + 1) * 128], b=wp[:, kc, :], st=(kc == 0), sp_=(kc == 1): e.matmul(o_, a, b, start=st, stop=sp_),
                              reads=[(sp2, 0), (BpT, j)], writes=[("ps", pb2)], signal=(kc == 1))
                    sgb = Bsg[(cb * 2 + j) % 2]
                    sgv = view(sgb, F32)
                    sc.op("act", lambda e, o_=sgv, i=ps[:]: e.activation(out=o_, in_=i, func=AF.Sigmoid), reads=[("ps", pb)], writes=[(sgb, 0)])
                    sc.op("dve", lambda e, o_=sgv, b=ps2[:]: e.tensor_tensor(out=o_, in0=o_, in1=b, op=ALU.mult), reads=[(sgb, 0), ("ps", pb2)], writes=[(sgb, 0)])
                    xs_ = mo[:, j, cb * 512:(cb + 1) * 512]
                    sc.op("dve", lambda e, o_=xs_, b=sgv: e.tensor_tensor(out=o_, in0=o_, in1=b, op=ALU.add), reads=[(sgb, 0), (Bmo, (j, cb))], writes=[(Bmo, (j, cb))])
            for j in range(2):
                tok0 = t0 + j * 128
                sc.dma("pool", ydst[tok0:tok0 + 128, :], mo[:, j, :], reads=[(Bmo, (j, cb)) for cb in range(8)], writes=[(ykey, tok0 // 128)])

    layers = list(range(nlayers))
    if "0" in phases:
        phase0(layers)
    for l in layers:
        if l == 0:
            xs_ap, xs_keys = x_in, (lambda tok0: [])
        else:
            xs_ap, xs_keys = xmid, (lambda tok0: [("xmid", tok0 // 128)])
        if "A" in phases:
            phaseA(l, xs_ap, xs_keys)
        if "B" in phases:
            phaseB1(l)
            phaseB2(l)
        if "C" in phases:
            if l == nlayers - 1:
                phaseC(l, xs_ap, xs_keys, y_out, "y")
            else:
                phaseC(l, xs_ap, xs_keys, xmid, "xmid")
    sc.finish()
    block = es.enter_context(nc.Block())
    sc.emit(block)
    es.close()
    return nc, sc


def _const_tables():
    c = np.arange(128, dtype=np.float32)
    cst = np.zeros((128, CST_W), np.float32)
    cst[:, 0] = 127.0 - c
    cst[:, 1] = c
    t = np.arange(128, dtype=np.float32)
    cst[:, 2:130] = (t + 1.0)[None, :]
    cst[:, 130:258] = (128.0 - t)[None, :]
    s_ = c[:, None]
    tt = t[None, :]
    cst[:, 258:386] = np.maximum(tt - s_, 0.0)
    cst[:, 386:514] = np.maximum(s_ - tt, 0.0)
    cst[:, 514:642] = (tt >= s_).astype(np.float32)
    cst[:, 642:770] = (s_ >= tt).astype(np.float32)
    half = 64
    inv = (10000.0 ** (-np.arange(half, dtype=np.float32) / half)).astype(np.float32)
    ang = np.arange(S, dtype=np.float32)[:, None] * inv[None, :]
    rope = np.stack([np.cos(ang), np.sin(ang)]).astype(np.float32)
    cols = np.arange(64)
    cs = np.clip(cols - 8, 0, 48)
    valid = (cols[None, :] >= cs[:, None]) & (cols[None, :] < cs[:, None] + 16)
    m = valid.T.astype(np.float32)
    mask = np.tile(np.tile(m, (2, 1))[:, None, :], (1, 16, 1)).reshape(128, 1024)
    return cst, rope, np.ascontiguousarray(mask)


def _bias_table(rpb):
    cols = np.arange(64)
    cidx = np.clip(cols[:, None] - cols[None, :] + 15, 0, 30)
    out = np.zeros((NL, 16, 128, 2, 8, 64), np.float32)
    for tab in range(2):
        for i in range(8):
            for jp in range(2):
                ri = 2 * i + tab + jp
                if ri > 14:
                    continue
                out[:, :, jp * 64:(jp + 1) * 64, tab, i, :] = rpb[:, :, ri][:, :, cidx]
    return out.reshape(NL, 16, 128, 1024)


_CACHE = {}


def kernel(x_prompt, x_sample, p_prompt, p_sample, w_in, ln_pre, ln_post, na_rpb,
           ret_log_decay_fwd, ret_log_decay_bwd, ret_gn_gain, w_proj_a, w_proj_b,
           w_out, w_ple, w_ple_gate):
    if "nc" not in _CACHE:
        _CACHE["nc"] = build_program()[0]
    nc = _CACHE["nc"]
    f = lambda a: np.ascontiguousarray(np.asarray(a, dtype=np.float32))
    cst, rope, mask = _const_tables()
    shared = {
        "w_in": f(w_in), "w_pa": f(w_proj_a), "w_pb": f(w_proj_b), "w_out": f(w_out),
        "w_pg": f(w_ple_gate), "w_ple": f(w_ple),
        "ln_preT": np.ascontiguousarray(f(ln_pre).reshape(NL, 32, 128).transpose(0, 2, 1)),
        "ln_post": f(ln_post), "gn": f(ret_gn_gain), "ldf": f(ret_log_decay_fwd), "ldb": f(ret_log_decay_bwd),
        "bias_tab": _bias_table(f(na_rpb)), "mask_tab": mask, "cst": cst, "rope": rope,
        "ident": np.eye(128, dtype=np.float32).astype(ml_dtypes.bfloat16),
        "ones": np.ones((128, 128), np.float32).astype(ml_dtypes.bfloat16),
    }
    xp, xs_, pp, ps_ = f(x_prompt), f(x_sample), f(p_prompt), f(p_sample)
    seqs = [(xp[i], pp[:, i]) for i in range(4)] + [(xs_[i], ps_[:, i]) for i in range(2)]
    seqs = seqs + [seqs[0], seqs[1]]
    in_maps = []
    for c in range(8):
        d = dict(shared)
        d["x"] = np.ascontiguousarray(seqs[c][0])
        d["p"] = np.ascontiguousarray(seqs[c][1])
        in_maps.append(d)
    res = run_bass_kernel_spmd(nc, in_maps, core_ids=list(range(8)))
    ys = [np.asarray(res.results[c]["y"], dtype=np.float32) for c in range(6)]
    y_prompt = np.stack(ys[0:4])
    y_sample = np.stack(ys[4:6])
    return (y_prompt, y_sample)
```

```python
import math
import os
CSTOP = int(os.environ.get('CSTOP', '9'))
from contextlib import ExitStack
import numpy as np
import ml_dtypes
import concourse.bass as bass
import concourse.mybir as mybir
from concourse.bass_utils import run_bass_kernel_spmd

F32 = mybir.dt.float32
BF16 = mybir.dt.bfloat16
U8 = mybir.dt.uint8
AF = mybir.ActivationFunctionType
ALU = mybir.AluOpType

S = 4096
D = 4096
NL = 2
INW = 22528
EPS = 1e-6
NA_SCALE = 128.0 ** -0.5
LN_RSCALE = -0.5 * math.log(128.0)

class Buf:
    def __init__(self, name, lo, hi):
        self.name, self.lo, self.hi = name, lo, hi
        self.init_r = []
        self.dead = False


class Sched:
    K = 20

    def __init__(self, nc, es):
        self.nc = nc
        self.engs = {"pe": nc.tensor, "act": nc.scalar, "dve": nc.vector, "pool": nc.gpsimd, "sp": nc.sync}
        self.sem = {}
        self.semobj = {}
        for e in ["pe", "act", "dve", "pool"]:
            self.semobj["c_" + e] = es.enter_context(nc.semaphore("c_" + e))
            self.sem[e] = "c_" + e
        self.rings = {}
        for q in ["sp", "pool"]:
            self.rings[q] = []
            for i in range(self.K):
                nm = "d_%s%d" % (q, i)
                self.semobj[nm] = es.enter_context(nc.semaphore(nm))
                self.rings[q].append(nm)
        self.cnt = {e: 0 for e in self.sem}
        self.dman = {"sp": 0, "pool": 0}
        self.prog = {e: [] for e in self.engs}
        self.waited = {e: {} for e in self.engs}
        self.res = {}
        self.bufs = []
        self.grave = []
        self.nops = 0

    def newbuf(self, name, lo, size):
        b = Buf(name, lo, lo + size)
        for o in self.bufs:
            if not o.dead and o.lo < b.hi and b.lo < o.hi:
                fin = {}
                def add(ev):
                    if ev is not None and fin.get(ev[0], 0) < ev[1]:
                        fin[ev[0]] = ev[1]
                dk = []
                for k, st in self.res.items():
                    if k[0] is o:
                        add(st["w"])
                        for ev in st["r"]:
                            add(ev)
                        dk.append(k)
                for k in dk:
                    del self.res[k]
                for ev in o.init_r:
                    add(ev)
                o.final = list(fin.items())
                o.dead = True
                self.grave.append(o)
        self.bufs = [o for o in self.bufs if not o.dead]
        fin = {}
        for g in self.grave:
            if g.lo < b.hi and b.lo < g.hi:
                for s, v in g.final:
                    if fin.get(s, 0) < v:
                        fin[s] = v
        b.init_r = list(fin.items())
        self.bufs.append(b)
        return b

    def _st(self, key):
        st = self.res.get(key)
        if st is None:
            b = key[0]
            init = list(b.init_r) if isinstance(b, Buf) else []
            st = {"w": None, "r": init}
            self.res[key] = st
        if isinstance(key[0], Buf):
            assert not key[0].dead, key[0].name
        return st

    def _deps(self, eng, reads, writes):
        deps = {}
        def add(ev):
            if ev is None:
                return
            s, v = ev
            if deps.get(s, 0) < v:
                deps[s] = v
        for k in reads:
            add(self._st(k)["w"])
        for k in writes:
            st = self._st(k)
            add(st["w"])
            for ev in st["r"]:
                add(ev)
        own = self.sem.get(eng)
        wd = self.waited[eng]
        for s, v in deps.items():
            if eng == "pe" and s == own:
                continue
            if wd.get(s, 0) >= v:
                continue
            wd[s] = v
            self.prog[eng].append(("w", s, v))

    def _record(self, ev, reads, writes):
        for k in reads:
            st = self._st(k)
            rl = st["r"]
            for i, (s, v) in enumerate(rl):
                if s == ev[0]:
                    if v < ev[1]:
                        rl[i] = ev
                    break
            else:
                rl.append(ev)
        for k in writes:
            st = self._st(k)
            st["w"] = ev
            st["r"] = []

    def op(self, eng, fn, reads=(), writes=(), signal=True):
        self.nops += 1
        if any(k[0] == "ps" for k in reads):
            writes = list(writes) + [k for k in reads if k[0] == "ps"]
            reads = [k for k in reads if k[0] != "ps"]
        self._deps(eng, reads, writes)
        s = self.sem[eng]
        if signal:
            self.cnt[eng] += 1
            ev = (s, self.cnt[eng])
            self.prog[eng].append(("o", fn, s))
        else:
            ev = (s, self.cnt[eng] + 1)
            self.prog[eng].append(("o", fn, None))
        self._record(ev, reads, writes)

    def dma(self, q, out, in_, reads=(), writes=()):
        self.nops += 1
        n = self.dman[q]
        self.dman[q] += 1
        s = self.rings[q][n % self.K]
        prev = 16 * (n // self.K)
        wd = self.waited[q]
        if prev > 0 and wd.get(s, 0) < prev:
            wd[s] = prev
            self.prog[q].append(("w", s, prev))
        self._deps(q, reads, writes)
        self.prog[q].append(("d", out, in_, s))
        ev = (s, prev + 16)
        self._record(ev, reads, writes)

    def finish(self):
        for q in ["sp", "pool"]:
            n = self.dman[q]
            for i in range(min(n, self.K)):
                cnt = (n - 1 - i) // self.K + 1
                self.prog[q].append(("w", self.rings[q][i], 16 * cnt))
        n = self.dman["pool"]
        for i in range(min(n, self.K)):
            cnt = (n - 1 - i) // self.K + 1
            self.prog["sp"].append(("w", self.rings["pool"][i], 16 * cnt))

    def emit(self, block):
        so = self.semobj

        def replay(name):
            def run(e):
                for it in self.prog[name]:
                    if it[0] == "w":
                        e.wait_ge(so[it[1]], it[2])
                    elif it[0] == "o":
                        ins = it[1](e)
                        if it[2] is not None:
                            ins.then_inc(so[it[2]], 1)
                    else:
                        e.dma_start(out=it[1], in_=it[2]).then_inc(so[it[3]], 16)
            return run
        block.tensor(replay("pe"))
        block.scalar(replay("act"))
        block.vector(replay("dve"))
        block.gpsimd(replay("pool"))
        block.sync(replay("sp"))


W_SPECS = [
    ("w_in", 4096, INW), ("w_pa", 2048, 4096), ("w_pb", 2048, 4096),
    ("w_out", 4096, 4096), ("w_pg", 4096, 4096), ("w_ple", 256, 4096),
]

CST_W = 2 + 6 * 128


def build_program(debug=False, phases="0ABC", nlayers=NL):
    nc = bass.Bass("TRN2", target_bir_lowering=False)
    es = ExitStack()
    I = {}

    def din(name, shape, dt=F32):
        I[name] = nc.dram_tensor(name, list(shape), dt, kind="ExternalInput").ap()
        return I[name]

    x_in = din("x", [S, D])
    p_in = din("p", [NL, S, 256])
    wsrc = {}
    for nm, K, N in W_SPECS:
        wsrc[nm] = din(nm, [NL, K, N])
    lnpreT = din("ln_preT", [NL, 128, 32])
    lnpost = din("ln_post", [NL, D])
    gn = din("gn", [NL, 2048])
    ldf = din("ldf", [NL, 8])
    ldb = din("ldb", [NL, 8])
    bias_tab = din("bias_tab", [NL, 16, 128, 1024])
    mask_tab = din("mask_tab", [128, 1024])
    cst_in = din("cst", [128, CST_W])
    rope_in = din("rope", [2, S, 64])
    ident_in = din("ident", [128, 128], BF16)
    ones_in = din("ones", [128, 128], BF16)
    y_out = nc.dram_tensor("y", [S, D], F32, kind="ExternalOutput").ap()

    def dscr(name, shape, dt=BF16):
        kind = "ExternalOutput" if (debug and name in ("sF", "nav", "rq", "rk", "rv", "aT", "bT", "xmid")) else "Internal"
        return nc.dram_tensor(name, list(shape), dt, kind=kind).ap()

    wb = {}
    for nm, K, N in W_SPECS:
        wb[nm] = [dscr("wb_%s%d" % (nm, l_), [N // 512, 128, (K // 128) * 512]) for l_ in range(NL)]
    sF = dscr("sF", [16384, S])
    nav = dscr("nav", [S, 2048])
    rq = dscr("rq", [S, 1024])
    rk = dscr("rk", [S, 1024])
    rv = dscr("rv", [S, 2048])
    aT = dscr("aT", [2048, S])
    bT = dscr("bT", [2048, S])
    xmid = dscr("xmid", [S, D], F32)

    ARENA = 204 * 1024
    arena = es.enter_context(nc.sbuf_tensor("arena", [128, ARENA], U8))
    banks = [es.enter_context(nc.psum_tensor("psb%d" % i, [128, 512], F32)) for i in range(8)]
    sc = Sched(nc, es)

    def view(b, dt, shape=None, off=0, n=None):
        esz = 4 if dt == F32 else 2
        lo = b.lo + off * esz
        hi = b.hi if n is None else lo + n * esz
        assert hi <= b.hi
        v = arena[:, lo:hi].bitcast(dt)
        if shape is not None:
            if len(shape) == 2:
                v = v.rearrange("p (a b) -> p a b", a=shape[0], b=shape[1])
            else:
                v = v.rearrange("p (a b c) -> p a b c", a=shape[0], b=shape[1], c=shape[2])
        return v

    def psf(b):
        return banks[b]

    def psb(b):
        return banks[b][:].bitcast(BF16)

    top = [ARENA]

    def palloc(name, size):
        top[0] -= size
        return sc.newbuf(name, top[0], size)

    B_ident = palloc("ident", 256)
    B_ones = palloc("ones", 256)
    B_cst = palloc("cst", CST_W * 4 + 8)
    B_mask = palloc("mask", 4096)
    B_small = palloc("small", 2048)
    ident = view(B_ident, BF16)
    ones = view(B_ones, BF16)
    cst = view(B_cst, F32, n=CST_W)
    maskv = view(B_mask, F32)
    smallv = view(B_small, F32)

    sc.dma("sp", ident, ident_in, writes=[(B_ident, 0)])
    sc.dma("sp", ones, ones_in, writes=[(B_ones, 0)])
    sc.dma("sp", cst, cst_in, writes=[(B_cst, 0)])
    sc.dma("sp", maskv, mask_tab, writes=[(B_mask, 0)])
    KI, KO, KC_, KM = (B_ident, 0), (B_ones, 0), (B_cst, 0), (B_mask, 0)
    colA, colB = cst[:, 0:1], cst[:, 1:2]
    rowA, rowB = cst[:, 2:130], cst[:, 130:258]
    M1, M2 = cst[:, 258:386], cst[:, 386:514]
    mge, mle = cst[:, 514:642], cst[:, 642:770]

    _sm = [0]

    def small(n):
        o = _sm[0]
        _sm[0] += n
        assert _sm[0] <= 512
        return o

    def phase0(layers):
        base = 0
        cin = [sc.newbuf("cin%d" % i, base + i * 16384, 16384) for i in range(3)]
        cout = [sc.newbuf("cout%d" % i, base + 3 * 16384 + i * 8192, 8192) for i in range(3)]
        it = 0
        for l in layers:
            for nm, K, N in W_SPECS:
                KC = K // 128
                G = min(8, KC)
                src = wsrc[nm][l].rearrange("(kc p) n -> p kc n", p=128)
                for cb in range(N // 512):
                    for kg in range(KC // G):
                        bi, bo = cin[it % 3], cout[it % 3]
                        vin = view(bi, F32, shape=(G, 512), n=G * 512)
                        vout = view(bo, BF16, shape=(G, 512), n=G * 512)
                        sc.dma("sp", vin, src[:, kg * G:(kg + 1) * G, cb * 512:(cb + 1) * 512], writes=[(bi, 0)])
                        if it % 2 == 0:
                            sc.op("dve", lambda e, o=vout, i=vin: e.tensor_copy(out=o, in_=i), reads=[(bi, 0)], writes=[(bo, 0)])
                        else:
                            sc.op("act", lambda e, o=vout, i=vin: e.copy(out=o, in_=i), reads=[(bi, 0)], writes=[(bo, 0)])
                        dst = wb[nm][l][cb][:, kg * G * 512:(kg + 1) * G * 512]
                        sc.dma("pool", dst, view(bo, BF16, n=G * 512), reads=[(bo, 0)], writes=[("wb", nm, l, cb, kg)])
                        it += 1

    def wkeys(nm, l, cb):
        K = dict((a, b) for a, b, c in W_SPECS)[nm]
        KC = K // 128
        G = min(8, KC)
        return [("wb", nm, l, cb, kg) for kg in range(KC // G)]

    def cb_info(cb):
        if cb < 4:
            return ("F", 0 + cb * 512, None)
        if cb < 8:
            return ("F", 2048 + (cb - 4) * 512, None)
        if cb < 12:
            return ("T", nav, (cb - 8) * 512, "copy")
        if cb < 16:
            return ("F", 4096 + (cb - 12) * 512, AF.Silu)
        if cb < 18:
            return ("T", rq, (cb - 16) * 512, "rot")
        if cb < 20:
            return ("T", rk, (cb - 18) * 512, "rot")
        if cb < 24:
            return ("T", rv, (cb - 20) * 512, "copy")
        if cb < 28:
            return ("F", 6144 + (cb - 24) * 512, AF.Silu)
        if cb < 36:
            return ("F", 8192 + (cb - 28) * 512, AF.Sigmoid)
        return ("F", 12288 + (cb - 36) * 512, AF.Sigmoid)

    def phaseA(l, xsrc, xsrc_keys):
        o = 0
        Wb = [sc.newbuf("A_w%d" % i, o + i * 32768, 32768) for i in range(2)]; o += 65536
        Bxx = [sc.newbuf("A_xnT%d" % i, o + i * 32768, 32768) for i in range(2)]; o += 65536
        Bxl = [sc.newbuf("A_xld%d" % i, o + i * 16384, 16384) for i in range(1)]; o += 16384
        Bxs = sc.newbuf("A_xs", o, 8192); o += 8192
        Brope = sc.newbuf("A_rope", o, 16384); o += 16384
        Bst = [sc.newbuf("A_st%d" % i, o + i * 4096, 4096) for i in range(3)]; o += 12288
        Btmp = [sc.newbuf("A_tmp%d" % i, o + i * 1024, 1024) for i in range(2)]; o += 2048
        Bg = sc.newbuf("A_g", o, 128); o += 128
        assert o <= top[0], (o, top[0])
        ropev = view(Brope, F32, shape=(2, 32, 64))
        gT = view(Bg, F32)
        sc.dma("sp", gT, lnpreT[l], writes=[(Bg, 0)])
        sc.dma("sp", ropev[:, 0], rope_in[0].rearrange("(t p) d -> p t d", p=128), writes=[(Brope, 0)])
        sc.dma("sp", ropev[:, 1], rope_in[1].rearrange("(t p) d -> p t d", p=128), writes=[(Brope, 1)])
        ss_o = small(8)
        psrot = [2, 3, 4, 5, 6, 7]
        pi = [0]
        sti = [0]
        evi = [0]
        ldi = [0]
        def pro_norm(tt, j):
            tok0 = tt * 512 + j * 128
            bl = Bxl[0]
            xl = view(bl, F32)
            xs = view(Bxs, BF16)
            sc.dma("sp", xl, xsrc[tok0:tok0 + 128, :], reads=xsrc_keys(tok0), writes=[(bl, 0)])
            ssv = smallv[:, ss_o:ss_o + 1]; rsv = smallv[:, ss_o + 1:ss_o + 2]; rstd = smallv[:, ss_o + 2:ss_o + 3]
            sc.op("act", lambda e, o_=xs, i=xl, a=ssv: e.activation(out=o_, in_=i, func=AF.Square, accum_out=a),
                  reads=[(bl, 0)], writes=[(Bxs, 0), (B_small, "ss")])
            sc.op("act", lambda e, o_=rsv, i=ssv: e.activation(out=o_, in_=i, func=AF.Sqrt, scale=1.0 / D, bias=EPS),
                  reads=[(B_small, "ss")], writes=[(B_small, "rs")])
            sc.op("dve", lambda e, o_=rstd, i=rsv: e.reciprocal(out=o_, in_=i), reads=[(B_small, "rs")], writes=[(B_small, "rstd")])
            sc.op("act", lambda e, o_=xs, i=xl, s_=rstd: e.activation(out=o_, in_=i, func=AF.Copy, scale=s_),
                  reads=[(bl, 0), (B_small, "rstd")], writes=[(Bxs, 0)])
        def pro_tr(tt, j):
            Bx = Bxx[tt % 2]
            xnT = view(Bx, BF16, shape=(32, 512))
            xs = view(Bxs, BF16)
            for g in range(8):
                pb = g % 2
                pv = psb(pb)
                for q in range(4):
                    kc = g * 4 + q
                    sc.op("pe", lambda e, o_=pv[:, q * 128:(q + 1) * 128], i=xs[:, kc * 128:(kc + 1) * 128]: e.transpose(out=o_, in_=i, identity=ident),
                          reads=[(Bxs, 0), KI], writes=[("ps", pb)], signal=(q == 3))
                for q in range(4):
                    kc = g * 4 + q
                    dst = xnT[:, kc, j * 128:(j + 1) * 128]
                    srcp = pv[:, q * 128:(q + 1) * 128]
                    if g % 2 == 0:
                        sc.op("dve", lambda e, o_=dst, i=srcp, s_=gT[:, kc:kc + 1]: e.tensor_scalar(out=o_, in0=i, scalar1=s_, scalar2=None, op0=ALU.mult),
                              reads=[("ps", pb), (Bg, 0)], writes=[(Bx, j)])
                    else:
                        sc.op("act", lambda e, o_=dst, i=srcp, s_=gT[:, kc:kc + 1]: e.activation(out=o_, in_=i, func=AF.Copy, scale=s_),
                              reads=[("ps", pb), (Bg, 0)], writes=[(Bx, j)])
        for tt in range(8):
            if tt == 0:
                for j in range(4):
                    pro_norm(0, j)
                    pro_tr(0, j)
            Bx = Bxx[tt % 2]
            xnT = view(Bx, BF16, shape=(32, 512))
            xkeys = [(Bx, j) for j in range(4)]
            for cb in range(INW // 512):
                info = cb_info(cb)
                wslot = Wb[cb % 2]
                wv = view(wslot, BF16, shape=(32, 512))
                sc.dma("sp", view(wslot, BF16), wb["w_in"][l][cb], reads=wkeys("w_in", l, cb), writes=[(wslot, 0)])
                bst = Bst[sti[0] % 3]; sti[0] += 1
                stv = view(bst, BF16, shape=(4, 512))
                for sub in range(4):
                    pb = psrot[pi[0] % 6]; pi[0] += 1
                    ps = psf(pb)
                    for kc in range(32):
                        if info[0] == "F":
                            lhsT, rhs = wv[:, kc, sub * 128:(sub + 1) * 128], xnT[:, kc, :]
                        else:
                            lhsT, rhs = xnT[:, kc, sub * 128:(sub + 1) * 128], wv[:, kc, :]
                        sc.op("pe", lambda e, o_=ps[:], a=lhsT, b=rhs, st=(kc == 0), sp_=(kc == 31): e.matmul(o_, a, b, start=st, stop=sp_),
                              reads=[(wslot, 0)] + xkeys, writes=[("ps", pb)], signal=(kc == 31))
                    dst = stv[:, sub, :]
                    if info[0] == "F" or info[3] == "copy":
                        fn = info[2] if info[0] == "F" else None
                        if fn is None:
                            if evi[0] % 2 == 0:
                                sc.op("dve", lambda e, o_=dst, i=ps[:]: e.tensor_copy(out=o_, in_=i), reads=[("ps", pb)], writes=[(bst, sub)])
                            else:
                                sc.op("act", lambda e, o_=dst, i=ps[:]: e.copy(out=o_, in_=i), reads=[("ps", pb)], writes=[(bst, sub)])
                            evi[0] += 1
                        else:
                            sc.op("act", lambda e, o_=dst, i=ps[:], f=fn: e.activation(out=o_, in_=i, func=f), reads=[("ps", pb)], writes=[(bst, sub)])
                    else:
                        ti = tt * 4 + sub
                        p3 = ps[:].rearrange("p (h d) -> p h d", h=4, d=128)
                        d3 = dst.rearrange("p (h d) -> p h d", h=4, d=128)
                        cosb = ropev[:, 0, ti, :].unsqueeze(1).to_broadcast([128, 4, 64])
                        sinb = ropev[:, 1, ti, :].unsqueeze(1).to_broadcast([128, 4, 64])
                        tA = view(Btmp[0], F32, shape=(4, 64)); tB = view(Btmp[1], F32, shape=(4, 64))
                        t1, t2 = p3[:, :, 0:64], p3[:, :, 64:128]
                        rk_ = [(Brope, 0), (Brope, 1)]
                        sc.op("dve", lambda e, o_=tA, a=t1, b=cosb: e.tensor_tensor(out=o_, in0=a, in1=b, op=ALU.mult), reads=[("ps", pb)] + rk_, writes=[(Btmp[0], 0)])
                        sc.op("dve", lambda e, o_=tB, a=t2, b=sinb: e.tensor_tensor(out=o_, in0=a, in1=b, op=ALU.mult), reads=[("ps", pb)] + rk_, writes=[(Btmp[1], 0)])
                        sc.op("dve", lambda e, o_=d3[:, :, 0:64], a=tA, b=tB: e.tensor_tensor(out=o_, in0=a, in1=b, op=ALU.subtract),
                              reads=[(Btmp[0], 0), (Btmp[1], 0)], writes=[(bst, sub)])
                        sc.op("dve", lambda e, o_=tA, a=t1, b=sinb: e.tensor_tensor(out=o_, in0=a, in1=b, op=ALU.mult), reads=[("ps", pb)] + rk_, writes=[(Btmp[0], 0)])
                        sc.op("dve", lambda e, o_=tB, a=t2, b=cosb: e.tensor_tensor(out=o_, in0=a, in1=b, op=ALU.mult), reads=[("ps", pb)] + rk_, writes=[(Btmp[1], 0)])
                        sc.op("dve", lambda e, o_=d3[:, :, 64:128], a=tA, b=tB: e.tensor_tensor(out=o_, in0=a, in1=b, op=ALU.add),
                              reads=[(Btmp[0], 0), (Btmp[1], 0)], writes=[(bst, sub)])
                skeys = [(bst, s_) for s_ in range(4)]
                if info[0] == "F":
                    r0 = info[1]
                    dstd = sF[r0:r0 + 512, tt * 512:(tt + 1) * 512].rearrange("(s p) t -> p s t", p=128)
                    wk = [("sF", r0 // 128 + s_, tt) for s_ in range(4)]
                else:
                    c0 = info[2]
                    dstd = info[1][tt * 512:(tt + 1) * 512, c0:c0 + 512].rearrange("(j p) c -> p j c", p=128)
                    wk = [("TM", cb, tt)]
                sc.dma("pool", dstd, stv, reads=skeys, writes=wk)
                if tt + 1 < 8:
                    if cb in (4, 12, 20, 28):
                        pro_norm(tt + 1, (cb - 4) // 8)
                    if cb in (8, 16, 24, 32):
                        pro_tr(tt + 1, (cb - 8) // 8)

    def sF_keys(rowblk):
        return [("sF", rowblk, tt) for tt in range(8)]

    def tm_keys(cbs):
        return [("TM", cb, tt) for cb in cbs for tt in range(8)]

    def phaseB1(l):
        o = 0
        sets = []
        for i in range(2):
            d = {}
            for nm, sz in [("kT", 8192), ("qT", 8192), ("v", 8192), ("vsh", 8192), ("gT", 8192), ("aT", 8192), ("E", 4096)]:
                d[nm] = sc.newbuf("B1_%s%d" % (nm, i), o, sz); o += sz
            sets.append(d)
        Bpe = [sc.newbuf("B1_pexp%d" % i, o + i * 1024, 1024) for i in range(2)]; o += 2048
        Bp = [sc.newbuf("B1_p%d" % i, o + i * 512, 512) for i in range(2)]; o += 1024
        Brc = [sc.newbuf("B1_rc%d" % i, o + i * 256, 256) for i in range(2)]; o += 512
        assert o <= top[0]
        NIT = 16 * 64
        hv = {}

        def head_setup(h):
            d = sets[h % 2]
            kT = view(d["kT"], BF16); qT = view(d["qT"], BF16); gT = view(d["gT"], BF16); aTh = view(d["aT"], BF16)
            v = view(d["v"], BF16, shape=(32, 128)); vsh = view(d["vsh"], BF16, shape=(32, 128))
            E = view(d["E"], F32)
            E3 = view(d["E"], F32, shape=(2, 512))
            sc.dma("sp", qT, sF[h * 128:(h + 1) * 128, :], reads=sF_keys(h), writes=[(d["qT"], 0)])
            sc.dma("sp", kT, sF[2048 + h * 128:2048 + (h + 1) * 128, :], reads=sF_keys(16 + h), writes=[(d["kT"], 0)])
            sc.dma("sp", gT, sF[4096 + h * 128:4096 + (h + 1) * 128, :], reads=sF_keys(32 + h), writes=[(d["gT"], 0)])
            vk = tm_keys([8 + h // 4])
            sc.dma("sp", v, nav[:, h * 128:(h + 1) * 128].rearrange("(c p) d -> p c d", p=128), reads=vk, writes=[(d["v"], 0)])
            sc.dma("sp", vsh[:, 0:31, :], nav[64:64 + 31 * 128, h * 128:(h + 1) * 128].rearrange("(c p) d -> p c d", p=128), reads=vk, writes=[(d["vsh"], 0)])
            sc.dma("sp", E, bias_tab[l, h], writes=[(d["E"], 0)])
            sc.op("act", lambda e, o_=E: e.activation(out=o_, in_=o_, func=AF.Exp), reads=[(d["E"], 0)], writes=[(d["E"], 0)])
            sc.op("dve", lambda e, o_=E: e.tensor_tensor(out=o_, in0=o_, in1=maskv, op=ALU.mult), reads=[(d["E"], 0), KM], writes=[(d["E"], 0)])
            hv[h] = (d, kT, qT, gT, aTh, v, vsh, E3)

        def geo(k):
            h, r = k // 64, k % 64
            rs = min(max(r - 4, 0), 56)
            base = rs - r + 7
            return h, r, rs, base % 2, base // 2

        def emit_S(k):
            h, r, rs, tab, i0_ = geo(k)
            if r == 0 and h == 0:
                head_setup(0)
            if r == 16 and h + 1 < 16:
                head_setup(h + 1)
            d, kT, qT = hv[h][0], hv[h][1], hv[h][2]
            sb_ = k % 2
            Sps = psf(sb_)
            for m in range(4):
                kr = rs + 2 * m
                sc.op("pe", lambda e, o_=Sps[:, m * 64:(m + 1) * 64], a=kT[:, kr * 64:kr * 64 + 128], b=qT[:, r * 64:(r + 1) * 64]:
                      e.matmul(o_, a, b, start=True, stop=True),
                      reads=[(d["kT"], 0), (d["qT"], 0)], writes=[("ps", sb_)], signal=(m == 3))

        def emit_mid(k):
            h, r, rs, tab, i0_ = geo(k)
            d, E3 = hv[h][0], hv[h][7]
            sb_ = k % 2
            Sps = psf(sb_)
            pe_ = view(Bpe[k % 2], F32)
            pp = view(Bp[k % 2], BF16)
            sc.op("act", lambda e, o_=pe_, i=Sps[:, 0:256]: e.activation(out=o_, in_=i, func=AF.Exp, scale=NA_SCALE),
                  reads=[("ps", sb_)], writes=[(Bpe[k % 2], 0)])
            sc.op("dve", lambda e, o_=pp, a=pe_, b=E3[:, tab, i0_ * 64:(i0_ + 4) * 64]: e.tensor_tensor(out=o_, in0=a, in1=b, op=ALU.mult),
                  reads=[(Bpe[k % 2], 0), (d["E"], 0)], writes=[(Bp[k % 2], 0)])

        def emit_O(k):
            h, r, rs, tab, i0_ = geo(k)
            d, kT, qT, gT, aTh, v, vsh, E3 = hv[h]
            ob_ = 2 + k % 2
            Ops = psf(ob_)
            pp = view(Bp[k % 2], BF16)
            for m in range(4):
                kr = rs + 2 * m
                vc = v[:, kr // 2, :] if kr % 2 == 0 else vsh[:, (kr - 1) // 2, :]
                sc.op("pe", lambda e, o_=Ops[:, 0:64], a=vc, b=pp[:, m * 64:(m + 1) * 64], st=(m == 0), sp_=(m == 3): e.matmul(o_, a, b, start=st, stop=sp_),
                      reads=[(d["v"], 0), (d["vsh"], 0), (Bp[k % 2], 0)], writes=[("ps", ob_)], signal=False)
            for m in range(4):
                sc.op("pe", lambda e, o_=Ops[:, 64:128], b=pp[:, m * 64:(m + 1) * 64], st=(m == 0), sp_=(m == 3): e.matmul(o_, ones, b, start=st, stop=sp_),
                      reads=[KO, (Bp[k % 2], 0)], writes=[("ps", ob_)], signal=(m == 3))
            rc = view(Brc[k % 2], F32)
            sc.op("dve", lambda e, o_=rc, i=Ops[:, 64:128]: e.reciprocal(out=o_, in_=i), reads=[("ps", ob_)], writes=[(Brc[k % 2], 0)])
            sc.op("dve", lambda e, o_=rc, a=rc, b=gT[:, r * 64:(r + 1) * 64]: e.tensor_tensor(out=o_, in0=a, in1=b, op=ALU.mult),
                  reads=[(Brc[k % 2], 0), (d["gT"], 0)], writes=[(Brc[k % 2], 0)])
            sc.op("dve", lambda e, o_=aTh[:, r * 64:(r + 1) * 64], a=Ops[:, 0:64], b=rc: e.tensor_tensor(out=o_, in0=a, in1=b, op=ALU.mult),
                  reads=[("ps", ob_), (Brc[k % 2], 0)], writes=[(d["aT"], 0)])
            if r == 63:
                sc.dma("pool", aT[h * 128:(h + 1) * 128, :], aTh, reads=[(d["aT"], 0)], writes=[("aT", h)])

        emit_S(0); emit_S(1); emit_mid(0)
        for k in range(NIT):
            if k + 2 < NIT:
                emit_S(k + 2)
            if k + 1 < NIT:
                emit_mid(k + 1)
            emit_O(k)

    def phaseB2(l):
        o = 0
        def nb(nm, sz):
            nonlocal o
            b = sc.newbuf("B2_" + nm, o, sz); o += sz
            return b
        Brq, Brk, Brv, Brg, Bbt = nb("rq", 8192), nb("rk", 8192), nb("rv", 16384), nb("rg", 16384), nb("bt", 16384)
        BqT, BkT, BKf, BKb, BQf, BQb = [nb(n_, 8192) for n_ in ("qT", "kT", "Kf", "Kb", "Qf", "Qb")]
        BSf, BSb = nb("Sf", 16384), nb("Sb", 16384)
        Bqdf, Bqdb, BDT, Bgn = nb("qdf", 4096), nb("qdb", 4096), nb("DT", 4096), nb("gn", 8192)
        Btb = nb("tb", 512)
        Bstate = [nb("st%d" % i, 1024) for i in range(2)]
        Bt1, Bt2 = nb("t1", 512), nb("t2", 512)
        By = [nb("y%d" % i, 512) for i in range(2)]
        Bpt = [nb("pt%d" % i, 256) for i in range(2)]
        Bjk = nb("junk", 512)
        assert o <= top[0], (o, top[0])
        tb = view(Btb, F32)
        nldf, nldb, kdf, kdb, gcf, gcb = [tb[:, i * 8:(i + 1) * 8] for i in range(6)]
        qdf = view(Bqdf, F32, shape=(8, 128)); qdb = view(Bqdb, F32, shape=(8, 128)); DT = view(BDT, F32, shape=(8, 128))
        gnv = view(Bgn, F32)
        t1 = view(Bt1, F32); t2 = view(Bt2, F32)
        sc.dma("sp", nldf, ldf[l:l + 1, :].partition_broadcast(128), writes=[(Btb, "f")])
        sc.dma("sp", nldb, ldb[l:l + 1, :].partition_broadcast(128), writes=[(Btb, "b")])
        sc.dma("sp", gnv, gn[l:l + 1, :].partition_broadcast(128), writes=[(Bgn, 0)])
        for nm, ap_ in (("f", nldf), ("b", nldb)):
            sc.op("act", lambda e, o_=ap_: e.activation(out=o_, in_=o_, func=AF.Abs), reads=[(Btb, nm)], writes=[(Btb, nm)])
            sc.op("dve", lambda e, o_=ap_: e.tensor_scalar(out=o_, in0=o_, scalar1=-1.0, scalar2=None, op0=ALU.mult),
                  reads=[(Btb, nm)], writes=[(Btb, nm)])
        sc.op("act", lambda e: e.activation(out=kdf, in_=nldf, func=AF.Exp, scale=colA, bias=LN_RSCALE), reads=[(Btb, "f"), KC_], writes=[(Btb, "kdf")])
        sc.op("act", lambda e: e.activation(out=kdb, in_=nldb, func=AF.Exp, scale=colB, bias=LN_RSCALE), reads=[(Btb, "b"), KC_], writes=[(Btb, "kdb")])
        sc.op("act", lambda e: e.activation(out=gcf, in_=nldf, func=AF.Exp, scale=128.0), reads=[(Btb, "f")], writes=[(Btb, "gcf")])
        sc.op("act", lambda e: e.activation(out=gcb, in_=nldb, func=AF.Exp, scale=128.0), reads=[(Btb, "b")], writes=[(Btb, "gcb")])
        for h in range(8):
            sc.op("act", lambda e, o_=qdf[:, h, :], s_=nldf[:, h:h + 1]: e.activation(out=o_, in_=rowA, func=AF.Exp, scale=s_), reads=[(Btb, "f"), KC_], writes=[(Bqdf, h)])
            sc.op("act", lambda e, o_=qdb[:, h, :], s_=nldb[:, h:h + 1]: e.activation(out=o_, in_=rowB, func=AF.Exp, scale=s_), reads=[(Btb, "b"), KC_], writes=[(Bqdb, h)])
            sc.op("act", lambda e, s_=nldf[:, h:h + 1]: e.activation(out=t1, in_=M1, func=AF.Exp, scale=s_, bias=LN_RSCALE), reads=[(Btb, "f"), KC_], writes=[(Bt1, 0)])
            sc.op("dve", lambda e: e.tensor_tensor(out=t1, in0=t1, in1=mge, op=ALU.mult), reads=[(Bt1, 0), KC_], writes=[(Bt1, 0)])
            sc.op("act", lambda e, s_=nldb[:, h:h + 1]: e.activation(out=t2, in_=M2, func=AF.Exp, scale=s_, bias=LN_RSCALE), reads=[(Btb, "b"), KC_], writes=[(Bt2, 0)])
            sc.op("dve", lambda e: e.tensor_tensor(out=t2, in0=t2, in1=mle, op=ALU.mult), reads=[(Bt2, 0), KC_], writes=[(Bt2, 0)])
            sc.op("dve", lambda e, o_=DT[:, h, :]: e.tensor_tensor(out=o_, in0=t1, in1=t2, op=ALU.add), reads=[(Bt1, 0), (Bt2, 0)], writes=[(BDT, h)])
        pk = [0]
        for h in range(8):
            rqv = view(Brq, BF16, shape=(32, 128)); rkv = view(Brk, BF16, shape=(32, 128)); rvv = view(Brv, BF16, shape=(32, 256))
            rgv = view(Brg, BF16, shape=(2, S)); btv = view(Bbt, BF16, shape=(2, S))
            qT = view(BqT, BF16); kT = view(BkT, BF16)
            Kf = view(BKf, BF16, shape=(32, 128)); Kb = view(BKb, BF16, shape=(32, 128))
            Qf = view(BQf, BF16, shape=(32, 128)); Qb = view(BQb, BF16, shape=(32, 128))
            Sf = view(BSf, BF16, shape=(32, 256)); Sb = view(BSb, BF16, shape=(32, 256))
            sc.dma("sp", rqv, rq[:, h * 128:(h + 1) * 128].rearrange("(c p) d -> p c d", p=128), reads=tm_keys([16 + h // 4]), writes=[(Brq, 0)])
            sc.dma("sp", rkv, rk[:, h * 128:(h + 1) * 128].rearrange("(c p) d -> p c d", p=128), reads=tm_keys([18 + h // 4]), writes=[(Brk, 0)])
            sc.dma("sp", rvv, rv[:, h * 256:(h + 1) * 256].rearrange("(c p) d -> p c d", p=128), reads=tm_keys([20 + h // 2]), writes=[(Brv, 0)])
            sc.dma("sp", rgv, sF[6144 + h * 256:6144 + (h + 1) * 256, :].rearrange("(j p) t -> p j t", p=128),
                   reads=sF_keys(48 + 2 * h) + sF_keys(49 + 2 * h), writes=[(Brg, 0)])
            for (src3, srck, dstv, dstk) in ((rqv, Brq, qT, BqT), (rkv, Brk, kT, BkT)):
                for g in range(8):
                    pb = pk[0] % 2; pk[0] += 1
                    pv = psb(pb)
                    for q_ in range(4):
                        c = g * 4 + q_
                        sc.op("pe", lambda e, o_=pv[:, q_ * 128:(q_ + 1) * 128], i=src3[:, c, :]: e.transpose(out=o_, in_=i, identity=ident),
                              reads=[(srck, 0), KI], writes=[("ps", pb)], signal=(q_ == 3))
                    if g % 2 == 0:
                        sc.op("dve", lambda e, o_=dstv[:, g * 512:(g + 1) * 512], i=pv[:, 0:512]: e.tensor_copy(out=o_, in_=i), reads=[("ps", pb)], writes=[(dstk, g)])
                    else:
                        sc.op("act", lambda e, o_=dstv[:, g * 512:(g + 1) * 512], i=pv[:, 0:512]: e.copy(out=o_, in_=i), reads=[("ps", pb)], writes=[(dstk, g)])
            qTk = [(BqT, g) for g in range(8)]; kTk = [(BkT, g) for g in range(8)]
            rk2 = view(Brk, BF16)
            sc.op("dve", lambda e, o_=view(BKf, BF16), s_=kdf[:, h:h + 1]: e.tensor_scalar(out=o_, in0=rk2, scalar1=s_, scalar2=None, op0=ALU.mult),
                  reads=[(Brk, 0), (Btb, "kdf")], writes=[(BKf, 0)])
            sc.op("dve", lambda e, o_=view(BKb, BF16), s_=kdb[:, h:h + 1]: e.tensor_scalar(out=o_, in0=rk2, scalar1=s_, scalar2=None, op0=ALU.mult),
                  reads=[(Brk, 0), (Btb, "kdb")], writes=[(BKb, 0)])
            qT3 = view(BqT, BF16, shape=(32, 128))
            sc.op("dve", lambda e, b=qdf[:, h, :].unsqueeze(1).to_broadcast([128, 32, 128]): e.tensor_tensor(out=Qf, in0=qT3, in1=b, op=ALU.mult),
                  reads=qTk + [(Bqdf, h)], writes=[(BQf, 0)])
            sc.op("dve", lambda e, b=qdb[:, h, :].unsqueeze(1).to_broadcast([128, 32, 128]): e.tensor_tensor(out=Qb, in0=qT3, in1=b, op=ALU.mult),
                  reads=qTk + [(Bqdb, h)], writes=[(BQb, 0)])
            for (Kd, Kdk, Sd, Sdk, gc, order) in ((Kf, BKf, Sf, BSf, gcf, list(range(32))), (Kb, BKb, Sb, BSb, gcb, list(range(31, -1, -1)))):
                first = order[0]
                sc.op("dve", lambda e, o_=Sd[:, first, :]: e.memset(o_, 0.0), writes=[(Sdk, first)])
                prev_state = None
                for idx in range(31):
                    n = order[idx]; nxt = order[idx + 1]
                    pb = 4 + pk[0] % 2; pk[0] += 1
                    ps = psf(pb)
                    sc.op("pe", lambda e, o_=ps[:, 0:256], a=Kd[:, n, :], b=rvv[:, n, :]: e.matmul(o_, a, b, start=True, stop=True),
                          reads=[(Kdk, 0), (Brv, 0)], writes=[("ps", pb)])
                    bs = Bstate[idx % 2]
                    stv_ = view(bs, F32)
                    if prev_state is None:
                        sc.op("dve", lambda e, o_=stv_, i=ps[:, 0:256]: e.tensor_copy(out=o_, in_=i), reads=[("ps", pb)], writes=[(bs, 0)])
                    else:
                        pst = view(prev_state, F32)
                        sc.op("dve", lambda e, o_=stv_, a=pst, s_=gc[:, h:h + 1], b=ps[:, 0:256]: e.scalar_tensor_tensor(out=o_, in0=a, scalar=s_, in1=b, op0=ALU.mult, op1=ALU.add),
                              reads=[(prev_state, 0), ("ps", pb), (Btb, "gcf"), (Btb, "gcb")], writes=[(bs, 0)])
                    sc.op("act", lambda e, o_=Sd[:, nxt, :], i=stv_: e.copy(out=o_, in_=i), reads=[(bs, 0)], writes=[(Sdk, nxt)])
                    prev_state = bs
            ss_o = 16
            ssv = smallv[:, ss_o:ss_o + 1]; rsv = smallv[:, ss_o + 1:ss_o + 2]; rstd = smallv[:, ss_o + 2:ss_o + 3]
            jk = view(Bjk, BF16)

            def c_ST(n):
                pb = 2 + n % 2
                ps = psf(pb)
                sc.op("pe", lambda e, o_=ps[:, 0:128], a=kT[:, n * 128:(n + 1) * 128], b=qT[:, n * 128:(n + 1) * 128]: e.matmul(o_, a, b, start=True, stop=True),
                      reads=kTk + qTk, writes=[("ps", pb)])

            def c_pt(n):
                pb = 2 + n % 2
                ps = psf(pb)
                pt = view(Bpt[n % 2], BF16)
                sc.op("dve", lambda e, o_=pt, a=ps[:, 0:128], b=DT[:, h, :]: e.tensor_tensor(out=o_, in0=a, in1=b, op=ALU.mult),
                      reads=[("ps", pb), (BDT, h)], writes=[(Bpt[n % 2], 0)])

            def c_O(n):
                pt = view(Bpt[n % 2], BF16)
                ob = 6 + n % 2
                po = psf(ob)
                sc.op("pe", lambda e, o_=po[:, 0:256], a=pt, b=rvv[:, n, :]: e.matmul(o_, a, b, start=True, stop=False),
                      reads=[(Bpt[n % 2], 0), (Brv, 0)], writes=[("ps", ob)], signal=False)
                sc.op("pe", lambda e, o_=po[:, 0:256], a=Qf[:, n, :], b=Sf[:, n, :]: e.matmul(o_, a, b, start=False, stop=False),
                      reads=[(BQf, 0), (BSf, n)], writes=[("ps", ob)], signal=False)
                sc.op("pe", lambda e, o_=po[:, 0:256], a=Qb[:, n, :], b=Sb[:, n, :]: e.matmul(o_, a, b, start=False, stop=True),
                      reads=[(BQb, 0), (BSb, n)], writes=[("ps", ob)])
                sc.op("act", lambda e, i=po[:, 0:256]: e.activation(out=jk, in_=i, func=AF.Square, accum_out=ssv), reads=[("ps", ob)], writes=[(Bjk, 0), (B_small, "ss2")])
                sc.op("act", lambda e: e.activation(out=rsv, in_=ssv, func=AF.Sqrt, scale=1.0 / 256, bias=EPS), reads=[(B_small, "ss2")], writes=[(B_small, "rs2")])
                sc.op("dve", lambda e: e.reciprocal(out=rstd, in_=rsv), reads=[(B_small, "rs2")], writes=[(B_small, "rstd2")])
                yb = By[n % 2]
                yv = view(yb, BF16)
                sc.op("dve", lambda e, o_=yv, a=po[:, 0:256], b=gnv[:, h * 256:(h + 1) * 256]: e.scalar_tensor_tensor(out=o_, in0=a, scalar=rstd, in1=b, op0=ALU.mult, op1=ALU.mult),
                      reads=[("ps", ob), (B_small, "rstd2"), (Bgn, 0)], writes=[(yb, 0)])

            def c_TR(n):
                yb = By[n % 2]
                yv = view(yb, BF16)
                tb_ = n % 2
                pv = psb(tb_)
                for j in range(2):
                    sc.op("pe", lambda e, o_=pv[:, j * 128:(j + 1) * 128], i=yv[:, j * 128:(j + 1) * 128]: e.transpose(out=o_, in_=i, identity=ident),
                          reads=[(yb, 0), KI], writes=[("ps", tb_)], signal=(j == 1))
                sc.op("dve", lambda e, o_=btv[:, :, n * 128:(n + 1) * 128], a=pv[:, 0:256].rearrange("p (j t) -> p j t", j=2, t=128), b=rgv[:, :, n * 128:(n + 1) * 128]:
                      e.tensor_tensor(out=o_, in0=a, in1=b, op=ALU.mult),
                      reads=[("ps", tb_), (Brg, 0)], writes=[(Bbt, 0)])

            c_ST(0); c_ST(1); c_pt(0)
            for n in range(32):
                if n + 2 < 32:
                    c_ST(n + 2)
                if n + 1 < 32:
                    c_pt(n + 1)
                c_O(n)
                if n >= 1:
                    c_TR(n - 1)
            c_TR(31)
            sc.dma("pool", bT[h * 256:(h + 1) * 256, :].rearrange("(j p) t -> p j t", p=128), btv, reads=[(Bbt, 0)], writes=[("bT", h)])

    def phaseC(l, xsrc, xsrc_keys, ydst, ykey):
        T = 256
        o = 0
        def nb(nm, sz):
            nonlocal o
            b = sc.newbuf("C_" + nm, o, sz); o += sz
            return b
        Wh = [nb("wh%d" % i, 16384) for i in range(4)]
        Bwp = [nb("wple%d" % i, 2048) for i in range(2)]
        Bmo = nb("mo", 32768)
        Bxin = nb("xin", 16384)
        Blp = nb("lnpost", 16384)
        Bmg = nb("merged", 16384)
        Bab = nb("ab", 16384)
        Bx1b = nb("x1b", 8192)
        Bgt = [nb("gt%d" % i, 4096) for i in range(2)]
        Bm1 = [nb("m1_%d" % i, 1024) for i in range(2)]
        Bm2 = [nb("m2_%d" % i, 1024) for i in range(2)]
        Bsg = [nb("sg%d" % i, 2048) for i in range(2)]
        Bpl = nb("pl", 1024)
        Bplb = nb("plb", 512)
        BpT = nb("pT", 1024)
        Bjk = nb("junk", 1024)
        assert o <= top[0], (o, top[0])
        lpv = view(Blp, F32)
        for hh in range(2):
            sc.dma("sp", lpv[:, hh * 2048:(hh + 1) * 2048], lnpost[l:l + 1, hh * 2048:(hh + 1) * 2048].partition_broadcast(128), writes=[(Blp, hh)])
        mo = view(Bmo, F32, shape=(2, D))
        mg = view(Bmg, BF16, shape=(32, T))
        wi = [0]
        pi = [0]
        ssq_o = 32
        hi_ = [0]
        fi_ = [0]
        pli = [0]

        def wload(nm, cb):
            KC = dict((a, b // 128) for a, b, c in W_SPECS)[nm]
            if KC == 2:
                b = Bwp[pli[0] % 2]; pli[0] += 1
                sc.dma("sp", view(b, BF16, n=1024), wb[nm][l][cb], reads=wkeys(nm, l, cb), writes=[(b, 0)])
                return [(b, 0)], view(b, BF16, shape=(2, 512), n=1024)
            if KC == 16:
                b = Wh[hi_[0] % 4]; hi_[0] += 1
                sc.dma("sp", view(b, BF16, n=KC * 512), wb[nm][l][cb], reads=wkeys(nm, l, cb), writes=[(b, 0)])
                return [(b, 0)], view(b, BF16, shape=(KC, 512), n=KC * 512)
            pair = fi_[0] % 2; fi_[0] += 1
            b0, b1 = Wh[2 * pair], Wh[2 * pair + 1]
            esz = 2
            v_ = arena[:, b0.lo:b1.hi].bitcast(BF16)
            sc.dma("sp", v_, wb[nm][l][cb], reads=wkeys(nm, l, cb), writes=[(b0, 0), (b1, 0)])
            return [(b0, 0), (b1, 0)], v_.rearrange("p (a b) -> p a b", a=32, b=512)
        for tt in range(S // T):
            t0 = tt * T
            aTv = view(Bab, BF16, shape=(16, T), n=16 * T)
            bTv = view(Bab, BF16, shape=(16, T), off=16 * T, n=16 * T)
            sc.dma("sp", aTv, aT[:, t0:t0 + T].rearrange("(k p) t -> p k t", p=128), reads=[("aT", h) for h in range(16)], writes=[(Bab, 0)])
            sc.dma("sp", bTv, bT[:, t0:t0 + T].rearrange("(k p) t -> p k t", p=128), reads=[("bT", h) for h in range(8)], writes=[(Bab, 1)])
            if CSTOP < 1:
                continue
            for cb in range(8):
                sa, wa = wload("w_pa", cb)
                sb2, wbv = wload("w_pb", cb)
                gslot = Bgt[cb % 2]
                gv = view(gslot, BF16, shape=(2, 4, T))
                r0 = 8192 + cb * 512
                sc.dma("sp", gv[:, 0], sF[r0:r0 + 512, t0:t0 + T].rearrange("(s p) t -> p s t", p=128), reads=[("sF", r0 // 128 + s_, t0 // 512) for s_ in range(4)], writes=[(gslot, 0)])
                r1 = 12288 + cb * 512
                sc.dma("sp", gv[:, 1], sF[r1:r1 + 512, t0:t0 + T].rearrange("(s p) t -> p s t", p=128), reads=[("sF", r1 // 128 + s_, t0 // 512) for s_ in range(4)], writes=[(gslot, 1)])
                for sub in range(4):
                    fb = cb * 4 + sub
                    pa = 2 + pi[0] % 3; pbk = 5 + pi[0] % 3; pi[0] += 1
                    for kc in range(16):
                        sc.op("pe", lambda e, o_=psf(pa)[:, 0:T], a=wa[:, kc, sub * 128:(sub + 1) * 128], b=aTv[:, kc, :], st=(kc == 0), sp_=(kc == 15): e.matmul(o_, a, b, start=st, stop=sp_),
                              reads=sa + [(Bab, 0)], writes=[("ps", pa)], signal=(kc == 15))
                    for kc in range(16):
                        sc.op("pe", lambda e, o_=psf(pbk)[:, 0:T], a=wbv[:, kc, sub * 128:(sub + 1) * 128], b=bTv[:, kc, :], st=(kc == 0), sp_=(kc == 15): e.matmul(o_, a, b, start=st, stop=sp_),
                              reads=sb2 + [(Bab, 1)], writes=[("ps", pbk)], signal=(kc == 15))
                    m1 = view(Bm1[fb % 2], F32); m2 = view(Bm2[fb % 2], F32)
                    sc.op("dve", lambda e, o_=m1, a=psf(pa)[:, 0:T], b=gv[:, 0, sub, :]: e.tensor_tensor(out=o_, in0=a, in1=b, op=ALU.mult),
                          reads=[("ps", pa), (gslot, 0)], writes=[(Bm1[fb % 2], 0)])
                    sc.op("dve", lambda e, o_=m2, a=psf(pbk)[:, 0:T], b=gv[:, 1, sub, :]: e.tensor_tensor(out=o_, in0=a, in1=b, op=ALU.mult),
                          reads=[("ps", pbk), (gslot, 1)], writes=[(Bm2[fb % 2], 0)])
                    sc.op("dve", lambda e, o_=mg[:, fb, :], a=m1, b=m2: e.tensor_tensor(out=o_, in0=a, in1=b, op=ALU.add),
                          reads=[(Bm1[fb % 2], 0), (Bm2[fb % 2], 0)], writes=[(Bmg, fb)])
            mgk = [(Bmg, fb) for fb in range(32)]
            if CSTOP < 2:
                continue
            for cb in range(8):
                sw, wv = wload("w_out", cb)
                for j in range(2):
                    pb = 2 + pi[0] % 6; pi[0] += 1
                    ps = psf(pb)
                    for kc in range(32):
                        sc.op("pe", lambda e, o_=ps[:], a=mg[:, kc, j * 128:(j + 1) * 128], b=wv[:, kc, :], st=(kc == 0), sp_=(kc == 31): e.matmul(o_, a, b, start=st, stop=sp_),
                              reads=sw + mgk, writes=[("ps", pb)], signal=(kc == 31))
                    sc.op("dve", lambda e, o_=mo[:, j, cb * 512:(cb + 1) * 512], i=ps[:]: e.tensor_copy(out=o_, in_=i), reads=[("ps", pb)], writes=[(Bmo, (j, cb))])
                    sq = smallv[:, ssq_o + j * 8 + cb:ssq_o + j * 8 + cb + 1]
                    sc.op("act", lambda e, i=mo[:, j, cb * 512:(cb + 1) * 512], a=sq: e.activation(out=view(Bjk, BF16), in_=i, func=AF.Square, accum_out=a),
                          reads=[(Bmo, (j, cb))], writes=[(Bjk, 0), (B_small, ("ssq", j, cb))])
            if CSTOP < 3:
                continue
            for j in range(2):
                tok0 = t0 + j * 128
                tot = smallv[:, 48 + j:49 + j]; rs_ = smallv[:, 50 + j:51 + j]; rstd = smallv[:, 52 + j:53 + j]
                sc.op("dve", lambda e, o_=tot, i=smallv[:, ssq_o + j * 8:ssq_o + j * 8 + 8]: e.tensor_reduce(out=o_, in_=i, axis=mybir.AxisListType.X, op=ALU.add),
                      reads=[(B_small, ("ssq", j, cb)) for cb in range(8)], writes=[(B_small, ("tot", j))])
                sc.op("act", lambda e, o_=rs_, i=tot: e.activation(out=o_, in_=i, func=AF.Sqrt, scale=1.0 / D, bias=EPS), reads=[(B_small, ("tot", j))], writes=[(B_small, ("rs", j))])
                sc.op("dve", lambda e, o_=rstd, i=rs_: e.reciprocal(out=o_, in_=i), reads=[(B_small, ("rs", j))], writes=[(B_small, ("rstd", j))])
                xin = view(Bxin, F32)
                sc.dma("sp", xin, xsrc[tok0:tok0 + 128, :], reads=xsrc_keys(tok0), writes=[(Bxin, 0)])
                mok = [(Bmo, (j, cb)) for cb in range(8)]
                sc.op("dve", lambda e, o_=mo[:, j, :], s_=rstd: e.scalar_tensor_tensor(out=o_, in0=o_, scalar=s_, in1=lpv, op0=ALU.mult, op1=ALU.mult),
                      reads=mok + [(B_small, ("rstd", j)), (Blp, 0), (Blp, 1)], writes=mok)
                sc.op("dve", lambda e, o_=mo[:, j, :], b=xin: e.tensor_tensor(out=o_, in0=o_, in1=b, op=ALU.add), reads=mok + [(Bxin, 0)], writes=mok)
                x1b = view(Bx1b, BF16)
                sc.op("act", lambda e, i=mo[:, j, :]: e.copy(out=x1b, in_=i), reads=mok, writes=[(Bx1b, 0)])
                x1T = view(Bab, BF16, shape=(32, T))
                for g in range(8):
                    pb = g % 2
                    pv = psb(pb)
                    for q in range(4):
                        kc = g * 4 + q
                        sc.op("pe", lambda e, o_=pv[:, q * 128:(q + 1) * 128], i=x1b[:, kc * 128:(kc + 1) * 128]: e.transpose(out=o_, in_=i, identity=ident),
                              reads=[(Bx1b, 0), KI], writes=[("ps", pb)], signal=(q == 3))
                    dst = x1T[:, g * 4:(g + 1) * 4, j * 128:(j + 1) * 128]
                    srcp = pv[:, 0:512].rearrange("p (q t) -> p q t", q=4, t=128)
                    wkk = [(Bab, 0), (Bab, 1)]
                    if g % 2 == 0:
                        sc.op("dve", lambda e, o_=dst, i=srcp: e.tensor_copy(out=o_, in_=i), reads=[("ps", pb)], writes=wkk)
                    else:
                        sc.op("act", lambda e, o_=dst, i=srcp: e.copy(out=o_, in_=i), reads=[("ps", pb)], writes=wkk)
                pl = view(Bpl, F32, n=256)
                plb = view(Bplb, BF16, n=256)
                pT = view(BpT, BF16, shape=(2, T))
                sc.dma("sp", pl, p_in[l, tok0:tok0 + 128, :], writes=[(Bpl, 0)])
                sc.op("act", lambda e: e.copy(out=plb, in_=pl), reads=[(Bpl, 0)], writes=[(Bplb, 0)])
                pv = psb(1)
                for q in range(2):
                    sc.op("pe", lambda e, o_=pv[:, q * 128:(q + 1) * 128], i=plb[:, q * 128:(q + 1) * 128]: e.transpose(out=o_, in_=i, identity=ident),
                          reads=[(Bplb, 0), KI], writes=[("ps", 1)], signal=(q == 1))
                sc.op("dve", lambda e, o_=pT[:, :, j * 128:(j + 1) * 128], i=pv[:, 0:256].rearrange("p (q t) -> p q t", q=2, t=128): e.tensor_copy(out=o_, in_=i),
                      reads=[("ps", 1)], writes=[(BpT, j)])
            x1T = view(Bab, BF16, shape=(32, T))
            x1k = [(Bab, 0), (Bab, 1)]
            if CSTOP < 4:
                continue
            for cb in range(8):
                sg_, wg = wload("w_pg", cb)
                sp2, wp = wload("w_ple", cb)
                for j in range(2):
                    pb = 2 + pi[0] % 6; pi[0] += 1
                    ps = psf(pb)
                    for kc in range(32):
                        sc.op("pe", lambda e, o_=ps[:], a=x1T[:, kc, j * 128:(j + 1) * 128], b=wg[:, kc, :], st=(kc == 0), sp_=(kc == 31): e.matmul(o_, a, b, start=st, stop=sp_),
                              reads=sg_ + x1k, writes=[("ps", pb)], signal=(kc == 31))
                    pb2 = 2 + pi[0] % 6; pi[0] += 1
                    ps2 = psf(pb2)
                    pT = view(BpT, BF16, shape=(2, T))
                    for kc in range(2):
                        sc.op("pe", lambda e, o_=ps2[:], a=pT[:, kc, j * 128:(j + 1) * 128], b=wp[:, kc, :], st=(kc == 0), sp_=(kc == 1): e.matmul(o_, a, b, start=st, stop=sp_),
                              reads=sp2 + [(BpT, j)], writes=[("ps", pb2)], signal=(kc == 1))
                    sgb = Bsg[(cb * 2 + j) % 2]
                    sgv = view(sgb, F32)
                    sc.op("act", lambda e, o_=sgv, i=ps[:]: e.activation(out=o_, in_=i, func=AF.Sigmoid), reads=[("ps", pb)], writes=[(sgb, 0)])
                    sc.op("dve", lambda e, o_=sgv, b=ps2[:]: e.tensor_tensor(out=o_, in0=o_, in1=b, op=ALU.mult), reads=[(sgb, 0), ("ps", pb2)], writes=[(sgb, 0)])
                    xs_ = mo[:, j, cb * 512:(cb + 1) * 512]
                    sc.op("dve", lambda e, o_=xs_, b=sgv: e.tensor_tensor(out=o_, in0=o_, in1=b, op=ALU.add), reads=[(sgb, 0), (Bmo, (j, cb))], writes=[(Bmo, (j, cb))])
            for j in range(2):
                tok0 = t0 + j * 128
                sc.dma("pool", ydst[tok0:tok0 + 128, :], mo[:, j, :], reads=[(Bmo, (j, cb)) for cb in range(8)], writes=[(ykey, tok0 // 128)])

    layers = list(range(nlayers))
    if "0" in phases:
        phase0(layers)
    for l in layers:
        if l == 0:
            xs_ap, xs_keys = x_in, (lambda tok0: [])
        else:
            xs_ap, xs_keys = xmid, (lambda tok0: [("xmid", tok0 // 128)])
        if "A" in phases:
            phaseA(l, xs_ap, xs_keys)
        if "B" in phases:
            phaseB1(l)
            phaseB2(l)
        if "C" in phases:
            if l == nlayers - 1:
                phaseC(l, xs_ap, xs_keys, y_out, "y")
            else:
                phaseC(l, xs_ap, xs_keys, xmid, "xmid")
    sc.finish()
    block = es.enter_context(nc.Block())
    sc.emit(block)
    es.close()
    return nc, sc


def _const_tables():
    c = np.arange(128, dtype=np.float32)
    cst = np.zeros((128, CST_W), np.float32)
    cst[:, 0] = 127.0 - c
    cst[:, 1] = c
    t = np.arange(128, dtype=np.float32)
    cst[:, 2:130] = (t + 1.0)[None, :]
    cst[:, 130:258] = (128.0 - t)[None, :]
    s_ = c[:, None]
    tt = t[None, :]
    cst[:, 258:386] = np.maximum(tt - s_, 0.0)
    cst[:, 386:514] = np.maximum(s_ - tt, 0.0)
    cst[:, 514:642] = (tt >= s_).astype(np.float32)
    cst[:, 642:770] = (s_ >= tt).astype(np.float32)
    half = 64
    inv = (10000.0 ** (-np.arange(half, dtype=np.float32) / half)).astype(np.float32)
    ang = np.arange(S, dtype=np.float32)[:, None] * inv[None, :]
    rope = np.stack([np.cos(ang), np.sin(ang)]).astype(np.float32)
    cols = np.arange(64)
    cs = np.clip(cols - 8, 0, 48)
    valid = (cols[None, :] >= cs[:, None]) & (cols[None, :] < cs[:, None] + 16)
    m = valid.T.astype(np.float32)
    mask = np.tile(np.tile(m, (2, 1))[:, None, :], (1, 16, 1)).reshape(128, 1024)
    return cst, rope, np.ascontiguousarray(mask)


def _bias_table(rpb):
    cols = np.arange(64)
    cidx = np.clip(cols[:, None] - cols[None, :] + 15, 0, 30)
    out = np.zeros((NL, 16, 128, 2, 8, 64), np.float32)
    for tab in range(2):
        for i in range(8):
            for jp in range(2):
                ri = 2 * i + tab + jp
                if ri > 14:
                    continue
                out[:, :, jp * 64:(jp + 1) * 64, tab, i, :] = rpb[:, :, ri][:, :, cidx]
    return out.reshape(NL, 16, 128, 1024)


_CACHE = {}


def kernel(x_prompt, x_sample, p_prompt, p_sample, w_in, ln_pre, ln_post, na_rpb,
           ret_log_decay_fwd, ret_log_decay_bwd, ret_gn_gain, w_proj_a, w_proj_b,
           w_out, w_ple, w_ple_gate):
    if "nc" not in _CACHE:
        _CACHE["nc"] = build_program()[0]
    nc = _CACHE["nc"]
    f = lambda a: np.ascontiguousarray(np.asarray(a, dtype=np.float32))
    cst, rope, mask = _const_tables()
    shared = {
        "w_in": f(w_in), "w_pa": f(w_proj_a), "w_pb": f(w_proj_b), "w_out": f(w_out),
        "w_pg": f(w_ple_gate), "w_ple": f(w_ple),
        "ln_preT": np.ascontiguousarray(f(ln_pre).reshape(NL, 32, 128).transpose(0, 2, 1)),
        "ln_post": f(ln_post), "gn": f(ret_gn_gain), "ldf": f(ret_log_decay_fwd), "ldb": f(ret_log_decay_bwd),
        "bias_tab": _bias_table(f(na_rpb)), "mask_tab": mask, "cst": cst, "rope": rope,
        "ident": np.eye(128, dtype=np.float32).astype(ml_dtypes.bfloat16),
        "ones": np.ones((128, 128), np.float32).astype(ml_dtypes.bfloat16),
    }
    xp, xs_, pp, ps_ = f(x_prompt), f(x_sample), f(p_prompt), f(p_sample)
    seqs = [(xp[i], pp[:, i]) for i in range(4)] + [(xs_[i], ps_[:, i]) for i in range(2)]
    seqs = seqs + [seqs[0], seqs[1]]
    in_maps = []
    for c in range(8):
        d = dict(shared)
        d["x"] = np.ascontiguousarray(seqs[c][0])
        d["p"] = np.ascontiguousarray(seqs[c][1])
        in_maps.append(d)
    res = run_bass_kernel_spmd(nc, in_maps, core_ids=list(range(8)))
    ys = [np.asarray(res.results[c]["y"], dtype=np.float32) for c in range(6)]
    y_prompt = np.stack(ys[0:4])
    y_sample = np.stack(ys[4:6])
    return (y_prompt, y_sample)
```

```python
import math
import os
CSTOP = int(os.environ.get('CSTOP', '9'))
from contextlib import ExitStack
import numpy as np
import ml_dtypes
import concourse.bass as bass
import concourse.mybir as mybir
from concourse.bass_utils import run_bass_kernel_spmd

F32 = mybir.dt.float32
BF16 = mybir.dt.bfloat16
U8 = mybir.dt.uint8
AF = mybir.ActivationFunctionType
ALU = mybir.AluOpType

S = 4096
D = 4096
NL = 2
INW = 22528
EPS = 1e-6
NA_SCALE = 128.0 ** -0.5
LN_RSCALE = -0.5 * math.log(128.0)

class Buf:
    def __init__(self, name, lo, hi):
        self.name, self.lo, self.hi = name, lo, hi
        self.init_r = []
        self.dead = False


class Sched:
    K = 20

    def __init__(self, nc, es):
        self.nc = nc
        self.engs = {"pe": nc.tensor, "act": nc.scalar, "dve": nc.vector, "pool": nc.gpsimd, "sp": nc.sync}
        self.sem = {}
        self.semobj = {}
        for e in ["pe", "act", "dve", "pool"]:
            self.semobj["c_" + e] = es.enter_context(nc.semaphore("c_" + e))
            self.sem[e] = "c_" + e
        self.rings = {}
        for q in ["sp", "pool"]:
            self.rings[q] = []
            for i in range(self.K):
                nm = "d_%s%d" % (q, i)
                self.semobj[nm] = es.enter_context(nc.semaphore(nm))
                self.rings[q].append(nm)
        self.cnt = {e: 0 for e in self.sem}
        self.dman = {"sp": 0, "pool": 0}
        self.prog = {e: [] for e in self.engs}
        self.waited = {e: {} for e in self.engs}
        self.res = {}
        self.bufs = []
        self.grave = []
        self.nops = 0

    def newbuf(self, name, lo, size):
        b = Buf(name, lo, lo + size)
        for o in self.bufs:
            if not o.dead and o.lo < b.hi and b.lo < o.hi:
                fin = {}
                def add(ev):
                    if ev is not None and fin.get(ev[0], 0) < ev[1]:
                        fin[ev[0]] = ev[1]
                dk = []
                for k, st in self.res.items():
                    if k[0] is o:
                        add(st["w"])
                        for ev in st["r"]:
                            add(ev)
                        dk.append(k)
                for k in dk:
                    del self.res[k]
                for ev in o.init_r:
                    add(ev)
                o.final = list(fin.items())
                o.dead = True
                self.grave.append(o)
        self.bufs = [o for o in self.bufs if not o.dead]
        fin = {}
        for g in self.grave:
            if g.lo < b.hi and b.lo < g.hi:
                for s, v in g.final:
                    if fin.get(s, 0) < v:
                        fin[s] = v
        b.init_r = list(fin.items())
        self.bufs.append(b)
        return b

    def _st(self, key):
        st = self.res.get(key)
        if st is None:
            b = key[0]
            init = list(b.init_r) if isinstance(b, Buf) else []
            st = {"w": None, "r": init}
            self.res[key] = st
        if isinstance(key[0], Buf):
            assert not key[0].dead, key[0].name
        return st

    def _deps(self, eng, reads, writes):
        deps = {}
        def add(ev):
            if ev is None:
                return
            s, v = ev
            if deps.get(s, 0) < v:
                deps[s] = v
        for k in reads:
            add(self._st(k)["w"])
        for k in writes:
            st = self._st(k)
            add(st["w"])
            for ev in st["r"]:
                add(ev)
        own = self.sem.get(eng)
        wd = self.waited[eng]
        for s, v in deps.items():
            if eng == "pe" and s == own:
                continue
            if wd.get(s, 0) >= v:
                continue
            wd[s] = v
            self.prog[eng].append(("w", s, v))

    def _record(self, ev, reads, writes):
        for k in reads:
            st = self._st(k)
            rl = st["r"]
            for i, (s, v) in enumerate(rl):
                if s == ev[0]:
                    if v < ev[1]:
                        rl[i] = ev
                    break
            else:
                rl.append(ev)
        for k in writes:
            st = self._st(k)
            st["w"] = ev
            st["r"] = []

    def op(self, eng, fn, reads=(), writes=(), signal=True):
        self.nops += 1
        if any(k[0] == "ps" for k in reads):
            writes = list(writes) + [k for k in reads if k[0] == "ps"]
            reads = [k for k in reads if k[0] != "ps"]
        self._deps(eng, reads, writes)
        s = self.sem[eng]
        if signal:
            self.cnt[eng] += 1
            ev = (s, self.cnt[eng])
            self.prog[eng].append(("o", fn, s))
        else:
            ev = (s, self.cnt[eng] + 1)
            self.prog[eng].append(("o", fn, None))
        self._record(ev, reads, writes)

    def dma(self, q, out, in_, reads=(), writes=()):
        self.nops += 1
        n = self.dman[q]
        self.dman[q] += 1
        s = self.rings[q][n % self.K]
        prev = 16 * (n // self.K)
        wd = self.waited[q]
        if prev > 0 and wd.get(s, 0) < prev:
            wd[s] = prev
            self.prog[q].append(("w", s, prev))
        self._deps(q, reads, writes)
        self.prog[q].append(("d", out, in_, s))
        ev = (s, prev + 16)
        self._record(ev, reads, writes)

    def finish(self):
        for q in ["sp", "pool"]:
            n = self.dman[q]
            for i in range(min(n, self.K)):
                cnt = (n - 1 - i) // self.K + 1
                self.prog[q].append(("w", self.rings[q][i], 16 * cnt))
        n = self.dman["pool"]
        for i in range(min(n, self.K)):
            cnt = (n - 1 - i) // self.K + 1
            self.prog["sp"].append(("w", self.rings["pool"][i], 16 * cnt))

    def emit(self, block):
        so = self.semobj

        def replay(name):
            def run(e):
                for it in self.prog[name]:
                    if it[0] == "w":
                        e.wait_ge(so[it[1]], it[2])
                    elif it[0] == "o":
                        ins = it[1](e)
                        if it[2] is not None:
                            ins.then_inc(so[it[2]], 1)
                    else:
                        e.dma_start(out=it[1], in_=it[2]).then_inc(so[it[3]], 16)
            return run
        block.tensor(replay("pe"))
        block.scalar(replay("act"))
        block.vector(replay("dve"))
        block.gpsimd(replay("pool"))
        block.sync(replay("sp"))


W_SPECS = [
    ("w_in", 4096, INW), ("w_pa", 2048, 4096), ("w_pb", 2048, 4096),
    ("w_out", 4096, 4096), ("w_pg", 4096, 4096), ("w_ple", 256, 4096),
]

CST_W = 2 + 6 * 128


def build_program(debug=False, phases="0ABC", nlayers=NL):
    nc = bass.Bass("TRN2", target_bir_lowering=False)
    es = ExitStack()
    I = {}

    def din(name, shape, dt=F32):
        I[name] = nc.dram_tensor(name, list(shape), dt, kind="ExternalInput").ap()
        return I[name]

    x_in = din("x", [S, D])
    p_in = din("p", [NL, S, 256])
    wsrc = {}
    for nm, K, N in W_SPECS:
        wsrc[nm] = din(nm, [NL, K, N])
    lnpreT = din("ln_preT", [NL, 128, 32])
    lnpost = din("ln_post", [NL, D])
    gn = din("gn", [NL, 2048])
    ldf = din("ldf", [NL, 8])
    ldb = din("ldb", [NL, 8])
    bias_tab = din("bias_tab", [NL, 16, 128, 1024])
    mask_tab = din("mask_tab", [128, 1024])
    cst_in = din("cst", [128, CST_W])
    rope_in = din("rope", [2, S, 64])
    ident_in = din("ident", [128, 128], BF16)
    ones_in = din("ones", [128, 128], BF16)
    y_out = nc.dram_tensor("y", [S, D], F32, kind="ExternalOutput").ap()

    def dscr(name, shape, dt=BF16):
        kind = "ExternalOutput" if (debug and name in ("sF", "nav", "rq", "rk", "rv", "aT", "bT", "xmid")) else "Internal"
        return nc.dram_tensor(name, list(shape), dt, kind=kind).ap()

    wb = {}
    for nm, K, N in W_SPECS:
        wb[nm] = [dscr("wb_%s%d" % (nm, l_), [N // 512, 128, (K // 128) * 512]) for l_ in range(NL)]
    sF = dscr("sF", [16384, S])
    nav = dscr("nav", [S, 2048])
    rq = dscr("rq", [S, 1024])
    rk = dscr("rk", [S, 1024])
    rv = dscr("rv", [S, 2048])
    aT = dscr("aT", [2048, S])
    bT = dscr("bT", [2048, S])
    xmid = dscr("xmid", [S, D], F32)

    ARENA = 204 * 1024
    arena = es.enter_context(nc.sbuf_tensor("arena", [128, ARENA], U8))
    banks = [es.enter_context(nc.psum_tensor("psb%d" % i, [128, 512], F32)) for i in range(8)]
    sc = Sched(nc, es)

    def view(b, dt, shape=None, off=0, n=None):
        esz = 4 if dt == F32 else 2
        lo = b.lo + off * esz
        hi = b.hi if n is None else lo + n * esz
        assert hi <= b.hi
        v = arena[:, lo:hi].bitcast(dt)
        if shape is not None:
            if len(shape) == 2:
                v = v.rearrange("p (a b) -> p a b", a=shape[0], b=shape[1])
            else:
                v = v.rearrange("p (a b c) -> p a b c", a=shape[0], b=shape[1], c=shape[2])
        return v

    def psf(b):
        return banks[b]

    def psb(b):
        return banks[b][:].bitcast(BF16)

    top = [ARENA]

    def palloc(name, size):
        top[0] -= size
        return sc.newbuf(name, top[0], size)

    B_ident = palloc("ident", 256)
    B_ones = palloc("ones", 256)
    B_cst = palloc("cst", CST_W * 4 + 8)
    B_mask = palloc("mask", 4096)
    B_small = palloc("small", 2048)
    ident = view(B_ident, BF16)
    ones = view(B_ones, BF16)
    cst = view(B_cst, F32, n=CST_W)
    maskv = view(B_mask, F32)
    smallv = view(B_small, F32)

    sc.dma("sp", ident, ident_in, writes=[(B_ident, 0)])
    sc.dma("sp", ones, ones_in, writes=[(B_ones, 0)])
    sc.dma("sp", cst, cst_in, writes=[(B_cst, 0)])
    sc.dma("sp", maskv, mask_tab, writes=[(B_mask, 0)])
    KI, KO, KC_, KM = (B_ident, 0), (B_ones, 0), (B_cst, 0), (B_mask, 0)
    colA, colB = cst[:, 0:1], cst[:, 1:2]
    rowA, rowB = cst[:, 2:130], cst[:, 130:258]
    M1, M2 = cst[:, 258:386], cst[:, 386:514]
    mge, mle = cst[:, 514:642], cst[:, 642:770]

    _sm = [0]

    def small(n):
        o = _sm[0]
        _sm[0] += n
        assert _sm[0] <= 512
        return o

    def phase0(layers):
        for l in layers:
            for nm, K, N in W_SPECS:
                KC = K // 128
                src = wsrc[nm][l].rearrange("(kc p) n -> p kc n", p=128)
                for cb in range(N // 512):
                    dst = wb[nm][l][cb].rearrange("p (kc n) -> p kc n", kc=KC, n=512)
                    sc.dma("pool", dst, src[:, :, cb * 512:(cb + 1) * 512], writes=[("wb", nm, l, cb)])

    def wkeys(nm, l, cb):
        return [("wb", nm, l, cb)]

    def cb_info(cb):
        if cb < 4:
            return ("F", 0 + cb * 512, None)
        if cb < 8:
            return ("F", 2048 + (cb - 4) * 512, None)
        if cb < 12:
            return ("T", nav, (cb - 8) * 512, "copy")
        if cb < 16:
            return ("F", 4096 + (cb - 12) * 512, AF.Silu)
        if cb < 18:
            return ("T", rq, (cb - 16) * 512, "rot")
        if cb < 20:
            return ("T", rk, (cb - 18) * 512, "rot")
        if cb < 24:
            return ("T", rv, (cb - 20) * 512, "copy")
        if cb < 28:
            return ("F", 6144 + (cb - 24) * 512, AF.Silu)
        if cb < 36:
            return ("F", 8192 + (cb - 28) * 512, AF.Sigmoid)
        return ("F", 12288 + (cb - 36) * 512, AF.Sigmoid)

    def phaseA(l, xsrc, xsrc_keys):
        o = 0
        Wb = [sc.newbuf("A_w%d" % i, o + i * 32768, 32768) for i in range(2)]; o += 65536
        Bxx = [sc.newbuf("A_xnT%d" % i, o + i * 32768, 32768) for i in range(2)]; o += 65536
        Bxl = [sc.newbuf("A_xld%d" % i, o + i * 16384, 16384) for i in range(1)]; o += 16384
        Bxs = sc.newbuf("A_xs", o, 8192); o += 8192
        Brope = sc.newbuf("A_rope", o, 16384); o += 16384
        Bst = [sc.newbuf("A_st%d" % i, o + i * 4096, 4096) for i in range(3)]; o += 12288
        Btmp = [sc.newbuf("A_tmp%d" % i, o + i * 1024, 1024) for i in range(2)]; o += 2048
        Bg = sc.newbuf("A_g", o, 128); o += 128
        assert o <= top[0], (o, top[0])
        ropev = view(Brope, F32, shape=(2, 32, 64))
        gT = view(Bg, F32)
        sc.dma("sp", gT, lnpreT[l], writes=[(Bg, 0)])
        sc.dma("sp", ropev[:, 0], rope_in[0].rearrange("(t p) d -> p t d", p=128), writes=[(Brope, 0)])
        sc.dma("sp", ropev[:, 1], rope_in[1].rearrange("(t p) d -> p t d", p=128), writes=[(Brope, 1)])
        ss_o = small(8)
        psrot = [2, 3, 4, 5, 6, 7]
        pi = [0]
        sti = [0]
        evi = [0]
        ldi = [0]
        wl = [0]
        def pro_norm(tt, j):
            tok0 = tt * 512 + j * 128
            bl = Bxl[0]
            xl = view(bl, F32)
            xs = view(Bxs, BF16)
            sc.dma("sp", xl, xsrc[tok0:tok0 + 128, :], reads=xsrc_keys(tok0), writes=[(bl, 0)])
            ssv = smallv[:, ss_o:ss_o + 1]; rsv = smallv[:, ss_o + 1:ss_o + 2]; rstd = smallv[:, ss_o + 2:ss_o + 3]
            sc.op("act", lambda e, o_=xs, i=xl, a=ssv: e.activation(out=o_, in_=i, func=AF.Square, accum_out=a),
                  reads=[(bl, 0)], writes=[(Bxs, 0), (B_small, "ss")])
            sc.op("act", lambda e, o_=rsv, i=ssv: e.activation(out=o_, in_=i, func=AF.Sqrt, scale=1.0 / D, bias=EPS),
                  reads=[(B_small, "ss")], writes=[(B_small, "rs")])
            sc.op("dve", lambda e, o_=rstd, i=rsv: e.reciprocal(out=o_, in_=i), reads=[(B_small, "rs")], writes=[(B_small, "rstd")])
            sc.op("act", lambda e, o_=xs, i=xl, s_=rstd: e.activation(out=o_, in_=i, func=AF.Copy, scale=s_),
                  reads=[(bl, 0), (B_small, "rstd")], writes=[(Bxs, 0)])
        def pro_tr(tt, j):
            Bx = Bxx[tt % 2]
            xnT = view(Bx, BF16, shape=(32, 512))
            xs = view(Bxs, BF16)
            for g in range(8):
                pb = g % 2
                pv = psb(pb)
                for q in range(4):
                    kc = g * 4 + q
                    sc.op("pe", lambda e, o_=pv[:, q * 128:(q + 1) * 128], i=xs[:, kc * 128:(kc + 1) * 128]: e.transpose(out=o_, in_=i, identity=ident),
                          reads=[(Bxs, 0), KI], writes=[("ps", pb)], signal=(q == 3))
                for q in range(4):
                    kc = g * 4 + q
                    dst = xnT[:, kc, j * 128:(j + 1) * 128]
                    srcp = pv[:, q * 128:(q + 1) * 128]
                    if g % 2 == 0:
                        sc.op("dve", lambda e, o_=dst, i=srcp, s_=gT[:, kc:kc + 1]: e.tensor_scalar(out=o_, in0=i, scalar1=s_, scalar2=None, op0=ALU.mult),
                              reads=[("ps", pb), (Bg, 0)], writes=[(Bx, j)])
                    else:
                        sc.op("act", lambda e, o_=dst, i=srcp, s_=gT[:, kc:kc + 1]: e.activation(out=o_, in_=i, func=AF.Copy, scale=s_),
                              reads=[("ps", pb), (Bg, 0)], writes=[(Bx, j)])
        for tt in range(8):
            if tt == 0:
                for j in range(4):
                    pro_norm(0, j)
                    pro_tr(0, j)
            Bx = Bxx[tt % 2]
            xnT = view(Bx, BF16, shape=(32, 512))
            xkeys = [(Bx, j) for j in range(4)]
            for cb in range(INW // 512):
                info = cb_info(cb)
                gi = tt * 44 + cb
                while wl[0] <= min(gi + 1, 8 * 44 - 1):
                    g_ = wl[0]; wl[0] += 1
                    ws_ = Wb[g_ % 2]
                    sc.dma("sp", view(ws_, BF16), wb["w_in"][l][g_ % 44], reads=wkeys("w_in", l, g_ % 44), writes=[(ws_, 0)])
                wslot = Wb[gi % 2]
                wv = view(wslot, BF16, shape=(32, 512))
                bst = Bst[sti[0] % 3]; sti[0] += 1
                stv = view(bst, BF16, shape=(4, 512))
                for sub in range(4):
                    pb = psrot[pi[0] % 6]; pi[0] += 1
                    ps = psf(pb)
                    for kc in range(32):
                        if info[0] == "F":
                            lhsT, rhs = wv[:, kc, sub * 128:(sub + 1) * 128], xnT[:, kc, :]
                        else:
                            lhsT, rhs = xnT[:, kc, sub * 128:(sub + 1) * 128], wv[:, kc, :]
                        sc.op("pe", lambda e, o_=ps[:], a=lhsT, b=rhs, st=(kc == 0), sp_=(kc == 31): e.matmul(o_, a, b, start=st, stop=sp_),
                              reads=[(wslot, 0)] + xkeys, writes=[("ps", pb)], signal=(kc == 31))
                    dst = stv[:, sub, :]
                    if info[0] == "F" or info[3] == "copy":
                        fn = info[2] if info[0] == "F" else None
                        if fn is None:
                            if evi[0] % 2 == 0:
                                sc.op("dve", lambda e, o_=dst, i=ps[:]: e.tensor_copy(out=o_, in_=i), reads=[("ps", pb)], writes=[(bst, sub)])
                            else:
                                sc.op("act", lambda e, o_=dst, i=ps[:]: e.copy(out=o_, in_=i), reads=[("ps", pb)], writes=[(bst, sub)])
                            evi[0] += 1
                        else:
                            sc.op("act", lambda e, o_=dst, i=ps[:], f=fn: e.activation(out=o_, in_=i, func=f), reads=[("ps", pb)], writes=[(bst, sub)])
                    else:
                        ti = tt * 4 + sub
                        p3 = ps[:].rearrange("p (h d) -> p h d", h=4, d=128)
                        d3 = dst.rearrange("p (h d) -> p h d", h=4, d=128)
                        cosb = ropev[:, 0, ti, :].unsqueeze(1).to_broadcast([128, 4, 64])
                        sinb = ropev[:, 1, ti, :].unsqueeze(1).to_broadcast([128, 4, 64])
                        tA = view(Btmp[0], F32, shape=(4, 64)); tB = view(Btmp[1], F32, shape=(4, 64))
                        t1, t2 = p3[:, :, 0:64], p3[:, :, 64:128]
                        rk_ = [(Brope, 0), (Brope, 1)]
                        sc.op("dve", lambda e, o_=tA, a=t1, b=cosb: e.tensor_tensor(out=o_, in0=a, in1=b, op=ALU.mult), reads=[("ps", pb)] + rk_, writes=[(Btmp[0], 0)])
                        sc.op("dve", lambda e, o_=tB, a=t2, b=sinb: e.tensor_tensor(out=o_, in0=a, in1=b, op=ALU.mult), reads=[("ps", pb)] + rk_, writes=[(Btmp[1], 0)])
                        sc.op("dve", lambda e, o_=d3[:, :, 0:64], a=tA, b=tB: e.tensor_tensor(out=o_, in0=a, in1=b, op=ALU.subtract),
                              reads=[(Btmp[0], 0), (Btmp[1], 0)], writes=[(bst, sub)])
                        sc.op("dve", lambda e, o_=tA, a=t1, b=sinb: e.tensor_tensor(out=o_, in0=a, in1=b, op=ALU.mult), reads=[("ps", pb)] + rk_, writes=[(Btmp[0], 0)])
                        sc.op("dve", lambda e, o_=tB, a=t2, b=cosb: e.tensor_tensor(out=o_, in0=a, in1=b, op=ALU.mult), reads=[("ps", pb)] + rk_, writes=[(Btmp[1], 0)])
                        sc.op("dve", lambda e, o_=d3[:, :, 64:128], a=tA, b=tB: e.tensor_tensor(out=o_, in0=a, in1=b, op=ALU.add),
                              reads=[(Btmp[0], 0), (Btmp[1], 0)], writes=[(bst, sub)])
                skeys = [(bst, s_) for s_ in range(4)]
                if info[0] == "F":
                    r0 = info[1]
                    dstd = sF[r0:r0 + 512, tt * 512:(tt + 1) * 512].rearrange("(s p) t -> p s t", p=128)
                    wk = [("sF", r0 // 128 + s_, tt) for s_ in range(4)]
                else:
                    c0 = info[2]
                    dstd = info[1][tt * 512:(tt + 1) * 512, c0:c0 + 512].rearrange("(j p) c -> p j c", p=128)
                    wk = [("TM", cb, tt)]
                sc.dma("sp", dstd, stv, reads=skeys, writes=wk)
                if tt + 1 < 8:
                    if cb in (4, 12, 20, 28):
                        pro_norm(tt + 1, (cb - 4) // 8)
                    if cb in (8, 16, 24, 32):
                        pro_tr(tt + 1, (cb - 8) // 8)

    def sF_keys(rowblk):
        return [("sF", rowblk, tt) for tt in range(8)]

    def tm_keys(cbs):
        return [("TM", cb, tt) for cb in cbs for tt in range(8)]

    def phaseB1(l):
        o = 0
        sets = []
        for i in range(2):
            d = {}
            for nm, sz in [("kT", 8192), ("qT", 8192), ("v", 8192), ("vsh", 8192), ("gT", 8192), ("aT", 8192), ("E", 4096)]:
                d[nm] = sc.newbuf("B1_%s%d" % (nm, i), o, sz); o += sz
            sets.append(d)
        Bpe = [sc.newbuf("B1_pexp%d" % i, o + i * 1024, 1024) for i in range(2)]; o += 2048
        Bp = [sc.newbuf("B1_p%d" % i, o + i * 512, 512) for i in range(2)]; o += 1024
        Brc = [sc.newbuf("B1_rc%d" % i, o + i * 256, 256) for i in range(2)]; o += 512
        assert o <= top[0]
        NIT = 16 * 64
        hv = {}

        def head_setup(h):
            d = sets[h % 2]
            kT = view(d["kT"], BF16); qT = view(d["qT"], BF16); gT = view(d["gT"], BF16); aTh = view(d["aT"], BF16)
            v = view(d["v"], BF16, shape=(32, 128)); vsh = view(d["vsh"], BF16, shape=(32, 128))
            E = view(d["E"], F32)
            E3 = view(d["E"], F32, shape=(2, 512))
            sc.dma("sp", qT, sF[h * 128:(h + 1) * 128, :], reads=sF_keys(h), writes=[(d["qT"], 0)])
            sc.dma("sp", kT, sF[2048 + h * 128:2048 + (h + 1) * 128, :], reads=sF_keys(16 + h), writes=[(d["kT"], 0)])
            sc.dma("sp", gT, sF[4096 + h * 128:4096 + (h + 1) * 128, :], reads=sF_keys(32 + h), writes=[(d["gT"], 0)])
            vk = tm_keys([8 + h // 4])
            sc.dma("sp", v, nav[:, h * 128:(h + 1) * 128].rearrange("(c p) d -> p c d", p=128), reads=vk, writes=[(d["v"], 0)])
            sc.dma("sp", vsh[:, 0:31, :], nav[64:64 + 31 * 128, h * 128:(h + 1) * 128].rearrange("(c p) d -> p c d", p=128), reads=vk, writes=[(d["vsh"], 0)])
            sc.dma("sp", E, bias_tab[l, h], writes=[(d["E"], 0)])
            sc.op("act", lambda e, o_=E: e.activation(out=o_, in_=o_, func=AF.Exp), reads=[(d["E"], 0)], writes=[(d["E"], 0)])
            sc.op("dve", lambda e, o_=E: e.tensor_tensor(out=o_, in0=o_, in1=maskv, op=ALU.mult), reads=[(d["E"], 0), KM], writes=[(d["E"], 0)])
            hv[h] = (d, kT, qT, gT, aTh, v, vsh, E3)

        def geo(k):
            h, r = k // 64, k % 64
            rs = min(max(r - 4, 0), 56)
            base = rs - r + 7
            return h, r, rs, base % 2, base // 2

        def emit_S(k):
            h, r, rs, tab, i0_ = geo(k)
            if r == 0 and h == 0:
                head_setup(0)
            if r == 16 and h + 1 < 16:
                head_setup(h + 1)
            d, kT, qT = hv[h][0], hv[h][1], hv[h][2]
            sb_ = k % 2
            Sps = psf(sb_)
            for m in range(4):
                kr = rs + 2 * m
                sc.op("pe", lambda e, o_=Sps[:, m * 64:(m + 1) * 64], a=kT[:, kr * 64:kr * 64 + 128], b=qT[:, r * 64:(r + 1) * 64]:
                      e.matmul(o_, a, b, start=True, stop=True),
                      reads=[(d["kT"], 0), (d["qT"], 0)], writes=[("ps", sb_)], signal=(m == 3))

        def emit_mid(k):
            h, r, rs, tab, i0_ = geo(k)
            d, E3 = hv[h][0], hv[h][7]
            sb_ = k % 2
            Sps = psf(sb_)
            pe_ = view(Bpe[k % 2], F32)
            pp = view(Bp[k % 2], BF16)
            sc.op("act", lambda e, o_=pe_, i=Sps[:, 0:256]: e.activation(out=o_, in_=i, func=AF.Exp, scale=NA_SCALE),
                  reads=[("ps", sb_)], writes=[(Bpe[k % 2], 0)])
            sc.op("dve", lambda e, o_=pp, a=pe_, b=E3[:, tab, i0_ * 64:(i0_ + 4) * 64]: e.tensor_tensor(out=o_, in0=a, in1=b, op=ALU.mult),
                  reads=[(Bpe[k % 2], 0), (d["E"], 0)], writes=[(Bp[k % 2], 0)])

        def emit_O(k):
            h, r, rs, tab, i0_ = geo(k)
            d, kT, qT, gT, aTh, v, vsh, E3 = hv[h]
            ob_ = 2 + k % 2
            Ops = psf(ob_)
            pp = view(Bp[k % 2], BF16)
            for m in range(4):
                kr = rs + 2 * m
                vc = v[:, kr // 2, :] if kr % 2 == 0 else vsh[:, (kr - 1) // 2, :]
                sc.op("pe", lambda e, o_=Ops[:, 0:64], a=vc, b=pp[:, m * 64:(m + 1) * 64], st=(m == 0), sp_=(m == 3): e.matmul(o_, a, b, start=st, stop=sp_),
                      reads=[(d["v"], 0), (d["vsh"], 0), (Bp[k % 2], 0)], writes=[("ps", ob_)], signal=False)
            for m in range(4):
                sc.op("pe", lambda e, o_=Ops[:, 64:128], b=pp[:, m * 64:(m + 1) * 64], st=(m == 0), sp_=(m == 3): e.matmul(o_, ones, b, start=st, stop=sp_),
                      reads=[KO, (Bp[k % 2], 0)], writes=[("ps", ob_)], signal=(m == 3))
            rc = view(Brc[k % 2], F32)
            sc.op("dve", lambda e, o_=rc, i=Ops[:, 64:128]: e.reciprocal(out=o_, in_=i), reads=[("ps", ob_)], writes=[(Brc[k % 2], 0)])
            sc.op("dve", lambda e, o_=rc, a=rc, b=gT[:, r * 64:(r + 1) * 64]: e.tensor_tensor(out=o_, in0=a, in1=b, op=ALU.mult),
                  reads=[(Brc[k % 2], 0), (d["gT"], 0)], writes=[(Brc[k % 2], 0)])
            sc.op("dve", lambda e, o_=aTh[:, r * 64:(r + 1) * 64], a=Ops[:, 0:64], b=rc: e.tensor_tensor(out=o_, in0=a, in1=b, op=ALU.mult),
                  reads=[("ps", ob_), (Brc[k % 2], 0)], writes=[(d["aT"], 0)])
            if r == 63:
                sc.dma("pool", aT[h * 128:(h + 1) * 128, :], aTh, reads=[(d["aT"], 0)], writes=[("aT", h)])

        emit_S(0); emit_S(1); emit_mid(0)
        for k in range(NIT):
            if k + 2 < NIT:
                emit_S(k + 2)
            if k + 1 < NIT:
                emit_mid(k + 1)
            emit_O(k)

    def phaseB2(l):
        o = 0
        def nb(nm, sz):
            nonlocal o
            b = sc.newbuf("B2_" + nm, o, sz); o += sz
            return b
        Brq, Brk, Brv, Brg, Bbt = nb("rq", 8192), nb("rk", 8192), nb("rv", 16384), nb("rg", 16384), nb("bt", 16384)
        BqT, BkT, BKf, BKb, BQf, BQb = [nb(n_, 8192) for n_ in ("qT", "kT", "Kf", "Kb", "Qf", "Qb")]
        BSf, BSb = nb("Sf", 16384), nb("Sb", 16384)
        Bqdf, Bqdb, BDT, Bgn = nb("qdf", 4096), nb("qdb", 4096), nb("DT", 4096), nb("gn", 8192)
        Btb = nb("tb", 512)
        Bstate = [nb("st%d" % i, 1024) for i in range(2)]
        Bt1, Bt2 = nb("t1", 512), nb("t2", 512)
        By = [nb("y%d" % i, 512) for i in range(2)]
        Bpt = [nb("pt%d" % i, 256) for i in range(2)]
        Bjk = nb("junk", 512)
        assert o <= top[0], (o, top[0])
        tb = view(Btb, F32)
        nldf, nldb, kdf, kdb, gcf, gcb = [tb[:, i * 8:(i + 1) * 8] for i in range(6)]
        qdf = view(Bqdf, F32, shape=(8, 128)); qdb = view(Bqdb, F32, shape=(8, 128)); DT = view(BDT, F32, shape=(8, 128))
        gnv = view(Bgn, F32)
        t1 = view(Bt1, F32); t2 = view(Bt2, F32)
        sc.dma("sp", nldf, ldf[l:l + 1, :].partition_broadcast(128), writes=[(Btb, "f")])
        sc.dma("sp", nldb, ldb[l:l + 1, :].partition_broadcast(128), writes=[(Btb, "b")])
        sc.dma("sp", gnv, gn[l:l + 1, :].partition_broadcast(128), writes=[(Bgn, 0)])
        for nm, ap_ in (("f", nldf), ("b", nldb)):
            sc.op("act", lambda e, o_=ap_: e.activation(out=o_, in_=o_, func=AF.Abs), reads=[(Btb, nm)], writes=[(Btb, nm)])
            sc.op("dve", lambda e, o_=ap_: e.tensor_scalar(out=o_, in0=o_, scalar1=-1.0, scalar2=None, op0=ALU.mult),
                  reads=[(Btb, nm)], writes=[(Btb, nm)])
        sc.op("act", lambda e: e.activation(out=kdf, in_=nldf, func=AF.Exp, scale=colA, bias=LN_RSCALE), reads=[(Btb, "f"), KC_], writes=[(Btb, "kdf")])
        sc.op("act", lambda e: e.activation(out=kdb, in_=nldb, func=AF.Exp, scale=colB, bias=LN_RSCALE), reads=[(Btb, "b"), KC_], writes=[(Btb, "kdb")])
        sc.op("act", lambda e: e.activation(out=gcf, in_=nldf, func=AF.Exp, scale=128.0), reads=[(Btb, "f")], writes=[(Btb, "gcf")])
        sc.op("act", lambda e: e.activation(out=gcb, in_=nldb, func=AF.Exp, scale=128.0), reads=[(Btb, "b")], writes=[(Btb, "gcb")])
        for h in range(8):
            sc.op("act", lambda e, o_=qdf[:, h, :], s_=nldf[:, h:h + 1]: e.activation(out=o_, in_=rowA, func=AF.Exp, scale=s_), reads=[(Btb, "f"), KC_], writes=[(Bqdf, h)])
            sc.op("act", lambda e, o_=qdb[:, h, :], s_=nldb[:, h:h + 1]: e.activation(out=o_, in_=rowB, func=AF.Exp, scale=s_), reads=[(Btb, "b"), KC_], writes=[(Bqdb, h)])
            sc.op("act", lambda e, s_=nldf[:, h:h + 1]: e.activation(out=t1, in_=M1, func=AF.Exp, scale=s_, bias=LN_RSCALE), reads=[(Btb, "f"), KC_], writes=[(Bt1, 0)])
            sc.op("dve", lambda e: e.tensor_tensor(out=t1, in0=t1, in1=mge, op=ALU.mult), reads=[(Bt1, 0), KC_], writes=[(Bt1, 0)])
            sc.op("act", lambda e, s_=nldb[:, h:h + 1]: e.activation(out=t2, in_=M2, func=AF.Exp, scale=s_, bias=LN_RSCALE), reads=[(Btb, "b"), KC_], writes=[(Bt2, 0)])
            sc.op("dve", lambda e: e.tensor_tensor(out=t2, in0=t2, in1=mle, op=ALU.mult), reads=[(Bt2, 0), KC_], writes=[(Bt2, 0)])
            sc.op("dve", lambda e, o_=DT[:, h, :]: e.tensor_tensor(out=o_, in0=t1, in1=t2, op=ALU.add), reads=[(Bt1, 0), (Bt2, 0)], writes=[(BDT, h)])
        pk = [0]
        for h in range(8):
            rqv = view(Brq, BF16, shape=(32, 128)); rkv = view(Brk, BF16, shape=(32, 128)); rvv = view(Brv, BF16, shape=(32, 256))
            rgv = view(Brg, BF16, shape=(2, S)); btv = view(Bbt, BF16, shape=(2, S))
            qT = view(BqT, BF16); kT = view(BkT, BF16)
            Kf = view(BKf, BF16, shape=(32, 128)); Kb = view(BKb, BF16, shape=(32, 128))
            Qf = view(BQf, BF16, shape=(32, 128)); Qb = view(BQb, BF16, shape=(32, 128))
            Sf = view(BSf, BF16, shape=(32, 256)); Sb = view(BSb, BF16, shape=(32, 256))
            sc.dma("sp", rqv, rq[:, h * 128:(h + 1) * 128].rearrange("(c p) d -> p c d", p=128), reads=tm_keys([16 + h // 4]), writes=[(Brq, 0)])
            sc.dma("sp", rkv, rk[:, h * 128:(h + 1) * 128].rearrange("(c p) d -> p c d", p=128), reads=tm_keys([18 + h // 4]), writes=[(Brk, 0)])
            sc.dma("sp", rvv, rv[:, h * 256:(h + 1) * 256].rearrange("(c p) d -> p c d", p=128), reads=tm_keys([20 + h // 2]), writes=[(Brv, 0)])
            sc.dma("sp", rgv, sF[6144 + h * 256:6144 + (h + 1) * 256, :].rearrange("(j p) t -> p j t", p=128),
                   reads=sF_keys(48 + 2 * h) + sF_keys(49 + 2 * h), writes=[(Brg, 0)])
            for (src3, srck, dstv, dstk) in ((rqv, Brq, qT, BqT), (rkv, Brk, kT, BkT)):
                for g in range(8):
                    pb = pk[0] % 2; pk[0] += 1
                    pv = psb(pb)
                    for q_ in range(4):
                        c = g * 4 + q_
                        sc.op("pe", lambda e, o_=pv[:, q_ * 128:(q_ + 1) * 128], i=src3[:, c, :]: e.transpose(out=o_, in_=i, identity=ident),
                              reads=[(srck, 0), KI], writes=[("ps", pb)], signal=(q_ == 3))
                    if g % 2 == 0:
                        sc.op("dve", lambda e, o_=dstv[:, g * 512:(g + 1) * 512], i=pv[:, 0:512]: e.tensor_copy(out=o_, in_=i), reads=[("ps", pb)], writes=[(dstk, g)])
                    else:
                        sc.op("act", lambda e, o_=dstv[:, g * 512:(g + 1) * 512], i=pv[:, 0:512]: e.copy(out=o_, in_=i), reads=[("ps", pb)], writes=[(dstk, g)])
            qTk = [(BqT, g) for g in range(8)]; kTk = [(BkT, g) for g in range(8)]
            rk2 = view(Brk, BF16)
            sc.op("dve", lambda e, o_=view(BKf, BF16), s_=kdf[:, h:h + 1]: e.tensor_scalar(out=o_, in0=rk2, scalar1=s_, scalar2=None, op0=ALU.mult),
                  reads=[(Brk, 0), (Btb, "kdf")], writes=[(BKf, 0)])
            sc.op("dve", lambda e, o_=view(BKb, BF16), s_=kdb[:, h:h + 1]: e.tensor_scalar(out=o_, in0=rk2, scalar1=s_, scalar2=None, op0=ALU.mult),
                  reads=[(Brk, 0), (Btb, "kdb")], writes=[(BKb, 0)])
            qT3 = view(BqT, BF16, shape=(32, 128))
            sc.op("dve", lambda e, b=qdf[:, h, :].unsqueeze(1).to_broadcast([128, 32, 128]): e.tensor_tensor(out=Qf, in0=qT3, in1=b, op=ALU.mult),
                  reads=qTk + [(Bqdf, h)], writes=[(BQf, 0)])
            sc.op("dve", lambda e, b=qdb[:, h, :].unsqueeze(1).to_broadcast([128, 32, 128]): e.tensor_tensor(out=Qb, in0=qT3, in1=b, op=ALU.mult),
                  reads=qTk + [(Bqdb, h)], writes=[(BQb, 0)])
            for (Kd, Kdk, Sd, Sdk, gc, order) in ((Kf, BKf, Sf, BSf, gcf, list(range(32))), (Kb, BKb, Sb, BSb, gcb, list(range(31, -1, -1)))):
                first = order[0]
                sc.op("dve", lambda e, o_=Sd[:, first, :]: e.memset(o_, 0.0), writes=[(Sdk, first)])
                prev_state = None
                for idx in range(31):
                    n = order[idx]; nxt = order[idx + 1]
                    pb = 4 + pk[0] % 2; pk[0] += 1
                    ps = psf(pb)
                    sc.op("pe", lambda e, o_=ps[:, 0:256], a=Kd[:, n, :], b=rvv[:, n, :]: e.matmul(o_, a, b, start=True, stop=True),
                          reads=[(Kdk, 0), (Brv, 0)], writes=[("ps", pb)])
                    bs = Bstate[idx % 2]
                    stv_ = view(bs, F32)
                    if prev_state is None:
                        sc.op("dve", lambda e, o_=stv_, i=ps[:, 0:256]: e.tensor_copy(out=o_, in_=i), reads=[("ps", pb)], writes=[(bs, 0)])
                    else:
                        pst = view(prev_state, F32)
                        sc.op("dve", lambda e, o_=stv_, a=pst, s_=gc[:, h:h + 1], b=ps[:, 0:256]: e.scalar_tensor_tensor(out=o_, in0=a, scalar=s_, in1=b, op0=ALU.mult, op1=ALU.add),
                              reads=[(prev_state, 0), ("ps", pb), (Btb, "gcf"), (Btb, "gcb")], writes=[(bs, 0)])
                    sc.op("act", lambda e, o_=Sd[:, nxt, :], i=stv_: e.copy(out=o_, in_=i), reads=[(bs, 0)], writes=[(Sdk, nxt)])
                    prev_state = bs
            ss_o = 16
            ssv = smallv[:, ss_o:ss_o + 1]; rsv = smallv[:, ss_o + 1:ss_o + 2]; rstd = smallv[:, ss_o + 2:ss_o + 3]
            jk = view(Bjk, BF16)

            def c_ST(n):
                pb = 2 + n % 2
                ps = psf(pb)
                sc.op("pe", lambda e, o_=ps[:, 0:128], a=kT[:, n * 128:(n + 1) * 128], b=qT[:, n * 128:(n + 1) * 128]: e.matmul(o_, a, b, start=True, stop=True),
                      reads=kTk + qTk, writes=[("ps", pb)])

            def c_pt(n):
                pb = 2 + n % 2
                ps = psf(pb)
                pt = view(Bpt[n % 2], BF16)
                sc.op("dve", lambda e, o_=pt, a=ps[:, 0:128], b=DT[:, h, :]: e.tensor_tensor(out=o_, in0=a, in1=b, op=ALU.mult),
                      reads=[("ps", pb), (BDT, h)], writes=[(Bpt[n % 2], 0)])

            def c_O(n):
                pt = view(Bpt[n % 2], BF16)
                ob = 6 + n % 2
                po = psf(ob)
                sc.op("pe", lambda e, o_=po[:, 0:256], a=pt, b=rvv[:, n, :]: e.matmul(o_, a, b, start=True, stop=False),
                      reads=[(Bpt[n % 2], 0), (Brv, 0)], writes=[("ps", ob)], signal=False)
                sc.op("pe", lambda e, o_=po[:, 0:256], a=Qf[:, n, :], b=Sf[:, n, :]: e.matmul(o_, a, b, start=False, stop=False),
                      reads=[(BQf, 0), (BSf, n)], writes=[("ps", ob)], signal=False)
                sc.op("pe", lambda e, o_=po[:, 0:256], a=Qb[:, n, :], b=Sb[:, n, :]: e.matmul(o_, a, b, start=False, stop=True),
                      reads=[(BQb, 0), (BSb, n)], writes=[("ps", ob)])
                sc.op("act", lambda e, i=po[:, 0:256]: e.activation(out=jk, in_=i, func=AF.Square, accum_out=ssv), reads=[("ps", ob)], writes=[(Bjk, 0), (B_small, "ss2")])
                sc.op("act", lambda e: e.activation(out=rsv, in_=ssv, func=AF.Sqrt, scale=1.0 / 256, bias=EPS), reads=[(B_small, "ss2")], writes=[(B_small, "rs2")])
                sc.op("dve", lambda e: e.reciprocal(out=rstd, in_=rsv), reads=[(B_small, "rs2")], writes=[(B_small, "rstd2")])
                yb = By[n % 2]
                yv = view(yb, BF16)
                sc.op("dve", lambda e, o_=yv, a=po[:, 0:256], b=gnv[:, h * 256:(h + 1) * 256]: e.scalar_tensor_tensor(out=o_, in0=a, scalar=rstd, in1=b, op0=ALU.mult, op1=ALU.mult),
                      reads=[("ps", ob), (B_small, "rstd2"), (Bgn, 0)], writes=[(yb, 0)])

            def c_TR(n):
                yb = By[n % 2]
                yv = view(yb, BF16)
                tb_ = n % 2
                pv = psb(tb_)
                for j in range(2):
                    sc.op("pe", lambda e, o_=pv[:, j * 128:(j + 1) * 128], i=yv[:, j * 128:(j + 1) * 128]: e.transpose(out=o_, in_=i, identity=ident),
                          reads=[(yb, 0), KI], writes=[("ps", tb_)], signal=(j == 1))
                sc.op("dve", lambda e, o_=btv[:, :, n * 128:(n + 1) * 128], a=pv[:, 0:256].rearrange("p (j t) -> p j t", j=2, t=128), b=rgv[:, :, n * 128:(n + 1) * 128]:
                      e.tensor_tensor(out=o_, in0=a, in1=b, op=ALU.mult),
                      reads=[("ps", tb_), (Brg, 0)], writes=[(Bbt, 0)])

            c_ST(0); c_ST(1); c_pt(0)
            for n in range(32):
                if n + 2 < 32:
                    c_ST(n + 2)
                if n + 1 < 32:
                    c_pt(n + 1)
                c_O(n)
                if n >= 1:
                    c_TR(n - 1)
            c_TR(31)
            sc.dma("pool", bT[h * 256:(h + 1) * 256, :].rearrange("(j p) t -> p j t", p=128), btv, reads=[(Bbt, 0)], writes=[("bT", h)])

    def phaseC(l, xsrc, xsrc_keys, ydst, ykey):
        T = 256
        o = 0
        def nb(nm, sz):
            nonlocal o
            b = sc.newbuf("C_" + nm, o, sz); o += sz
            return b
        Wh = [nb("wh%d" % i, 16384) for i in range(4)]
        Bwp = [nb("wple%d" % i, 2048) for i in range(2)]
        Bmo = nb("mo", 32768)
        Bxin = nb("xin", 16384)
        Blp = nb("lnpost", 16384)
        Bmg = nb("merged", 16384)
        Bab = nb("ab", 16384)
        Bx1b = nb("x1b", 8192)
        Bgt = [nb("gt%d" % i, 4096) for i in range(2)]
        Bm1 = [nb("m1_%d" % i, 1024) for i in range(2)]
        Bm2 = [nb("m2_%d" % i, 1024) for i in range(2)]
        Bsg = [nb("sg%d" % i, 2048) for i in range(2)]
        Bpl = nb("pl", 1024)
        Bplb = nb("plb", 512)
        BpT = nb("pT", 1024)
        Bjk = nb("junk", 1024)
        assert o <= top[0], (o, top[0])
        lpv = view(Blp, F32)
        for hh in range(2):
            sc.dma("sp", lpv[:, hh * 2048:(hh + 1) * 2048], lnpost[l:l + 1, hh * 2048:(hh + 1) * 2048].partition_broadcast(128), writes=[(Blp, hh)])
        mo = view(Bmo, F32, shape=(2, D))
        mg = view(Bmg, BF16, shape=(32, T))
        wi = [0]
        pi = [0]
        ssq_o = 32
        hi_ = [0]
        fi_ = [0]
        pli = [0]

        def wload(nm, cb):
            KC = dict((a, b // 128) for a, b, c in W_SPECS)[nm]
            if KC == 2:
                b = Bwp[pli[0] % 2]; pli[0] += 1
                sc.dma("sp", view(b, BF16, n=1024), wb[nm][l][cb], reads=wkeys(nm, l, cb), writes=[(b, 0)])
                return [(b, 0)], view(b, BF16, shape=(2, 512), n=1024)
            if KC == 16:
                b = Wh[hi_[0] % 4]; hi_[0] += 1
                sc.dma("sp", view(b, BF16, n=KC * 512), wb[nm][l][cb], reads=wkeys(nm, l, cb), writes=[(b, 0)])
                return [(b, 0)], view(b, BF16, shape=(KC, 512), n=KC * 512)
            pair = fi_[0] % 2; fi_[0] += 1
            b0, b1 = Wh[2 * pair], Wh[2 * pair + 1]
            esz = 2
            v_ = arena[:, b0.lo:b1.hi].bitcast(BF16)
            sc.dma("sp", v_, wb[nm][l][cb], reads=wkeys(nm, l, cb), writes=[(b0, 0), (b1, 0)])
            return [(b0, 0), (b1, 0)], v_.rearrange("p (a b) -> p a b", a=32, b=512)
        for tt in range(S // T):
            t0 = tt * T
            aTv = view(Bab, BF16, shape=(16, T), n=16 * T)
            bTv = view(Bab, BF16, shape=(16, T), off=16 * T, n=16 * T)
            sc.dma("sp", aTv, aT[:, t0:t0 + T].rearrange("(k p) t -> p k t", p=128), reads=[("aT", h) for h in range(16)], writes=[(Bab, 0)])
            sc.dma("sp", bTv, bT[:, t0:t0 + T].rearrange("(k p) t -> p k t", p=128), reads=[("bT", h) for h in range(8)], writes=[(Bab, 1)])
            if CSTOP < 1:
                continue
            for cb in range(8):
                sa, wa = wload("w_pa", cb)
                sb2, wbv = wload("w_pb", cb)
                gslot = Bgt[cb % 2]
                gv = view(gslot, BF16, shape=(2, 4, T))
                r0 = 8192 + cb * 512
                sc.dma("sp", gv[:, 0], sF[r0:r0 + 512, t0:t0 + T].rearrange("(s p) t -> p s t", p=128), reads=[("sF", r0 // 128 + s_, t0 // 512) for s_ in range(4)], writes=[(gslot, 0)])
                r1 = 12288 + cb * 512
                sc.dma("sp", gv[:, 1], sF[r1:r1 + 512, t0:t0 + T].rearrange("(s p) t -> p s t", p=128), reads=[("sF", r1 // 128 + s_, t0 // 512) for s_ in range(4)], writes=[(gslot, 1)])
                for sub in range(4):
                    fb = cb * 4 + sub
                    pa = 2 + pi[0] % 3; pbk = 5 + pi[0] % 3; pi[0] += 1
                    for kc in range(16):
                        sc.op("pe", lambda e, o_=psf(pa)[:, 0:T], a=wa[:, kc, sub * 128:(sub + 1) * 128], b=aTv[:, kc, :], st=(kc == 0), sp_=(kc == 15): e.matmul(o_, a, b, start=st, stop=sp_),
                              reads=sa + [(Bab, 0)], writes=[("ps", pa)], signal=(kc == 15))
                    for kc in range(16):
                        sc.op("pe", lambda e, o_=psf(pbk)[:, 0:T], a=wbv[:, kc, sub * 128:(sub + 1) * 128], b=bTv[:, kc, :], st=(kc == 0), sp_=(kc == 15): e.matmul(o_, a, b, start=st, stop=sp_),
                              reads=sb2 + [(Bab, 1)], writes=[("ps", pbk)], signal=(kc == 15))
                    m1 = view(Bm1[fb % 2], F32); m2 = view(Bm2[fb % 2], F32)
                    sc.op("dve", lambda e, o_=m1, a=psf(pa)[:, 0:T], b=gv[:, 0, sub, :]: e.tensor_tensor(out=o_, in0=a, in1=b, op=ALU.mult),
                          reads=[("ps", pa), (gslot, 0)], writes=[(Bm1[fb % 2], 0)])
                    sc.op("dve", lambda e, o_=m2, a=psf(pbk)[:, 0:T], b=gv[:, 1, sub, :]: e.tensor_tensor(out=o_, in0=a, in1=b, op=ALU.mult),
                          reads=[("ps", pbk), (gslot, 1)], writes=[(Bm2[fb % 2], 0)])
                    sc.op("dve", lambda e, o_=mg[:, fb, :], a=m1, b=m2: e.tensor_tensor(out=o_, in0=a, in1=b, op=ALU.add),
                          reads=[(Bm1[fb % 2], 0), (Bm2[fb % 2], 0)], writes=[(Bmg, fb)])
            mgk = [(Bmg, fb) for fb in range(32)]
            if CSTOP < 2:
                continue
            for cb in range(8):
                sw, wv = wload("w_out", cb)
                for j in range(2):
                    pb = 2 + pi[0] % 6; pi[0] += 1
                    ps = psf(pb)
                    for kc in range(32):
                        sc.op("pe", lambda e, o_=ps[:], a=mg[:, kc, j * 128:(j + 1) * 128], b=wv[:, kc, :], st=(kc == 0), sp_=(kc == 31): e.matmul(o_, a, b, start=st, stop=sp_),
                              reads=sw + mgk, writes=[("ps", pb)], signal=(kc == 31))
                    sc.op("dve", lambda e, o_=mo[:, j, cb * 512:(cb + 1) * 512], i=ps[:]: e.tensor_copy(out=o_, in_=i), reads=[("ps", pb)], writes=[(Bmo, (j, cb))])
                    sq = smallv[:, ssq_o + j * 8 + cb:ssq_o + j * 8 + cb + 1]
                    sc.op("act", lambda e, i=mo[:, j, cb * 512:(cb + 1) * 512], a=sq: e.activation(out=view(Bjk, BF16), in_=i, func=AF.Square, accum_out=a),
                          reads=[(Bmo, (j, cb))], writes=[(Bjk, 0), (B_small, ("ssq", j, cb))])
            if CSTOP < 3:
                continue
            for j in range(2):
                tok0 = t0 + j * 128
                tot = smallv[:, 48 + j:49 + j]; rs_ = smallv[:, 50 + j:51 + j]; rstd = smallv[:, 52 + j:53 + j]
                sc.op("dve", lambda e, o_=tot, i=smallv[:, ssq_o + j * 8:ssq_o + j * 8 + 8]: e.tensor_reduce(out=o_, in_=i, axis=mybir.AxisListType.X, op=ALU.add),
                      reads=[(B_small, ("ssq", j, cb)) for cb in range(8)], writes=[(B_small, ("tot", j))])
                sc.op("act", lambda e, o_=rs_, i=tot: e.activation(out=o_, in_=i, func=AF.Sqrt, scale=1.0 / D, bias=EPS), reads=[(B_small, ("tot", j))], writes=[(B_small, ("rs", j))])
                sc.op("dve", lambda e, o_=rstd, i=rs_: e.reciprocal(out=o_, in_=i), reads=[(B_small, ("rs", j))], writes=[(B_small, ("rstd", j))])
                xin = view(Bxin, F32)
                sc.dma("sp", xin, xsrc[tok0:tok0 + 128, :], reads=xsrc_keys(tok0), writes=[(Bxin, 0)])
                mok = [(Bmo, (j, cb)) for cb in range(8)]
                sc.op("dve", lambda e, o_=mo[:, j, :], s_=rstd: e.scalar_tensor_tensor(out=o_, in0=o_, scalar=s_, in1=lpv, op0=ALU.mult, op1=ALU.mult),
                      reads=mok + [(B_small, ("rstd", j)), (Blp, 0), (Blp, 1)], writes=mok)
                sc.op("dve", lambda e, o_=mo[:, j, :], b=xin: e.tensor_tensor(out=o_, in0=o_, in1=b, op=ALU.add), reads=mok + [(Bxin, 0)], writes=mok)
                x1b = view(Bx1b, BF16)
                sc.op("act", lambda e, i=mo[:, j, :]: e.copy(out=x1b, in_=i), reads=mok, writes=[(Bx1b, 0)])
                x1T = view(Bab, BF16, shape=(32, T))
                for g in range(8):
                    pb = g % 2
                    pv = psb(pb)
                    for q in range(4):
                        kc = g * 4 + q
                        sc.op("pe", lambda e, o_=pv[:, q * 128:(q + 1) * 128], i=x1b[:, kc * 128:(kc + 1) * 128]: e.transpose(out=o_, in_=i, identity=ident),
                              reads=[(Bx1b, 0), KI], writes=[("ps", pb)], signal=(q == 3))
                    dst = x1T[:, g * 4:(g + 1) * 4, j * 128:(j + 1) * 128]
                    srcp = pv[:, 0:512].rearrange("p (q t) -> p q t", q=4, t=128)
                    wkk = [(Bab, 0), (Bab, 1)]
                    if g % 2 == 0:
                        sc.op("dve", lambda e, o_=dst, i=srcp: e.tensor_copy(out=o_, in_=i), reads=[("ps", pb)], writes=wkk)
                    else:
                        sc.op("act", lambda e, o_=dst, i=srcp: e.copy(out=o_, in_=i), reads=[("ps", pb)], writes=wkk)
                pl = view(Bpl, F32, n=256)
                plb = view(Bplb, BF16, n=256)
                pT = view(BpT, BF16, shape=(2, T))
                sc.dma("sp", pl, p_in[l, tok0:tok0 + 128, :], writes=[(Bpl, 0)])
                sc.op("act", lambda e: e.copy(out=plb, in_=pl), reads=[(Bpl, 0)], writes=[(Bplb, 0)])
                pv = psb(1)
                for q in range(2):
                    sc.op("pe", lambda e, o_=pv[:, q * 128:(q + 1) * 128], i=plb[:, q * 128:(q + 1) * 128]: e.transpose(out=o_, in_=i, identity=ident),
                          reads=[(Bplb, 0), KI], writes=[("ps", 1)], signal=(q == 1))
                sc.op("dve", lambda e, o_=pT[:, :, j * 128:(j + 1) * 128], i=pv[:, 0:256].rearrange("p (q t) -> p q t", q=2, t=128): e.tensor_copy(out=o_, in_=i),
                      reads=[("ps", 1)], writes=[(BpT, j)])
            x1T = view(Bab, BF16, shape=(32, T))
            x1k = [(Bab, 0), (Bab, 1)]
            if CSTOP < 4:
                continue
            for cb in range(8):
                sg_, wg = wload("w_pg", cb)
                sp2, wp = wload("w_ple", cb)
                for j in range(2):
                    pb = 2 + pi[0] % 6; pi[0] += 1
                    ps = psf(pb)
                    for kc in range(32):
                        sc.op("pe", lambda e, o_=ps[:], a=x1T[:, kc, j * 128:(j + 1) * 128], b=wg[:, kc, :], st=(kc == 0), sp_=(kc == 31): e.matmul(o_, a, b, start=st, stop=sp_),
                              reads=sg_ + x1k, writes=[("ps", pb)], signal=(kc == 31))
                    pb2 = 2 + pi[0] % 6; pi[0] += 1
                    ps2 = psf(pb2)
                    pT = view(BpT, BF16, shape=(2, T))
                    for kc in range(2):
                        sc.op("pe", lambda e, o_=ps2[:], a=pT[:, kc, j * 128:(j + 1) * 128], b=wp[:, kc, :], st=(kc == 0), sp_=(kc == 1): e.matmul(o_, a, b, start=st, stop=sp_),
                              reads=sp2 + [(BpT, j)], writes=[("ps", pb2)], signal=(kc == 1))
                    sgb = Bsg[(cb * 2 + j) % 2]
                    sgv = view(sgb, F32)
                    sc.op("act", lambda e, o_=sgv, i=ps[:]: e.activation(out=o_, in_=i, func=AF.Sigmoid), reads=[("ps", pb)], writes=[(sgb, 0)])
                    sc.op("dve", lambda e, o_=sgv, b=ps2[:]: e.tensor_tensor(out=o_, in0=o_, in1=b, op=ALU.mult), reads=[(sgb, 0), ("ps", pb2)], writes=[(sgb, 0)])
                    xs_ = mo[:, j, cb * 512:(cb + 1) * 512]
                    sc.op("dve", lambda e, o_=xs_, b=sgv: e.tensor_tensor(out=o_, in0=o_, in1=b, op=ALU.add), reads=[(sgb, 0), (Bmo, (j, cb))], writes=[(Bmo, (j, cb))])
            for j in range(2):
                tok0 = t0 + j * 128
                sc.dma("pool", ydst[tok0:tok0 + 128, :], mo[:, j, :], reads=[(Bmo, (j, cb)) for cb in range(8)], writes=[(ykey, tok0 // 128)])

    layers = list(range(nlayers))
    if "0" in phases:
        phase0(layers)
    for l in layers:
        if l == 0:
            xs_ap, xs_keys = x_in, (lambda tok0: [])
        else:
            xs_ap, xs_keys = xmid, (lambda tok0: [("xmid", tok0 // 128)])
        if "A" in phases:
            phaseA(l, xs_ap, xs_keys)
        if "B" in phases:
            phaseB1(l)
            phaseB2(l)
        if "C" in phases:
            if l == nlayers - 1:
                phaseC(l, xs_ap, xs_keys, y_out, "y")
            else:
                phaseC(l, xs_ap, xs_keys, xmid, "xmid")
    sc.finish()
    block = es.enter_context(nc.Block())
    sc.emit(block)
    es.close()
    return nc, sc


def _const_tables():
    c = np.arange(128, dtype=np.float32)
    cst = np.zeros((128, CST_W), np.float32)
    cst[:, 0] = 127.0 - c
    cst[:, 1] = c
    t = np.arange(128, dtype=np.float32)
    cst[:, 2:130] = (t + 1.0)[None, :]
    cst[:, 130:258] = (128.0 - t)[None, :]
    s_ = c[:, None]
    tt = t[None, :]
    cst[:, 258:386] = np.maximum(tt - s_, 0.0)
    cst[:, 386:514] = np.maximum(s_ - tt, 0.0)
    cst[:, 514:642] = (tt >= s_).astype(np.float32)
    cst[:, 642:770] = (s_ >= tt).astype(np.float32)
    half = 64
    inv = (10000.0 ** (-np.arange(half, dtype=np.float32) / half)).astype(np.float32)
    ang = np.arange(S, dtype=np.float32)[:, None] * inv[None, :]
    rope = np.stack([np.cos(ang), np.sin(ang)]).astype(np.float32)
    cols = np.arange(64)
    cs = np.clip(cols - 8, 0, 48)
    valid = (cols[None, :] >= cs[:, None]) & (cols[None, :] < cs[:, None] + 16)
    m = valid.T.astype(np.float32)
    mask = np.tile(np.tile(m, (2, 1))[:, None, :], (1, 16, 1)).reshape(128, 1024)
    return cst, rope, np.ascontiguousarray(mask)


def _bias_table(rpb):
    cols = np.arange(64)
    cidx = np.clip(cols[:, None] - cols[None, :] + 15, 0, 30)
    out = np.zeros((NL, 16, 128, 2, 8, 64), np.float32)
    for tab in range(2):
        for i in range(8):
            for jp in range(2):
                ri = 2 * i + tab + jp
                if ri > 14:
                    continue
                out[:, :, jp * 64:(jp + 1) * 64, tab, i, :] = rpb[:, :, ri][:, :, cidx]
    return out.reshape(NL, 16, 128, 1024)


_CACHE = {}


def kernel(x_prompt, x_sample, p_prompt, p_sample, w_in, ln_pre, ln_post, na_rpb,
           ret_log_decay_fwd, ret_log_decay_bwd, ret_gn_gain, w_proj_a, w_proj_b,
           w_out, w_ple, w_ple_gate):
    if "nc" not in _CACHE:
        _CACHE["nc"] = build_program()[0]
    nc = _CACHE["nc"]
    f = lambda a: np.ascontiguousarray(np.asarray(a, dtype=np.float32))
    cst, rope, mask = _const_tables()
    shared = {
        "w_in": f(w_in), "w_pa": f(w_proj_a), "w_pb": f(w_proj_b), "w_out": f(w_out),
        "w_pg": f(w_ple_gate), "w_ple": f(w_ple),
        "ln_preT": np.ascontiguousarray(f(ln_pre).reshape(NL, 32, 128).transpose(0, 2, 1)),
        "ln_post": f(ln_post), "gn": f(ret_gn_gain), "ldf": f(ret_log_decay_fwd), "ldb": f(ret_log_decay_bwd),
        "bias_tab": _bias_table(f(na_rpb)), "mask_tab": mask, "cst": cst, "rope": rope,
        "ident": np.eye(128, dtype=np.float32).astype(ml_dtypes.bfloat16),
        "ones": np.ones((128, 128), np.float32).astype(ml_dtypes.bfloat16),
    }
    xp, xs_, pp, ps_ = f(x_prompt), f(x_sample), f(p_prompt), f(p_sample)
    seqs = [(xp[i], pp[:, i]) for i in range(4)] + [(xs_[i], ps_[:, i]) for i in range(2)]
    seqs = seqs + [seqs[0], seqs[1]]
    in_maps = []
    for c in range(8):
        d = dict(shared)
        d["x"] = np.ascontiguousarray(seqs[c][0])
        d["p"] = np.ascontiguousarray(seqs[c][1])
        in_maps.append(d)
    res = run_bass_kernel_spmd(nc, in_maps, core_ids=list(range(8)))
    ys = [np.asarray(res.results[c]["y"], dtype=np.float32) for c in range(6)]
    y_prompt = np.stack(ys[0:4])
    y_sample = np.stack(ys[4:6])
    return (y_prompt, y_sample)
```

```python
import math
import os
CSTOP = int(os.environ.get('CSTOP', '9'))
from contextlib import ExitStack
import numpy as np
import ml_dtypes
import concourse.bass as bass
import concourse.mybir as mybir
from concourse.bass_utils import run_bass_kernel_spmd

F32 = mybir.dt.float32
BF16 = mybir.dt.bfloat16
U8 = mybir.dt.uint8
AF = mybir.ActivationFunctionType
ALU = mybir.AluOpType

S = 4096
D = 4096
NL = 2
INW = 22528
EPS = 1e-6
NA_SCALE = 128.0 ** -0.5
LN_RSCALE = -0.5 * math.log(128.0)

class Buf:
    def __init__(self, name, lo, hi):
        self.name, self.lo, self.hi = name, lo, hi
        self.init_r = []
        self.dead = False


class Sched:
    K = 20

    def __init__(self, nc, es):
        self.nc = nc
        self.engs = {"pe": nc.tensor, "act": nc.scalar, "dve": nc.vector, "pool": nc.gpsimd, "sp": nc.sync}
        self.sem = {}
        self.semobj = {}
        for e in ["pe", "act", "dve", "pool"]:
            self.semobj["c_" + e] = es.enter_context(nc.semaphore("c_" + e))
            self.sem[e] = "c_" + e
        self.rings = {}
        for q in ["sp", "pool"]:
            self.rings[q] = []
            for i in range(self.K):
                nm = "d_%s%d" % (q, i)
                self.semobj[nm] = es.enter_context(nc.semaphore(nm))
                self.rings[q].append(nm)
        self.cnt = {e: 0 for e in self.sem}
        self.dman = {"sp": 0, "pool": 0}
        self.prog = {e: [] for e in self.engs}
        self.waited = {e: {} for e in self.engs}
        self.res = {}
        self.bufs = []
        self.grave = []
        self.nops = 0

    def newbuf(self, name, lo, size):
        b = Buf(name, lo, lo + size)
        for o in self.bufs:
            if not o.dead and o.lo < b.hi and b.lo < o.hi:
                fin = {}
                def add(ev):
                    if ev is not None and fin.get(ev[0], 0) < ev[1]:
                        fin[ev[0]] = ev[1]
                dk = []
                for k, st in self.res.items():
                    if k[0] is o:
                        add(st["w"])
                        for ev in st["r"]:
                            add(ev)
                        dk.append(k)
                for k in dk:
                    del self.res[k]
                for ev in o.init_r:
                    add(ev)
                o.final = list(fin.items())
                o.dead = True
                self.grave.append(o)
        self.bufs = [o for o in self.bufs if not o.dead]
        fin = {}
        for g in self.grave:
            if g.lo < b.hi and b.lo < g.hi:
                for s, v in g.final:
                    if fin.get(s, 0) < v:
                        fin[s] = v
        b.init_r = list(fin.items())
        self.bufs.append(b)
        return b

    def _st(self, key):
        st = self.res.get(key)
        if st is None:
            b = key[0]
            init = list(b.init_r) if isinstance(b, Buf) else []
            st = {"w": None, "r": init}
            self.res[key] = st
        if isinstance(key[0], Buf):
            assert not key[0].dead, key[0].name
        return st

    def _deps(self, eng, reads, writes):
        deps = {}
        def add(ev):
            if ev is None:
                return
            s, v = ev
            if deps.get(s, 0) < v:
                deps[s] = v
        for k in reads:
            add(self._st(k)["w"])
        for k in writes:
            st = self._st(k)
            add(st["w"])
            for ev in st["r"]:
                add(ev)
        own = self.sem.get(eng)
        wd = self.waited[eng]
        for s, v in deps.items():
            if eng == "pe" and s == own:
                continue
            if wd.get(s, 0) >= v:
                continue
            wd[s] = v
            self.prog[eng].append(("w", s, v))

    def _record(self, ev, reads, writes):
        for k in reads:
            st = self._st(k)
            rl = st["r"]
            for i, (s, v) in enumerate(rl):
                if s == ev[0]:
                    if v < ev[1]:
                        rl[i] = ev
                    break
            else:
                rl.append(ev)
        for k in writes:
            st = self._st(k)
            st["w"] = ev
            st["r"] = []

    def op(self, eng, fn, reads=(), writes=(), signal=True):
        self.nops += 1
        if any(k[0] == "ps" for k in reads):
            writes = list(writes) + [k for k in reads if k[0] == "ps"]
            reads = [k for k in reads if k[0] != "ps"]
        self._deps(eng, reads, writes)
        s = self.sem[eng]
        if signal:
            self.cnt[eng] += 1
            ev = (s, self.cnt[eng])
            self.prog[eng].append(("o", fn, s))
        else:
            ev = (s, self.cnt[eng] + 1)
            self.prog[eng].append(("o", fn, None))
        self._record(ev, reads, writes)

    def dma(self, q, out, in_, reads=(), writes=()):
        self.nops += 1
        n = self.dman[q]
        self.dman[q] += 1
        s = self.rings[q][n % self.K]
        prev = 16 * (n // self.K)
        wd = self.waited[q]
        if prev > 0 and wd.get(s, 0) < prev:
            wd[s] = prev
            self.prog[q].append(("w", s, prev))
        self._deps(q, reads, writes)
        self.prog[q].append(("d", out, in_, s))
        ev = (s, prev + 16)
        self._record(ev, reads, writes)

    def finish(self):
        for q in ["sp", "pool"]:
            n = self.dman[q]
            for i in range(min(n, self.K)):
                cnt = (n - 1 - i) // self.K + 1
                self.prog[q].append(("w", self.rings[q][i], 16 * cnt))
        n = self.dman["pool"]
        for i in range(min(n, self.K)):
            cnt = (n - 1 - i) // self.K + 1
            self.prog["sp"].append(("w", self.rings["pool"][i], 16 * cnt))

    def emit(self, block):
        so = self.semobj

        def replay(name):
            def run(e):
                for it in self.prog[name]:
                    if it[0] == "w":
                        e.wait_ge(so[it[1]], it[2])
                    elif it[0] == "o":
                        ins = it[1](e)
                        if it[2] is not None:
                            ins.then_inc(so[it[2]], 1)
                    else:
                        e.dma_start(out=it[1], in_=it[2]).then_inc(so[it[3]], 16)
            return run
        block.tensor(replay("pe"))
        block.scalar(replay("act"))
        block.vector(replay("dve"))
        block.gpsimd(replay("pool"))
        block.sync(replay("sp"))


W_SPECS = [
    ("w_in", 4096, INW), ("w_pa", 2048, 4096), ("w_pb", 2048, 4096),
    ("w_out", 4096, 4096), ("w_pg", 4096, 4096), ("w_ple", 256, 4096),
]

CST_W = 2 + 6 * 128


def build_program(debug=False, phases="0ABC", nlayers=NL):
    nc = bass.Bass("TRN2", target_bir_lowering=False)
    es = ExitStack()
    I = {}

    def din(name, shape, dt=F32):
        I[name] = nc.dram_tensor(name, list(shape), dt, kind="ExternalInput").ap()
        return I[name]

    x_in = din("x", [S, D])
    p_in = din("p", [NL, S, 256])
    wsrc = {}
    for nm, K, N in W_SPECS:
        wsrc[nm] = din(nm, [NL, K, N])
    lnpreT = din("ln_preT", [NL, 128, 32])
    lnpost = din("ln_post", [NL, D])
    gn = din("gn", [NL, 2048])
    ldf = din("ldf", [NL, 8])
    ldb = din("ldb", [NL, 8])
    bias_tab = din("bias_tab", [NL, 16, 128, 1024])
    mask_tab = din("mask_tab", [128, 1024])
    cst_in = din("cst", [128, CST_W])
    rope_in = din("rope", [2, S, 64])
    ident_in = din("ident", [128, 128], BF16)
    ones_in = din("ones", [128, 128], BF16)
    y_out = nc.dram_tensor("y", [S, D], F32, kind="ExternalOutput").ap()

    def dscr(name, shape, dt=BF16):
        kind = "ExternalOutput" if (debug and name in ("sF", "nav", "rq", "rk", "rv", "aT", "bT", "xmid")) else "Internal"
        return nc.dram_tensor(name, list(shape), dt, kind=kind).ap()

    wb = {}
    for nm, K, N in W_SPECS:
        wb[nm] = [dscr("wb_%s%d" % (nm, l_), [N // 512, 128, (K // 128) * 512]) for l_ in range(NL)]
    sF = dscr("sF", [16384, S])
    nav = dscr("nav", [S, 2048])
    rq = dscr("rq", [S, 1024])
    rk = dscr("rk", [S, 1024])
    rv = dscr("rv", [S, 2048])
    aT = dscr("aT", [2048, S])
    bT = dscr("bT", [2048, S])
    xmid = dscr("xmid", [S, D], F32)

    ARENA = 204 * 1024
    arena = es.enter_context(nc.sbuf_tensor("arena", [128, ARENA], U8))
    banks = [es.enter_context(nc.psum_tensor("psb%d" % i, [128, 512], F32)) for i in range(8)]
    sc = Sched(nc, es)

    def view(b, dt, shape=None, off=0, n=None):
        esz = 4 if dt == F32 else 2
        lo = b.lo + off * esz
        hi = b.hi if n is None else lo + n * esz
        assert hi <= b.hi
        v = arena[:, lo:hi].bitcast(dt)
        if shape is not None:
            if len(shape) == 2:
                v = v.rearrange("p (a b) -> p a b", a=shape[0], b=shape[1])
            else:
                v = v.rearrange("p (a b c) -> p a b c", a=shape[0], b=shape[1], c=shape[2])
        return v

    def psf(b):
        return banks[b]

    def psb(b):
        return banks[b][:].bitcast(BF16)

    top = [ARENA]

    def palloc(name, size):
        top[0] -= size
        return sc.newbuf(name, top[0], size)

    B_ident = palloc("ident", 256)
    B_ones = palloc("ones", 256)
    B_cst = palloc("cst", CST_W * 4 + 8)
    B_mask = palloc("mask", 4096)
    B_small = palloc("small", 2048)
    ident = view(B_ident, BF16)
    ones = view(B_ones, BF16)
    cst = view(B_cst, F32, n=CST_W)
    maskv = view(B_mask, F32)
    smallv = view(B_small, F32)

    sc.dma("sp", ident, ident_in, writes=[(B_ident, 0)])
    sc.dma("sp", ones, ones_in, writes=[(B_ones, 0)])
    sc.dma("sp", cst, cst_in, writes=[(B_cst, 0)])
    sc.dma("sp", maskv, mask_tab, writes=[(B_mask, 0)])
    KI, KO, KC_, KM = (B_ident, 0), (B_ones, 0), (B_cst, 0), (B_mask, 0)
    sc.op("dve", lambda e: e.tensor_scalar(out=maskv, in0=maskv, scalar1=30000.0, scalar2=-30000.0, op0=ALU.mult, op1=ALU.add),
          reads=[KM], writes=[KM])
    colA, colB = cst[:, 0:1], cst[:, 1:2]
    rowA, rowB = cst[:, 2:130], cst[:, 130:258]
    M1, M2 = cst[:, 258:386], cst[:, 386:514]
    mge, mle = cst[:, 514:642], cst[:, 642:770]

    _sm = [0]

    def small(n):
        o = _sm[0]
        _sm[0] += n
        assert _sm[0] <= 512
        return o

    def phase0(layers):
        for l in layers:
            for nm, K, N in W_SPECS:
                KC = K // 128
                src = wsrc[nm][l].rearrange("(kc p) n -> p kc n", p=128)
                for cb in range(N // 512):
                    dst = wb[nm][l][cb].rearrange("p (kc n) -> p kc n", kc=KC, n=512)
                    sc.dma("pool", dst, src[:, :, cb * 512:(cb + 1) * 512], writes=[("wb", nm, l, cb)])

    def wkeys(nm, l, cb):
        return [("wb", nm, l, cb)]

    def cb_info(cb):
        if cb < 4:
            return ("F", 0 + cb * 512, None)
        if cb < 8:
            return ("F", 2048 + (cb - 4) * 512, None)
        if cb < 12:
            return ("T", nav, (cb - 8) * 512, "copy")
        if cb < 16:
            return ("F", 4096 + (cb - 12) * 512, AF.Silu)
        if cb < 18:
            return ("T", rq, (cb - 16) * 512, "rot")
        if cb < 20:
            return ("T", rk, (cb - 18) * 512, "rot")
        if cb < 24:
            return ("T", rv, (cb - 20) * 512, "copy")
        if cb < 28:
            return ("F", 6144 + (cb - 24) * 512, AF.Silu)
        if cb < 36:
            return ("F", 8192 + (cb - 28) * 512, AF.Sigmoid)
        return ("F", 12288 + (cb - 36) * 512, AF.Sigmoid)

    def phaseA(l, xsrc, xsrc_keys):
        o = 0
        Wb = [sc.newbuf("A_w%d" % i, o + i * 32768, 32768) for i in range(2)]; o += 65536
        Bxx = [sc.newbuf("A_xnT%d" % i, o + i * 32768, 32768) for i in range(2)]; o += 65536
        Bxl = [sc.newbuf("A_xld%d" % i, o + i * 16384, 16384) for i in range(1)]; o += 16384
        Bxs = sc.newbuf("A_xs", o, 8192); o += 8192
        Brope = sc.newbuf("A_rope", o, 16384); o += 16384
        Bst = [sc.newbuf("A_st%d" % i, o + i * 4096, 4096) for i in range(3)]; o += 12288
        Btmp = [sc.newbuf("A_tmp%d" % i, o + i * 1024, 1024) for i in range(2)]; o += 2048
        Bg = sc.newbuf("A_g", o, 128); o += 128
        assert o <= top[0], (o, top[0])
        ropev = view(Brope, F32, shape=(2, 32, 64))
        gT = view(Bg, F32)
        sc.dma("sp", gT, lnpreT[l], writes=[(Bg, 0)])
        sc.dma("sp", ropev[:, 0], rope_in[0].rearrange("(t p) d -> p t d", p=128), writes=[(Brope, 0)])
        sc.dma("sp", ropev[:, 1], rope_in[1].rearrange("(t p) d -> p t d", p=128), writes=[(Brope, 1)])
        ss_o = small(8)
        psrot = [2, 3, 4, 5, 6, 7]
        pi = [0]
        sti = [0]
        evi = [0]
        ldi = [0]
        wl = [0]
        def pro_norm(tt, j):
            tok0 = tt * 512 + j * 128
            bl = Bxl[0]
            xl = view(bl, F32)
            xs = view(Bxs, BF16)
            sc.dma("sp", xl, xsrc[tok0:tok0 + 128, :], reads=xsrc_keys(tok0), writes=[(bl, 0)])
            ssv = smallv[:, ss_o:ss_o + 1]; rsv = smallv[:, ss_o + 1:ss_o + 2]; rstd = smallv[:, ss_o + 2:ss_o + 3]
            sc.op("act", lambda e, o_=xs, i=xl, a=ssv: e.activation(out=o_, in_=i, func=AF.Square, accum_out=a),
                  reads=[(bl, 0)], writes=[(Bxs, 0), (B_small, "ss")])
            sc.op("act", lambda e, o_=rsv, i=ssv: e.activation(out=o_, in_=i, func=AF.Sqrt, scale=1.0 / D, bias=EPS),
                  reads=[(B_small, "ss")], writes=[(B_small, "rs")])
            sc.op("dve", lambda e, o_=rstd, i=rsv: e.reciprocal(out=o_, in_=i), reads=[(B_small, "rs")], writes=[(B_small, "rstd")])
            sc.op("act", lambda e, o_=xs, i=xl, s_=rstd: e.activation(out=o_, in_=i, func=AF.Copy, scale=s_),
                  reads=[(bl, 0), (B_small, "rstd")], writes=[(Bxs, 0)])
        def pro_tr(tt, j):
            Bx = Bxx[tt % 2]
            xnT = view(Bx, BF16, shape=(32, 512))
            xs = view(Bxs, BF16)
            for g in range(8):
                pb = g % 2
                pv = psb(pb)
                for q in range(4):
                    kc = g * 4 + q
                    sc.op("pe", lambda e, o_=pv[:, q * 128:(q + 1) * 128], i=xs[:, kc * 128:(kc + 1) * 128]: e.transpose(out=o_, in_=i, identity=ident),
                          reads=[(Bxs, 0), KI], writes=[("ps", pb)], signal=(q == 3))
                for q in range(4):
                    kc = g * 4 + q
                    dst = xnT[:, kc, j * 128:(j + 1) * 128]
                    srcp = pv[:, q * 128:(q + 1) * 128]
                    if g % 2 == 0:
                        sc.op("dve", lambda e, o_=dst, i=srcp, s_=gT[:, kc:kc + 1]: e.tensor_scalar(out=o_, in0=i, scalar1=s_, scalar2=None, op0=ALU.mult),
                              reads=[("ps", pb), (Bg, 0)], writes=[(Bx, j)])
                    else:
                        sc.op("act", lambda e, o_=dst, i=srcp, s_=gT[:, kc:kc + 1]: e.activation(out=o_, in_=i, func=AF.Copy, scale=s_),
                              reads=[("ps", pb), (Bg, 0)], writes=[(Bx, j)])
        for tt in range(8):
            if tt == 0:
                for j in range(4):
                    pro_norm(0, j)
                    pro_tr(0, j)
            Bx = Bxx[tt % 2]
            xnT = view(Bx, BF16, shape=(32, 512))
            xkeys = [(Bx, j) for j in range(4)]
            for cb in range(INW // 512):
                info = cb_info(cb)
                gi = tt * 44 + cb
                while wl[0] <= min(gi + 1, 8 * 44 - 1):
                    g_ = wl[0]; wl[0] += 1
                    ws_ = Wb[g_ % 2]
                    sc.dma("sp", view(ws_, BF16), wb["w_in"][l][g_ % 44], reads=wkeys("w_in", l, g_ % 44), writes=[(ws_, 0)])
                wslot = Wb[gi % 2]
                wv = view(wslot, BF16, shape=(32, 512))
                bst = Bst[sti[0] % 3]; sti[0] += 1
                stv = view(bst, BF16, shape=(4, 512))
                for sub in range(4):
                    pb = psrot[pi[0] % 6]; pi[0] += 1
                    ps = psf(pb)
                    for kc in range(32):
                        if info[0] == "F":
                            lhsT, rhs = wv[:, kc, sub * 128:(sub + 1) * 128], xnT[:, kc, :]
                        else:
                            lhsT, rhs = xnT[:, kc, sub * 128:(sub + 1) * 128], wv[:, kc, :]
                        sc.op("pe", lambda e, o_=ps[:], a=lhsT, b=rhs, st=(kc == 0), sp_=(kc == 31): e.matmul(o_, a, b, start=st, stop=sp_),
                              reads=[(wslot, 0)] + xkeys, writes=[("ps", pb)], signal=(kc == 31))
                    dst = stv[:, sub, :]
                    if info[0] == "F" or info[3] == "copy":
                        fn = info[2] if info[0] == "F" else None
                        if fn is None:
                            if evi[0] % 2 == 0:
                                sc.op("dve", lambda e, o_=dst, i=ps[:]: e.tensor_copy(out=o_, in_=i), reads=[("ps", pb)], writes=[(bst, sub)])
                            else:
                                sc.op("act", lambda e, o_=dst, i=ps[:]: e.copy(out=o_, in_=i), reads=[("ps", pb)], writes=[(bst, sub)])
                            evi[0] += 1
                        else:
                            sc.op("act", lambda e, o_=dst, i=ps[:], f=fn: e.activation(out=o_, in_=i, func=f), reads=[("ps", pb)], writes=[(bst, sub)])
                    else:
                        ti = tt * 4 + sub
                        p3 = ps[:].rearrange("p (h d) -> p h d", h=4, d=128)
                        d3 = dst.rearrange("p (h d) -> p h d", h=4, d=128)
                        cosb = ropev[:, 0, ti, :].unsqueeze(1).to_broadcast([128, 4, 64])
                        sinb = ropev[:, 1, ti, :].unsqueeze(1).to_broadcast([128, 4, 64])
                        tA = view(Btmp[0], F32, shape=(4, 64)); tB = view(Btmp[1], F32, shape=(4, 64))
                        t1, t2 = p3[:, :, 0:64], p3[:, :, 64:128]
                        rk_ = [(Brope, 0), (Brope, 1)]
                        sc.op("dve", lambda e, o_=tA, a=t1, b=cosb: e.tensor_tensor(out=o_, in0=a, in1=b, op=ALU.mult), reads=[("ps", pb)] + rk_, writes=[(Btmp[0], 0)])
                        sc.op("dve", lambda e, o_=tB, a=t2, b=sinb: e.tensor_tensor(out=o_, in0=a, in1=b, op=ALU.mult), reads=[("ps", pb)] + rk_, writes=[(Btmp[1], 0)])
                        sc.op("dve", lambda e, o_=d3[:, :, 0:64], a=tA, b=tB: e.tensor_tensor(out=o_, in0=a, in1=b, op=ALU.subtract),
                              reads=[(Btmp[0], 0), (Btmp[1], 0)], writes=[(bst, sub)])
                        sc.op("dve", lambda e, o_=tA, a=t1, b=sinb: e.tensor_tensor(out=o_, in0=a, in1=b, op=ALU.mult), reads=[("ps", pb)] + rk_, writes=[(Btmp[0], 0)])
                        sc.op("dve", lambda e, o_=tB, a=t2, b=cosb: e.tensor_tensor(out=o_, in0=a, in1=b, op=ALU.mult), reads=[("ps", pb)] + rk_, writes=[(Btmp[1], 0)])
                        sc.op("dve", lambda e, o_=d3[:, :, 64:128], a=tA, b=tB: e.tensor_tensor(out=o_, in0=a, in1=b, op=ALU.add),
                              reads=[(Btmp[0], 0), (Btmp[1], 0)], writes=[(bst, sub)])
                skeys = [(bst, s_) for s_ in range(4)]
                if info[0] == "F":
                    r0 = info[1]
                    dstd = sF[r0:r0 + 512, tt * 512:(tt + 1) * 512].rearrange("(s p) t -> p s t", p=128)
                    wk = [("sF", r0 // 128 + s_, tt) for s_ in range(4)]
                else:
                    c0 = info[2]
                    dstd = info[1][tt * 512:(tt + 1) * 512, c0:c0 + 512].rearrange("(j p) c -> p j c", p=128)
                    wk = [("TM", cb, tt)]
                sc.dma("sp", dstd, stv, reads=skeys, writes=wk)
                if tt + 1 < 8:
                    if cb in (4, 12, 20, 28):
                        pro_norm(tt + 1, (cb - 4) // 8)
                    if cb in (8, 16, 24, 32):
                        pro_tr(tt + 1, (cb - 8) // 8)

    def sF_keys(rowblk):
        return [("sF", rowblk, tt) for tt in range(8)]

    def tm_keys(cbs):
        return [("TM", cb, tt) for cb in cbs for tt in range(8)]

    def phaseB1(l):
        o = 0
        sets = []
        for i in range(2):
            d = {}
            for nm, sz in [("kT", 8192), ("qT", 8192), ("v", 8192), ("vsh", 8192), ("gT", 8192), ("aT", 8192), ("E", 4096), ("Eb", 2048)]:
                d[nm] = sc.newbuf("B1_%s%d" % (nm, i), o, sz); o += sz
            sets.append(d)
        Bpe = [sc.newbuf("B1_pexp%d" % i, o + i * 1024, 1024) for i in range(2)]; o += 2048
        Bp = [sc.newbuf("B1_p%d" % i, o + i * 512, 512) for i in range(2)]; o += 1024
        Brc = [sc.newbuf("B1_rc%d" % i, o + i * 256, 256) for i in range(2)]; o += 512
        assert o <= top[0]
        NIT = 16 * 64
        hv = {}

        def head_setup(h):
            d = sets[h % 2]
            kT = view(d["kT"], BF16); qT = view(d["qT"], BF16); gT = view(d["gT"], BF16); aTh = view(d["aT"], BF16)
            v = view(d["v"], BF16, shape=(32, 128)); vsh = view(d["vsh"], BF16, shape=(32, 128))
            E = view(d["E"], F32)
            E3 = view(d["E"], F32, shape=(2, 512))
            sc.dma("sp", qT, sF[h * 128:(h + 1) * 128, :], reads=sF_keys(h), writes=[(d["qT"], 0)])
            sc.dma("sp", kT, sF[2048 + h * 128:2048 + (h + 1) * 128, :], reads=sF_keys(16 + h), writes=[(d["kT"], 0)])
            sc.dma("sp", gT, sF[4096 + h * 128:4096 + (h + 1) * 128, :], reads=sF_keys(32 + h), writes=[(d["gT"], 0)])
            vk = tm_keys([8 + h // 4])
            sc.dma("sp", v, nav[:, h * 128:(h + 1) * 128].rearrange("(c p) d -> p c d", p=128), reads=vk, writes=[(d["v"], 0)])
            sc.dma("sp", vsh[:, 0:31, :], nav[64:64 + 31 * 128, h * 128:(h + 1) * 128].rearrange("(c p) d -> p c d", p=128), reads=vk, writes=[(d["vsh"], 0)])
            sc.dma("sp", E, bias_tab[l, h], writes=[(d["E"], 0)])
            Eb = view(d["Eb"], BF16)
            Eb3 = view(d["Eb"], BF16, shape=(2, 512))
            sc.op("dve", lambda e, o_=E: e.tensor_scalar(out=o_, in0=o_, scalar1=1.0 / NA_SCALE, scalar2=None, op0=ALU.mult), reads=[(d["E"], 0)], writes=[(d["E"], 0)])
            sc.op("dve", lambda e, o_=Eb, i=E: e.tensor_tensor(out=o_, in0=i, in1=maskv, op=ALU.add), reads=[(d["E"], 0), KM], writes=[(d["Eb"], 0)])
            hv[h] = (d, kT, qT, gT, aTh, v, vsh, Eb3)

        def geo(k):
            h, r = k // 64, k % 64
            rs = min(max(r - 4, 0), 56)
            base = rs - r + 7
            return h, r, rs, base % 2, base // 2

        def emit_S(k):
            h, r, rs, tab, i0_ = geo(k)
            if r == 0 and h == 0:
                head_setup(0)
            if r == 16 and h + 1 < 16:
                head_setup(h + 1)
            d, kT, qT, Eb3 = hv[h][0], hv[h][1], hv[h][2], hv[h][7]
            sb_ = k % 2
            Sps = psf(sb_)
            for m in range(4):
                kr = rs + 2 * m
                sc.op("pe", lambda e, o_=Sps[:, m * 64:(m + 1) * 64], a=kT[:, kr * 64:kr * 64 + 128], b=qT[:, r * 64:(r + 1) * 64]:
                      e.matmul(o_, a, b, start=True, stop=False),
                      reads=[(d["kT"], 0), (d["qT"], 0)], writes=[("ps", sb_)], signal=False)
                sc.op("pe", lambda e, o_=Sps[:, m * 64:(m + 1) * 64], b=Eb3[:, tab, (i0_ + m) * 64:(i0_ + m + 1) * 64]:
                      e.matmul(o_, ident, b, start=False, stop=True),
                      reads=[KI, (d["Eb"], 0)], writes=[("ps", sb_)], signal=(m == 3))

        def emit_mid(k):
            h, r, rs, tab, i0_ = geo(k)
            sb_ = k % 2
            Sps = psf(sb_)
            pp = view(Bp[k % 2], BF16)
            sc.op("act", lambda e, o_=pp, i=Sps[:, 0:256]: e.activation(out=o_, in_=i, func=AF.Exp, scale=NA_SCALE),
                  reads=[("ps", sb_)], writes=[(Bp[k % 2], 0)])

        def emit_O(k):
            h, r, rs, tab, i0_ = geo(k)
            d, kT, qT, gT, aTh, v, vsh, E3 = hv[h]
            ob_ = 2 + k % 2
            Ops = psf(ob_)
            pp = view(Bp[k % 2], BF16)
            for m in range(4):
                kr = rs + 2 * m
                vc = v[:, kr // 2, :] if kr % 2 == 0 else vsh[:, (kr - 1) // 2, :]
                sc.op("pe", lambda e, o_=Ops[:, 0:64], a=vc, b=pp[:, m * 64:(m + 1) * 64], st=(m == 0), sp_=(m == 3): e.matmul(o_, a, b, start=st, stop=sp_),
                      reads=[(d["v"], 0), (d["vsh"], 0), (Bp[k % 2], 0)], writes=[("ps", ob_)], signal=False)
            for m in range(4):
                sc.op("pe", lambda e, o_=Ops[:, 64:128], b=pp[:, m * 64:(m + 1) * 64], st=(m == 0), sp_=(m == 3): e.matmul(o_, ones, b, start=st, stop=sp_),
                      reads=[KO, (Bp[k % 2], 0)], writes=[("ps", ob_)], signal=(m == 3))
            rc = view(Brc[k % 2], F32)
            sc.op("dve", lambda e, o_=rc, i=Ops[:, 64:128]: e.reciprocal(out=o_, in_=i), reads=[("ps", ob_)], writes=[(Brc[k % 2], 0)])
            sc.op("dve", lambda e, o_=rc, a=rc, b=gT[:, r * 64:(r + 1) * 64]: e.tensor_tensor(out=o_, in0=a, in1=b, op=ALU.mult),
                  reads=[(Brc[k % 2], 0), (d["gT"], 0)], writes=[(Brc[k % 2], 0)])
            sc.op("dve", lambda e, o_=aTh[:, r * 64:(r + 1) * 64], a=Ops[:, 0:64], b=rc: e.tensor_tensor(out=o_, in0=a, in1=b, op=ALU.mult),
                  reads=[("ps", ob_), (Brc[k % 2], 0)], writes=[(d["aT"], 0)])
            if r == 63:
                sc.dma("pool", aT[h * 128:(h + 1) * 128, :], aTh, reads=[(d["aT"], 0)], writes=[("aT", h)])

        emit_S(0); emit_S(1); emit_mid(0)
        for k in range(NIT):
            if k + 2 < NIT:
                emit_S(k + 2)
            if k + 1 < NIT:
                emit_mid(k + 1)
            emit_O(k)

    def phaseB2(l):
        o = 0
        def nb(nm, sz):
            nonlocal o
            b = sc.newbuf("B2_" + nm, o, sz); o += sz
            return b
        Brq, Brk, Brv, Brg, Bbt = nb("rq", 8192), nb("rk", 8192), nb("rv", 16384), nb("rg", 16384), nb("bt", 16384)
        BqT, BkT, BKf, BKb, BQf, BQb = [nb(n_, 8192) for n_ in ("qT", "kT", "Kf", "Kb", "Qf", "Qb")]
        BSf, BSb = nb("Sf", 16384), nb("Sb", 16384)
        Bqdf, Bqdb, BDT, Bgn = nb("qdf", 4096), nb("qdb", 4096), nb("DT", 4096), nb("gn", 8192)
        Btb = nb("tb", 512)
        Bstate = [nb("st%d" % i, 1024) for i in range(2)]
        Bt1, Bt2 = nb("t1", 512), nb("t2", 512)
        By = [nb("y%d" % i, 512) for i in range(2)]
        Bpt = [nb("pt%d" % i, 256) for i in range(2)]
        Bjk = nb("junk", 512)
        assert o <= top[0], (o, top[0])
        tb = view(Btb, F32)
        nldf, nldb, kdf, kdb, gcf, gcb = [tb[:, i * 8:(i + 1) * 8] for i in range(6)]
        qdf = view(Bqdf, F32, shape=(8, 128)); qdb = view(Bqdb, F32, shape=(8, 128)); DT = view(BDT, F32, shape=(8, 128))
        gnv = view(Bgn, F32)
        t1 = view(Bt1, F32); t2 = view(Bt2, F32)
        sc.dma("sp", nldf, ldf[l:l + 1, :].partition_broadcast(128), writes=[(Btb, "f")])
        sc.dma("sp", nldb, ldb[l:l + 1, :].partition_broadcast(128), writes=[(Btb, "b")])
        sc.dma("sp", gnv, gn[l:l + 1, :].partition_broadcast(128), writes=[(Bgn, 0)])
        for nm, ap_ in (("f", nldf), ("b", nldb)):
            sc.op("act", lambda e, o_=ap_: e.activation(out=o_, in_=o_, func=AF.Abs), reads=[(Btb, nm)], writes=[(Btb, nm)])
            sc.op("dve", lambda e, o_=ap_: e.tensor_scalar(out=o_, in0=o_, scalar1=-1.0, scalar2=None, op0=ALU.mult),
                  reads=[(Btb, nm)], writes=[(Btb, nm)])
        sc.op("act", lambda e: e.activation(out=kdf, in_=nldf, func=AF.Exp, scale=colA, bias=LN_RSCALE), reads=[(Btb, "f"), KC_], writes=[(Btb, "kdf")])
        sc.op("act", lambda e: e.activation(out=kdb, in_=nldb, func=AF.Exp, scale=colB, bias=LN_RSCALE), reads=[(Btb, "b"), KC_], writes=[(Btb, "kdb")])
        sc.op("act", lambda e: e.activation(out=gcf, in_=nldf, func=AF.Exp, scale=128.0), reads=[(Btb, "f")], writes=[(Btb, "gcf")])
        sc.op("act", lambda e: e.activation(out=gcb, in_=nldb, func=AF.Exp, scale=128.0), reads=[(Btb, "b")], writes=[(Btb, "gcb")])
        for h in range(8):
            sc.op("act", lambda e, o_=qdf[:, h, :], s_=nldf[:, h:h + 1]: e.activation(out=o_, in_=rowA, func=AF.Exp, scale=s_), reads=[(Btb, "f"), KC_], writes=[(Bqdf, h)])
            sc.op("act", lambda e, o_=qdb[:, h, :], s_=nldb[:, h:h + 1]: e.activation(out=o_, in_=rowB, func=AF.Exp, scale=s_), reads=[(Btb, "b"), KC_], writes=[(Bqdb, h)])
            sc.op("act", lambda e, s_=nldf[:, h:h + 1]: e.activation(out=t1, in_=M1, func=AF.Exp, scale=s_, bias=LN_RSCALE), reads=[(Btb, "f"), KC_], writes=[(Bt1, 0)])
            sc.op("dve", lambda e: e.tensor_tensor(out=t1, in0=t1, in1=mge, op=ALU.mult), reads=[(Bt1, 0), KC_], writes=[(Bt1, 0)])
            sc.op("act", lambda e, s_=nldb[:, h:h + 1]: e.activation(out=t2, in_=M2, func=AF.Exp, scale=s_, bias=LN_RSCALE), reads=[(Btb, "b"), KC_], writes=[(Bt2, 0)])
            sc.op("dve", lambda e: e.tensor_tensor(out=t2, in0=t2, in1=mle, op=ALU.mult), reads=[(Bt2, 0), KC_], writes=[(Bt2, 0)])
            sc.op("dve", lambda e, o_=DT[:, h, :]: e.tensor_tensor(out=o_, in0=t1, in1=t2, op=ALU.add), reads=[(Bt1, 0), (Bt2, 0)], writes=[(BDT, h)])
        pk = [0]
        for h in range(8):
            rqv = view(Brq, BF16, shape=(32, 128)); rkv = view(Brk, BF16, shape=(32, 128)); rvv = view(Brv, BF16, shape=(32, 256))
            rgv = view(Brg, BF16, shape=(2, S)); btv = view(Bbt, BF16, shape=(2, S))
            qT = view(BqT, BF16); kT = view(BkT, BF16)
            Kf = view(BKf, BF16, shape=(32, 128)); Kb = view(BKb, BF16, shape=(32, 128))
            Qf = view(BQf, BF16, shape=(32, 128)); Qb = view(BQb, BF16, shape=(32, 128))
            Sf = view(BSf, BF16, shape=(32, 256)); Sb = view(BSb, BF16, shape=(32, 256))
            sc.dma("sp", rqv, rq[:, h * 128:(h + 1) * 128].rearrange("(c p) d -> p c d", p=128), reads=tm_keys([16 + h // 4]), writes=[(Brq, 0)])
            sc.dma("sp", rkv, rk[:, h * 128:(h + 1) * 128].rearrange("(c p) d -> p c d", p=128), reads=tm_keys([18 + h // 4]), writes=[(Brk, 0)])
            sc.dma("sp", rvv, rv[:, h * 256:(h + 1) * 256].rearrange("(c p) d -> p c d", p=128), reads=tm_keys([20 + h // 2]), writes=[(Brv, 0)])
            sc.dma("sp", rgv, sF[6144 + h * 256:6144 + (h + 1) * 256, :].rearrange("(j p) t -> p j t", p=128),
                   reads=sF_keys(48 + 2 * h) + sF_keys(49 + 2 * h), writes=[(Brg, 0)])
            for (src3, srck, dstv, dstk) in ((rqv, Brq, qT, BqT), (rkv, Brk, kT, BkT)):
                for g in range(8):
                    pb = pk[0] % 2; pk[0] += 1
                    pv = psb(pb)
                    for q_ in range(4):
                        c = g * 4 + q_
                        sc.op("pe", lambda e, o_=pv[:, q_ * 128:(q_ + 1) * 128], i=src3[:, c, :]: e.transpose(out=o_, in_=i, identity=ident),
                              reads=[(srck, 0), KI], writes=[("ps", pb)], signal=(q_ == 3))
                    if g % 2 == 0:
                        sc.op("dve", lambda e, o_=dstv[:, g * 512:(g + 1) * 512], i=pv[:, 0:512]: e.tensor_copy(out=o_, in_=i), reads=[("ps", pb)], writes=[(dstk, g)])
                    else:
                        sc.op("act", lambda e, o_=dstv[:, g * 512:(g + 1) * 512], i=pv[:, 0:512]: e.copy(out=o_, in_=i), reads=[("ps", pb)], writes=[(dstk, g)])
            qTk = [(BqT, g) for g in range(8)]; kTk = [(BkT, g) for g in range(8)]
            rk2 = view(Brk, BF16)
            sc.op("dve", lambda e, o_=view(BKf, BF16), s_=kdf[:, h:h + 1]: e.tensor_scalar(out=o_, in0=rk2, scalar1=s_, scalar2=None, op0=ALU.mult),
                  reads=[(Brk, 0), (Btb, "kdf")], writes=[(BKf, 0)])
            sc.op("dve", lambda e, o_=view(BKb, BF16), s_=kdb[:, h:h + 1]: e.tensor_scalar(out=o_, in0=rk2, scalar1=s_, scalar2=None, op0=ALU.mult),
                  reads=[(Brk, 0), (Btb, "kdb")], writes=[(BKb, 0)])
            qT3 = view(BqT, BF16, shape=(32, 128))
            sc.op("dve", lambda e, b=qdf[:, h, :].unsqueeze(1).to_broadcast([128, 32, 128]): e.tensor_tensor(out=Qf, in0=qT3, in1=b, op=ALU.mult),
                  reads=qTk + [(Bqdf, h)], writes=[(BQf, 0)])
            sc.op("dve", lambda e, b=qdb[:, h, :].unsqueeze(1).to_broadcast([128, 32, 128]): e.tensor_tensor(out=Qb, in0=qT3, in1=b, op=ALU.mult),
                  reads=qTk + [(Bqdb, h)], writes=[(BQb, 0)])
            for (Kd, Kdk, Sd, Sdk, gc, order) in ((Kf, BKf, Sf, BSf, gcf, list(range(32))), (Kb, BKb, Sb, BSb, gcb, list(range(31, -1, -1)))):
                first = order[0]
                sc.op("dve", lambda e, o_=Sd[:, first, :]: e.memset(o_, 0.0), writes=[(Sdk, first)])
                prev_state = None
                for idx in range(31):
                    n = order[idx]; nxt = order[idx + 1]
                    pb = 4 + pk[0] % 2; pk[0] += 1
                    ps = psf(pb)
                    sc.op("pe", lambda e, o_=ps[:, 0:256], a=Kd[:, n, :], b=rvv[:, n, :]: e.matmul(o_, a, b, start=True, stop=True),
                          reads=[(Kdk, 0), (Brv, 0)], writes=[("ps", pb)])
                    bs = Bstate[idx % 2]
                    stv_ = view(bs, F32)
                    if prev_state is None:
                        sc.op("dve", lambda e, o_=stv_, i=ps[:, 0:256]: e.tensor_copy(out=o_, in_=i), reads=[("ps", pb)], writes=[(bs, 0)])
                    else:
                        pst = view(prev_state, F32)
                        sc.op("dve", lambda e, o_=stv_, a=pst, s_=gc[:, h:h + 1], b=ps[:, 0:256]: e.scalar_tensor_tensor(out=o_, in0=a, scalar=s_, in1=b, op0=ALU.mult, op1=ALU.add),
                              reads=[(prev_state, 0), ("ps", pb), (Btb, "gcf"), (Btb, "gcb")], writes=[(bs, 0)])
                    sc.op("act", lambda e, o_=Sd[:, nxt, :], i=stv_: e.copy(out=o_, in_=i), reads=[(bs, 0)], writes=[(Sdk, nxt)])
                    prev_state = bs
            ss_o = 16
            ssv = smallv[:, ss_o:ss_o + 1]; rsv = smallv[:, ss_o + 1:ss_o + 2]; rstd = smallv[:, ss_o + 2:ss_o + 3]
            jk = view(Bjk, BF16)

            def c_ST(n):
                pb = 2 + n % 2
                ps = psf(pb)
                sc.op("pe", lambda e, o_=ps[:, 0:128], a=kT[:, n * 128:(n + 1) * 128], b=qT[:, n * 128:(n + 1) * 128]: e.matmul(o_, a, b, start=True, stop=True),
                      reads=kTk + qTk, writes=[("ps", pb)])

            def c_pt(n):
                pb = 2 + n % 2
                ps = psf(pb)
                pt = view(Bpt[n % 2], BF16)
                sc.op("dve", lambda e, o_=pt, a=ps[:, 0:128], b=DT[:, h, :]: e.tensor_tensor(out=o_, in0=a, in1=b, op=ALU.mult),
                      reads=[("ps", pb), (BDT, h)], writes=[(Bpt[n % 2], 0)])

            def c_O(n):
                pt = view(Bpt[n % 2], BF16)
                ob = 6 + n % 2
                po = psf(ob)
                sc.op("pe", lambda e, o_=po[:, 0:256], a=pt, b=rvv[:, n, :]: e.matmul(o_, a, b, start=True, stop=False),
                      reads=[(Bpt[n % 2], 0), (Brv, 0)], writes=[("ps", ob)], signal=False)
                sc.op("pe", lambda e, o_=po[:, 0:256], a=Qf[:, n, :], b=Sf[:, n, :]: e.matmul(o_, a, b, start=False, stop=False),
                      reads=[(BQf, 0), (BSf, n)], writes=[("ps", ob)], signal=False)
                sc.op("pe", lambda e, o_=po[:, 0:256], a=Qb[:, n, :], b=Sb[:, n, :]: e.matmul(o_, a, b, start=False, stop=True),
                      reads=[(BQb, 0), (BSb, n)], writes=[("ps", ob)])
                sc.op("act", lambda e, i=po[:, 0:256]: e.activation(out=jk, in_=i, func=AF.Square, accum_out=ssv), reads=[("ps", ob)], writes=[(Bjk, 0), (B_small, "ss2")])
                sc.op("act", lambda e: e.activation(out=rsv, in_=ssv, func=AF.Sqrt, scale=1.0 / 256, bias=EPS), reads=[(B_small, "ss2")], writes=[(B_small, "rs2")])
                sc.op("dve", lambda e: e.reciprocal(out=rstd, in_=rsv), reads=[(B_small, "rs2")], writes=[(B_small, "rstd2")])
                yb = By[n % 2]
                yv = view(yb, BF16)
                sc.op("dve", lambda e, o_=yv, a=po[:, 0:256], b=gnv[:, h * 256:(h + 1) * 256]: e.scalar_tensor_tensor(out=o_, in0=a, scalar=rstd, in1=b, op0=ALU.mult, op1=ALU.mult),
                      reads=[("ps", ob), (B_small, "rstd2"), (Bgn, 0)], writes=[(yb, 0)])

            def c_TR(n):
                yb = By[n % 2]
                yv = view(yb, BF16)
                tb_ = n % 2
                pv = psb(tb_)
                for j in range(2):
                    sc.op("pe", lambda e, o_=pv[:, j * 128:(j + 1) * 128], i=yv[:, j * 128:(j + 1) * 128]: e.transpose(out=o_, in_=i, identity=ident),
                          reads=[(yb, 0), KI], writes=[("ps", tb_)], signal=(j == 1))
                sc.op("dve", lambda e, o_=btv[:, :, n * 128:(n + 1) * 128], a=pv[:, 0:256].rearrange("p (j t) -> p j t", j=2, t=128), b=rgv[:, :, n * 128:(n + 1) * 128]:
                      e.tensor_tensor(out=o_, in0=a, in1=b, op=ALU.mult),
                      reads=[("ps", tb_), (Brg, 0)], writes=[(Bbt, 0)])

            c_ST(0); c_ST(1); c_pt(0)
            for n in range(32):
                if n + 2 < 32:
                    c_ST(n + 2)
                if n + 1 < 32:
                    c_pt(n + 1)
                c_O(n)
                if n >= 1:
                    c_TR(n - 1)
            c_TR(31)
            sc.dma("pool", bT[h * 256:(h + 1) * 256, :].rearrange("(j p) t -> p j t", p=128), btv, reads=[(Bbt, 0)], writes=[("bT", h)])

    def phaseC(l, xsrc, xsrc_keys, ydst, ykey):
        T = 256
        o = 0
        def nb(nm, sz):
            nonlocal o
            b = sc.newbuf("C_" + nm, o, sz); o += sz
            return b
        Wh = [nb("wh%d" % i, 16384) for i in range(4)]
        Bwp = [nb("wple%d" % i, 2048) for i in range(2)]
        Bmo = nb("mo", 32768)
        Bxin = nb("xin", 16384)
        Blp = nb("lnpost", 16384)
        Bmg = nb("merged", 16384)
        Bab = nb("ab", 16384)
        Bx1b = nb("x1b", 8192)
        Bgt = [nb("gt%d" % i, 4096) for i in range(2)]
        Bm1 = [nb("m1_%d" % i, 1024) for i in range(2)]
        Bm2 = [nb("m2_%d" % i, 1024) for i in range(2)]
        Bsg = [nb("sg%d" % i, 2048) for i in range(2)]
        Bpl = nb("pl", 1024)
        Bplb = nb("plb", 512)
        BpT = nb("pT", 1024)
        Bjk = nb("junk", 1024)
        assert o <= top[0], (o, top[0])
        lpv = view(Blp, F32)
        for hh in range(2):
            sc.dma("sp", lpv[:, hh * 2048:(hh + 1) * 2048], lnpost[l:l + 1, hh * 2048:(hh + 1) * 2048].partition_broadcast(128), writes=[(Blp, hh)])
        mo = view(Bmo, F32, shape=(2, D))
        mg = view(Bmg, BF16, shape=(32, T))
        wi = [0]
        pi = [0]
        ssq_o = 32
        hi_ = [0]
        fi_ = [0]
        pli = [0]

        def wload(nm, cb):
            KC = dict((a, b // 128) for a, b, c in W_SPECS)[nm]
            if KC == 2:
                b = Bwp[pli[0] % 2]; pli[0] += 1
                sc.dma("sp", view(b, BF16, n=1024), wb[nm][l][cb], reads=wkeys(nm, l, cb), writes=[(b, 0)])
                return [(b, 0)], view(b, BF16, shape=(2, 512), n=1024)
            if KC == 16:
                b = Wh[hi_[0] % 4]; hi_[0] += 1
                sc.dma("sp", view(b, BF16, n=KC * 512), wb[nm][l][cb], reads=wkeys(nm, l, cb), writes=[(b, 0)])
                return [(b, 0)], view(b, BF16, shape=(KC, 512), n=KC * 512)
            pair = fi_[0] % 2; fi_[0] += 1
            b0, b1 = Wh[2 * pair], Wh[2 * pair + 1]
            esz = 2
            v_ = arena[:, b0.lo:b1.hi].bitcast(BF16)
            sc.dma("sp", v_, wb[nm][l][cb], reads=wkeys(nm, l, cb), writes=[(b0, 0), (b1, 0)])
            return [(b0, 0), (b1, 0)], v_.rearrange("p (a b) -> p a b", a=32, b=512)
        for tt in range(S // T):
            t0 = tt * T
            aTv = view(Bab, BF16, shape=(16, T), n=16 * T)
            bTv = view(Bab, BF16, shape=(16, T), off=16 * T, n=16 * T)
            sc.dma("sp", aTv, aT[:, t0:t0 + T].rearrange("(k p) t -> p k t", p=128), reads=[("aT", h) for h in range(16)], writes=[(Bab, 0)])
            sc.dma("sp", bTv, bT[:, t0:t0 + T].rearrange("(k p) t -> p k t", p=128), reads=[("bT", h) for h in range(8)], writes=[(Bab, 1)])
            if CSTOP < 1:
                continue
            for cb in range(8):
                sa, wa = wload("w_pa", cb)
                sb2, wbv = wload("w_pb", cb)
                gslot = Bgt[cb % 2]
                gv = view(gslot, BF16, shape=(2, 4, T))
                r0 = 8192 + cb * 512
                sc.dma("sp", gv[:, 0], sF[r0:r0 + 512, t0:t0 + T].rearrange("(s p) t -> p s t", p=128), reads=[("sF", r0 // 128 + s_, t0 // 512) for s_ in range(4)], writes=[(gslot, 0)])
                r1 = 12288 + cb * 512
                sc.dma("sp", gv[:, 1], sF[r1:r1 + 512, t0:t0 + T].rearrange("(s p) t -> p s t", p=128), reads=[("sF", r1 // 128 + s_, t0 // 512) for s_ in range(4)], writes=[(gslot, 1)])
                for sub in range(4):
                    fb = cb * 4 + sub
                    pa = 2 + pi[0] % 3; pbk = 5 + pi[0] % 3; pi[0] += 1
                    for kc in range(16):
                        sc.op("pe", lambda e, o_=psf(pa)[:, 0:T], a=wa[:, kc, sub * 128:(sub + 1) * 128], b=aTv[:, kc, :], st=(kc == 0), sp_=(kc == 15): e.matmul(o_, a, b, start=st, stop=sp_),
                              reads=sa + [(Bab, 0)], writes=[("ps", pa)], signal=(kc == 15))
                    for kc in range(16):
                        sc.op("pe", lambda e, o_=psf(pbk)[:, 0:T], a=wbv[:, kc, sub * 128:(sub + 1) * 128], b=bTv[:, kc, :], st=(kc == 0), sp_=(kc == 15): e.matmul(o_, a, b, start=st, stop=sp_),
                              reads=sb2 + [(Bab, 1)], writes=[("ps", pbk)], signal=(kc == 15))
                    m1 = view(Bm1[fb % 2], F32); m2 = view(Bm2[fb % 2], F32)
                    sc.op("dve", lambda e, o_=m1, a=psf(pa)[:, 0:T], b=gv[:, 0, sub, :]: e.tensor_tensor(out=o_, in0=a, in1=b, op=ALU.mult),
                          reads=[("ps", pa), (gslot, 0)], writes=[(Bm1[fb % 2], 0)])
                    sc.op("dve", lambda e, o_=m2, a=psf(pbk)[:, 0:T], b=gv[:, 1, sub, :]: e.tensor_tensor(out=o_, in0=a, in1=b, op=ALU.mult),
                          reads=[("ps", pbk), (gslot, 1)], writes=[(Bm2[fb % 2], 0)])
                    sc.op("dve", lambda e, o_=mg[:, fb, :], a=m1, b=m2: e.tensor_tensor(out=o_, in0=a, in1=b, op=ALU.add),
                          reads=[(Bm1[fb % 2], 0), (Bm2[fb % 2], 0)], writes=[(Bmg, fb)])
            mgk = [(Bmg, fb) for fb in range(32)]
            if CSTOP < 2:
                continue
            for cb in range(8):
                sw, wv = wload("w_out", cb)
                for j in range(2):
                    pb = 2 + pi[0] % 6; pi[0] += 1
                    ps = psf(pb)
                    for kc in range(32):
                        sc.op("pe", lambda e, o_=ps[:], a=mg[:, kc, j * 128:(j + 1) * 128], b=wv[:, kc, :], st=(kc == 0), sp_=(kc == 31): e.matmul(o_, a, b, start=st, stop=sp_),
                              reads=sw + mgk, writes=[("ps", pb)], signal=(kc == 31))
                    sc.op("dve", lambda e, o_=mo[:, j, cb * 512:(cb + 1) * 512], i=ps[:]: e.tensor_copy(out=o_, in_=i), reads=[("ps", pb)], writes=[(Bmo, (j, cb))])
                    sq = smallv[:, ssq_o + j * 8 + cb:ssq_o + j * 8 + cb + 1]
                    sc.op("act", lambda e, i=mo[:, j, cb * 512:(cb + 1) * 512], a=sq: e.activation(out=view(Bjk, BF16), in_=i, func=AF.Square, accum_out=a),
                          reads=[(Bmo, (j, cb))], writes=[(Bjk, 0), (B_small, ("ssq", j, cb))])
            if CSTOP < 3:
                continue
            for j in range(2):
                tok0 = t0 + j * 128
                tot = smallv[:, 48 + j:49 + j]; rs_ = smallv[:, 50 + j:51 + j]; rstd = smallv[:, 52 + j:53 + j]
                sc.op("dve", lambda e, o_=tot, i=smallv[:, ssq_o + j * 8:ssq_o + j * 8 + 8]: e.tensor_reduce(out=o_, in_=i, axis=mybir.AxisListType.X, op=ALU.add),
                      reads=[(B_small, ("ssq", j, cb)) for cb in range(8)], writes=[(B_small, ("tot", j))])
                sc.op("act", lambda e, o_=rs_, i=tot: e.activation(out=o_, in_=i, func=AF.Sqrt, scale=1.0 / D, bias=EPS), reads=[(B_small, ("tot", j))], writes=[(B_small, ("rs", j))])
                sc.op("dve", lambda e, o_=rstd, i=rs_: e.reciprocal(out=o_, in_=i), reads=[(B_small, ("rs", j))], writes=[(B_small, ("rstd", j))])
                xin = view(Bxin, F32)
                sc.dma("sp", xin, xsrc[tok0:tok0 + 128, :], reads=xsrc_keys(tok0), writes=[(Bxin, 0)])
                mok = [(Bmo, (j, cb)) for cb in range(8)]
                sc.op("dve", lambda e, o_=mo[:, j, :], s_=rstd: e.scalar_tensor_tensor(out=o_, in0=o_, scalar=s_, in1=lpv, op0=ALU.mult, op1=ALU.mult),
                      reads=mok + [(B_small, ("rstd", j)), (Blp, 0), (Blp, 1)], writes=mok)
                sc.op("dve", lambda e, o_=mo[:, j, :], b=xin: e.tensor_tensor(out=o_, in0=o_, in1=b, op=ALU.add), reads=mok + [(Bxin, 0)], writes=mok)
                x1b = view(Bx1b, BF16)
                sc.op("act", lambda e, i=mo[:, j, :]: e.copy(out=x1b, in_=i), reads=mok, writes=[(Bx1b, 0)])
                x1T = view(Bab, BF16, shape=(32, T))
                for g in range(8):
                    pb = g % 2
                    pv = psb(pb)
                    for q in range(4):
                        kc = g * 4 + q
                        sc.op("pe", lambda e, o_=pv[:, q * 128:(q + 1) * 128], i=x1b[:, kc * 128:(kc + 1) * 128]: e.transpose(out=o_, in_=i, identity=ident),
                              reads=[(Bx1b, 0), KI], writes=[("ps", pb)], signal=(q == 3))
                    dst = x1T[:, g * 4:(g + 1) * 4, j * 128:(j + 1) * 128]
                    srcp = pv[:, 0:512].rearrange("p (q t) -> p q t", q=4, t=128)
                    wkk = [(Bab, 0), (Bab, 1)]
                    if g % 2 == 0:
                        sc.op("dve", lambda e, o_=dst, i=srcp: e.tensor_copy(out=o_, in_=i), reads=[("ps", pb)], writes=wkk)
                    else:
                        sc.op("act", lambda e, o_=dst, i=srcp: e.copy(out=o_, in_=i), reads=[("ps", pb)], writes=wkk)
                pl = view(Bpl, F32, n=256)
                plb = view(Bplb, BF16, n=256)
                pT = view(BpT, BF16, shape=(2, T))
                sc.dma("sp", pl, p_in[l, tok0:tok0 + 128, :], writes=[(Bpl, 0)])
                sc.op("act", lambda e: e.copy(out=plb, in_=pl), reads=[(Bpl, 0)], writes=[(Bplb, 0)])
                pv = psb(1)
                for q in range(2):
                    sc.op("pe", lambda e, o_=pv[:, q * 128:(q + 1) * 128], i=plb[:, q * 128:(q + 1) * 128]: e.transpose(out=o_, in_=i, identity=ident),
                          reads=[(Bplb, 0), KI], writes=[("ps", 1)], signal=(q == 1))
                sc.op("dve", lambda e, o_=pT[:, :, j * 128:(j + 1) * 128], i=pv[:, 0:256].rearrange("p (q t) -> p q t", q=2, t=128): e.tensor_copy(out=o_, in_=i),
                      reads=[("ps", 1)], writes=[(BpT, j)])
            x1T = view(Bab, BF16, shape=(32, T))
            x1k = [(Bab, 0), (Bab, 1)]
            if CSTOP < 4:
                continue
            for cb in range(8):
                sg_, wg = wload("w_pg", cb)
                sp2, wp = wload("w_ple", cb)
                for j in range(2):
                    pb = 2 + pi[0] % 6; pi[0] += 1
                    ps = psf(pb)
                    for kc in range(32):
                        sc.op("pe", lambda e, o_=ps[:], a=x1T[:, kc, j * 128:(j + 1) * 128], b=wg[:, kc, :], st=(kc == 0), sp_=(kc == 31): e.matmul(o_, a, b, start=st, stop=sp_),
                              reads=sg_ + x1k, writes=[("ps", pb)], signal=(kc == 31))
                    pb2 = 2 + pi[0] % 6; pi[0] += 1
                    ps2 = psf(pb2)
                    pT = view(BpT, BF16, shape=(2, T))
                    for kc in range(2):
                        sc.op("pe", lambda e, o_=ps2[:], a=pT[:, kc, j * 128:(j + 1) * 128], b=wp[:, kc, :], st=(kc == 0), sp_=(kc == 1): e.matmul(o_, a, b, start=st, stop=sp_),
                              reads=sp2 + [(BpT, j)], writes=[("ps", pb2)], signal=(kc == 1))
                    sgb = Bsg[(cb * 2 + j) % 2]
                    sgv = view(sgb, F32)
                    sc.op("act", lambda e, o_=sgv, i=ps[:]: e.activation(out=o_, in_=i, func=AF.Sigmoid), reads=[("ps", pb)], writes=[(sgb, 0)])
                    sc.op("dve", lambda e, o_=sgv, b=ps2[:]: e.tensor_tensor(out=o_, in0=o_, in1=b, op=ALU.mult), reads=[(sgb, 0), ("ps", pb2)], writes=[(sgb, 0)])
                    xs_ = mo[:, j, cb * 512:(cb + 1) * 512]
                    sc.op("dve", lambda e, o_=xs_, b=sgv: e.tensor_tensor(out=o_, in0=o_, in1=b, op=ALU.add), reads=[(sgb, 0), (Bmo, (j, cb))], writes=[(Bmo, (j, cb))])
            for j in range(2):
                tok0 = t0 + j * 128
                sc.dma("pool", ydst[tok0:tok0 + 128, :], mo[:, j, :], reads=[(Bmo, (j, cb)) for cb in range(8)], writes=[(ykey, tok0 // 128)])

    layers = list(range(nlayers))
    if "0" in phases:
        phase0(layers)
    for l in layers:
        if l == 0:
            xs_ap, xs_keys = x_in, (lambda tok0: [])
        else:
            xs_ap, xs_keys = xmid, (lambda tok0: [("xmid", tok0 // 128)])
        if "A" in phases:
            phaseA(l, xs_ap, xs_keys)
        if "B" in phases:
            phaseB1(l)
            phaseB2(l)
        if "C" in phases:
            if l == nlayers - 1:
                phaseC(l, xs_ap, xs_keys, y_out, "y")
            else:
                phaseC(l, xs_ap, xs_keys, xmid, "xmid")
    sc.finish()
    block = es.enter_context(nc.Block())
    sc.emit(block)
    es.close()
    return nc, sc


def _const_tables():
    c = np.arange(128, dtype=np.float32)
    cst = np.zeros((128, CST_W), np.float32)
    cst[:, 0] = 127.0 - c
    cst[:, 1] = c
    t = np.arange(128, dtype=np.float32)
    cst[:, 2:130] = (t + 1.0)[None, :]
    cst[:, 130:258] = (128.0 - t)[None, :]
    s_ = c[:, None]
    tt = t[None, :]
    cst[:, 258:386] = np.maximum(tt - s_, 0.0)
    cst[:, 386:514] = np.maximum(s_ - tt, 0.0)
    cst[:, 514:642] = (tt >= s_).astype(np.float32)
    cst[:, 642:770] = (s_ >= tt).astype(np.float32)
    half = 64
    inv = (10000.0 ** (-np.arange(half, dtype=np.float32) / half)).astype(np.float32)
    ang = np.arange(S, dtype=np.float32)[:, None] * inv[None, :]
    rope = np.stack([np.cos(ang), np.sin(ang)]).astype(np.float32)
    cols = np.arange(64)
    cs = np.clip(cols - 8, 0, 48)
    valid = (cols[None, :] >= cs[:, None]) & (cols[None, :] < cs[:, None] + 16)
    m = valid.T.astype(np.float32)
    mask = np.tile(np.tile(m, (2, 1))[:, None, :], (1, 16, 1)).reshape(128, 1024)
    return cst, rope, np.ascontiguousarray(mask)


def _bias_table(rpb):
    cols = np.arange(64)
    cidx = np.clip(cols[:, None] - cols[None, :] + 15, 0, 30)
    out = np.zeros((NL, 16, 128, 2, 8, 64), np.float32)
    for tab in range(2):
        for i in range(8):
            for jp in range(2):
                ri = 2 * i + tab + jp
                if ri > 14:
                    continue
                out[:, :, jp * 64:(jp + 1) * 64, tab, i, :] = rpb[:, :, ri][:, :, cidx]
    return out.reshape(NL, 16, 128, 1024)


_CACHE = {}


def kernel(x_prompt, x_sample, p_prompt, p_sample, w_in, ln_pre, ln_post, na_rpb,
           ret_log_decay_fwd, ret_log_decay_bwd, ret_gn_gain, w_proj_a, w_proj_b,
           w_out, w_ple, w_ple_gate):
    if "nc" not in _CACHE:
        _CACHE["nc"] = build_program()[0]
    nc = _CACHE["nc"]
    f = lambda a: np.ascontiguousarray(np.asarray(a, dtype=np.float32))
    cst, rope, mask = _const_tables()
    shared = {
        "w_in": f(w_in), "w_pa": f(w_proj_a), "w_pb": f(w_proj_b), "w_out": f(w_out),
        "w_pg": f(w_ple_gate), "w_ple": f(w_ple),
        "ln_preT": np.ascontiguousarray(f(ln_pre).reshape(NL, 32, 128).transpose(0, 2, 1)),
        "ln_post": f(ln_post), "gn": f(ret_gn_gain), "ldf": f(ret_log_decay_fwd), "ldb": f(ret_log_decay_bwd),
        "bias_tab": _bias_table(f(na_rpb)), "mask_tab": mask, "cst": cst, "rope": rope,
        "ident": np.eye(128, dtype=np.float32).astype(ml_dtypes.bfloat16),
        "ones": np.ones((128, 128), np.float32).astype(ml_dtypes.bfloat16),
    }
    xp, xs_, pp, ps_ = f(x_prompt), f(x_sample), f(p_prompt), f(p_sample)
    seqs = [(xp[i], pp[:, i]) for i in range(4)] + [(xs_[i], ps_[:, i]) for i in range(2)]
    seqs = seqs + [seqs[0], seqs[1]]
    in_maps = []
    for c in range(8):
        d = dict(shared)
        d["x"] = np.ascontiguousarray(seqs[c][0])
        d["p"] = np.ascontiguousarray(seqs[c][1])
        in_maps.append(d)
    res = run_bass_kernel_spmd(nc, in_maps, core_ids=list(range(8)))
    ys = [np.asarray(res.results[c]["y"], dtype=np.float32) for c in range(6)]
    y_prompt = np.stack(ys[0:4])
    y_sample = np.stack(ys[4:6])
    return (y_prompt, y_sample)
```

```python
import math
import os
CSTOP = int(os.environ.get('CSTOP', '9'))
from contextlib import ExitStack
import numpy as np
import ml_dtypes
import concourse.bass as bass
import concourse.mybir as mybir
from concourse.bass_utils import run_bass_kernel_spmd

F32 = mybir.dt.float32
BF16 = mybir.dt.bfloat16
U8 = mybir.dt.uint8
AF = mybir.ActivationFunctionType
ALU = mybir.AluOpType

S = 4096
D = 4096
NL = 2
INW = 22528
EPS = 1e-6
NA_SCALE = 128.0 ** -0.5
LN_RSCALE = -0.5 * math.log(128.0)

class Buf:
    def __init__(self, name, lo, hi):
        self.name, self.lo, self.hi = name, lo, hi
        self.init_r = []
        self.dead = False


class Sched:
    K = 20

    def __init__(self, nc, es):
        self.nc = nc
        self.engs = {"pe": nc.tensor, "act": nc.scalar, "dve": nc.vector, "pool": nc.gpsimd, "sp": nc.sync}
        self.sem = {}
        self.semobj = {}
        for e in ["pe", "act", "dve", "pool"]:
            self.semobj["c_" + e] = es.enter_context(nc.semaphore("c_" + e))
            self.sem[e] = "c_" + e
        self.rings = {}
        for q in ["sp", "pool"]:
            self.rings[q] = []
            for i in range(self.K):
                nm = "d_%s%d" % (q, i)
                self.semobj[nm] = es.enter_context(nc.semaphore(nm))
                self.rings[q].append(nm)
        self.cnt = {e: 0 for e in self.sem}
        self.dman = {"sp": 0, "pool": 0}
        self.prog = {e: [] for e in self.engs}
        self.waited = {e: {} for e in self.engs}
        self.res = {}
        self.bufs = []
        self.grave = []
        self.nops = 0

    def newbuf(self, name, lo, size):
        b = Buf(name, lo, lo + size)
        for o in self.bufs:
            if not o.dead and o.lo < b.hi and b.lo < o.hi:
                fin = {}
                def add(ev):
                    if ev is not None and fin.get(ev[0], 0) < ev[1]:
                        fin[ev[0]] = ev[1]
                dk = []
                for k, st in self.res.items():
                    if k[0] is o:
                        add(st["w"])
                        for ev in st["r"]:
                            add(ev)
                        dk.append(k)
                for k in dk:
                    del self.res[k]
                for ev in o.init_r:
                    add(ev)
                o.final = list(fin.items())
                o.dead = True
                self.grave.append(o)
        self.bufs = [o for o in self.bufs if not o.dead]
        fin = {}
        for g in self.grave:
            if g.lo < b.hi and b.lo < g.hi:
                for s, v in g.final:
                    if fin.get(s, 0) < v:
                        fin[s] = v
        b.init_r = list(fin.items())
        self.bufs.append(b)
        return b

    def _st(self, key):
        st = self.res.get(key)
        if st is None:
            b = key[0]
            init = list(b.init_r) if isinstance(b, Buf) else []
            st = {"w": None, "r": init}
            self.res[key] = st
        if isinstance(key[0], Buf):
            assert not key[0].dead, key[0].name
        return st

    def _deps(self, eng, reads, writes):
        deps = {}
        def add(ev):
            if ev is None:
                return
            s, v = ev
            if deps.get(s, 0) < v:
                deps[s] = v
        for k in reads:
            add(self._st(k)["w"])
        for k in writes:
            st = self._st(k)
            add(st["w"])
            for ev in st["r"]:
                add(ev)
        own = self.sem.get(eng)
        wd = self.waited[eng]
        for s, v in deps.items():
            if eng == "pe" and s == own:
                continue
            if wd.get(s, 0) >= v:
                continue
            wd[s] = v
            self.prog[eng].append(("w", s, v))

    def _record(self, ev, reads, writes):
        for k in reads:
            st = self._st(k)
            rl = st["r"]
            for i, (s, v) in enumerate(rl):
                if s == ev[0]:
                    if v < ev[1]:
                        rl[i] = ev
                    break
            else:
                rl.append(ev)
        for k in writes:
            st = self._st(k)
            st["w"] = ev
            st["r"] = []

    def op(self, eng, fn, reads=(), writes=(), signal=True):
        self.nops += 1
        if any(k[0] == "ps" for k in reads):
            writes = list(writes) + [k for k in reads if k[0] == "ps"]
            reads = [k for k in reads if k[0] != "ps"]
        self._deps(eng, reads, writes)
        s = self.sem[eng]
        if signal:
            self.cnt[eng] += 1
            ev = (s, self.cnt[eng])
            self.prog[eng].append(("o", fn, s))
        else:
            ev = (s, self.cnt[eng] + 1)
            self.prog[eng].append(("o", fn, None))
        self._record(ev, reads, writes)

    def dma(self, q, out, in_, reads=(), writes=()):
        self.nops += 1
        n = self.dman[q]
        self.dman[q] += 1
        s = self.rings[q][n % self.K]
        prev = 16 * (n // self.K)
        wd = self.waited[q]
        if prev > 0 and wd.get(s, 0) < prev:
            wd[s] = prev
            self.prog[q].append(("w", s, prev))
        self._deps(q, reads, writes)
        self.prog[q].append(("d", out, in_, s))
        ev = (s, prev + 16)
        self._record(ev, reads, writes)

    def finish(self):
        for q in ["sp", "pool"]:
            n = self.dman[q]
            for i in range(min(n, self.K)):
                cnt = (n - 1 - i) // self.K + 1
                self.prog[q].append(("w", self.rings[q][i], 16 * cnt))
        n = self.dman["pool"]
        for i in range(min(n, self.K)):
            cnt = (n - 1 - i) // self.K + 1
            self.prog["sp"].append(("w", self.rings["pool"][i], 16 * cnt))

    def emit(self, block):
        so = self.semobj

        def replay(name):
            def run(e):
                for it in self.prog[name]:
                    if it[0] == "w":
                        e.wait_ge(so[it[1]], it[2])
                    elif it[0] == "o":
                        ins = it[1](e)
                        if it[2] is not None:
                            ins.then_inc(so[it[2]], 1)
                    else:
                        e.dma_start(out=it[1], in_=it[2]).then_inc(so[it[3]], 16)
            return run
        block.tensor(replay("pe"))
        block.scalar(replay("act"))
        block.vector(replay("dve"))
        block.gpsimd(replay("pool"))
        block.sync(replay("sp"))


W_SPECS = [
    ("w_in", 4096, INW), ("w_pa", 2048, 4096), ("w_pb", 2048, 4096),
    ("w_out", 4096, 4096), ("w_pg", 4096, 4096), ("w_ple", 256, 4096),
]

CST_W = 2 + 6 * 128


def build_program(debug=False, phases="0ABC", nlayers=NL):
    nc = bass.Bass("TRN2", target_bir_lowering=False)
    es = ExitStack()
    I = {}

    def din(name, shape, dt=F32):
        I[name] = nc.dram_tensor(name, list(shape), dt, kind="ExternalInput").ap()
        return I[name]

    x_in = din("x", [S, D])
    p_in = din("p", [NL, S, 256])
    wsrc = {}
    for nm, K, N in W_SPECS:
        wsrc[nm] = din(nm, [NL, K, N])
    lnpreT = din("ln_preT", [NL, 128, 32])
    lnpost = din("ln_post", [NL, D])
    gn = din("gn", [NL, 2048])
    ldf = din("ldf", [NL, 8])
    ldb = din("ldb", [NL, 8])
    bias_tab = din("bias_tab", [NL, 16, 128, 1024])
    mask_tab = din("mask_tab", [128, 1024])
    cst_in = din("cst", [128, CST_W])
    rope_in = din("rope", [2, S, 64])
    ident_in = din("ident", [128, 128], BF16)
    ones_in = din("ones", [128, 128], BF16)
    y_out = nc.dram_tensor("y", [S, D], F32, kind="ExternalOutput").ap()

    def dscr(name, shape, dt=BF16):
        kind = "ExternalOutput" if (debug and name in ("sF", "nav", "rq", "rk", "rv", "aT", "bT", "xmid")) else "Internal"
        return nc.dram_tensor(name, list(shape), dt, kind=kind).ap()

    wb = {}
    for nm, K, N in W_SPECS:
        wb[nm] = [dscr("wb_%s%d" % (nm, l_), [N // 512, 128, (K // 128) * 512]) for l_ in range(NL)]
    sF = dscr("sF", [16384, S])
    nav = dscr("nav", [S, 2048])
    rq = dscr("rq", [S, 1024])
    rk = dscr("rk", [S, 1024])
    rv = dscr("rv", [S, 2048])
    aT = dscr("aT", [2048, S])
    bT = dscr("bT", [2048, S])
    xmid = dscr("xmid", [S, D], F32)

    ARENA = 204 * 1024
    arena = es.enter_context(nc.sbuf_tensor("arena", [128, ARENA], U8))
    banks = [es.enter_context(nc.psum_tensor("psb%d" % i, [128, 512], F32)) for i in range(8)]
    sc = Sched(nc, es)

    def view(b, dt, shape=None, off=0, n=None):
        esz = 4 if dt == F32 else 2
        lo = b.lo + off * esz
        hi = b.hi if n is None else lo + n * esz
        assert hi <= b.hi
        v = arena[:, lo:hi].bitcast(dt)
        if shape is not None:
            if len(shape) == 2:
                v = v.rearrange("p (a b) -> p a b", a=shape[0], b=shape[1])
            else:
                v = v.rearrange("p (a b c) -> p a b c", a=shape[0], b=shape[1], c=shape[2])
        return v

    def psf(b):
        return banks[b]

    def psb(b):
        return banks[b][:].bitcast(BF16)

    top = [ARENA]

    def palloc(name, size):
        top[0] -= size
        return sc.newbuf(name, top[0], size)

    B_ident = palloc("ident", 256)
    B_ones = palloc("ones", 256)
    B_cst = palloc("cst", CST_W * 4 + 8)
    B_mask = palloc("mask", 4096)
    B_small = palloc("small", 2048)
    ident = view(B_ident, BF16)
    ones = view(B_ones, BF16)
    cst = view(B_cst, F32, n=CST_W)
    maskv = view(B_mask, F32)
    smallv = view(B_small, F32)

    sc.dma("sp", ident, ident_in, writes=[(B_ident, 0)])
    sc.dma("sp", ones, ones_in, writes=[(B_ones, 0)])
    sc.dma("sp", cst, cst_in, writes=[(B_cst, 0)])
    sc.dma("sp", maskv, mask_tab, writes=[(B_mask, 0)])
    KI, KO, KC_, KM = (B_ident, 0), (B_ones, 0), (B_cst, 0), (B_mask, 0)
    sc.op("dve", lambda e: e.tensor_scalar(out=maskv, in0=maskv, scalar1=30000.0, scalar2=-30000.0, op0=ALU.mult, op1=ALU.add),
          reads=[KM], writes=[KM])
    colA, colB = cst[:, 0:1], cst[:, 1:2]
    rowA, rowB = cst[:, 2:130], cst[:, 130:258]
    M1, M2 = cst[:, 258:386], cst[:, 386:514]
    mge, mle = cst[:, 514:642], cst[:, 642:770]

    _sm = [0]

    def small(n):
        o = _sm[0]
        _sm[0] += n
        assert _sm[0] <= 512
        return o

    cvn = [0]

    def phase0(layers):
        for l in layers:
            for nm, K, N in W_SPECS:
                KC = K // 128
                src = wsrc[nm][l].rearrange("(kc p) n -> p kc n", p=128)
                for cb in range(N // 512):
                    dst = wb[nm][l][cb].rearrange("p (kc n) -> p kc n", kc=KC, n=512)
                    sc.dma("pool", dst, src[:, :, cb * 512:(cb + 1) * 512], writes=[("wb", nm, l, cb), ("cvslot", cvn[0] % 3)])
                    cvn[0] += 1

    def wkeys(nm, l, cb):
        return [("wb", nm, l, cb)]

    def cb_info(cb):
        if cb < 4:
            return ("F", 0 + cb * 512, None)
        if cb < 8:
            return ("F", 2048 + (cb - 4) * 512, None)
        if cb < 12:
            return ("T", nav, (cb - 8) * 512, "copy")
        if cb < 16:
            return ("F", 4096 + (cb - 12) * 512, AF.Silu)
        if cb < 18:
            return ("T", rq, (cb - 16) * 512, "rot")
        if cb < 20:
            return ("T", rk, (cb - 18) * 512, "rot")
        if cb < 24:
            return ("T", rv, (cb - 20) * 512, "copy")
        if cb < 28:
            return ("F", 6144 + (cb - 24) * 512, AF.Silu)
        if cb < 36:
            return ("F", 8192 + (cb - 28) * 512, AF.Sigmoid)
        return ("F", 12288 + (cb - 36) * 512, AF.Sigmoid)

    def phaseA(l, xsrc, xsrc_keys):
        o = 0
        Wb = [sc.newbuf("A_w%d" % i, o + i * 32768, 32768) for i in range(2)]; o += 65536
        Bxx = [sc.newbuf("A_xnT%d" % i, o + i * 32768, 32768) for i in range(2)]; o += 65536
        Bxl = [sc.newbuf("A_xld%d" % i, o + i * 16384, 16384) for i in range(1)]; o += 16384
        Bxs = sc.newbuf("A_xs", o, 8192); o += 8192
        Brope = sc.newbuf("A_rope", o, 16384); o += 16384
        Bst = [sc.newbuf("A_st%d" % i, o + i * 4096, 4096) for i in range(3)]; o += 12288
        Btmp = [sc.newbuf("A_tmp%d" % i, o + i * 1024, 1024) for i in range(2)]; o += 2048
        Bg = sc.newbuf("A_g", o, 128); o += 128
        assert o <= top[0], (o, top[0])
        ropev = view(Brope, F32, shape=(2, 32, 64))
        gT = view(Bg, F32)
        sc.dma("sp", gT, lnpreT[l], writes=[(Bg, 0)])
        sc.dma("sp", ropev[:, 0], rope_in[0].rearrange("(t p) d -> p t d", p=128), writes=[(Brope, 0)])
        sc.dma("sp", ropev[:, 1], rope_in[1].rearrange("(t p) d -> p t d", p=128), writes=[(Brope, 1)])
        ss_o = small(8)
        psrot = [2, 3, 4, 5, 6, 7]
        pi = [0]
        sti = [0]
        evi = [0]
        ldi = [0]
        wl = [0]
        def pro_norm(tt, j):
            tok0 = tt * 512 + j * 128
            bl = Bxl[0]
            xl = view(bl, F32)
            xs = view(Bxs, BF16)
            sc.dma("sp", xl, xsrc[tok0:tok0 + 128, :], reads=xsrc_keys(tok0), writes=[(bl, 0)])
            ssv = smallv[:, ss_o:ss_o + 1]; rsv = smallv[:, ss_o + 1:ss_o + 2]; rstd = smallv[:, ss_o + 2:ss_o + 3]
            sc.op("act", lambda e, o_=xs, i=xl, a=ssv: e.activation(out=o_, in_=i, func=AF.Square, accum_out=a),
                  reads=[(bl, 0)], writes=[(Bxs, 0), (B_small, "ss")])
            sc.op("act", lambda e, o_=rsv, i=ssv: e.activation(out=o_, in_=i, func=AF.Sqrt, scale=1.0 / D, bias=EPS),
                  reads=[(B_small, "ss")], writes=[(B_small, "rs")])
            sc.op("dve", lambda e, o_=rstd, i=rsv: e.reciprocal(out=o_, in_=i), reads=[(B_small, "rs")], writes=[(B_small, "rstd")])
            sc.op("act", lambda e, o_=xs, i=xl, s_=rstd: e.activation(out=o_, in_=i, func=AF.Copy, scale=s_),
                  reads=[(bl, 0), (B_small, "rstd")], writes=[(Bxs, 0)])
        def pro_tr(tt, j):
            Bx = Bxx[tt % 2]
            xnT = view(Bx, BF16, shape=(32, 512))
            xs = view(Bxs, BF16)
            for g in range(8):
                pb = g % 2
                pv = psb(pb)
                for q in range(4):
                    kc = g * 4 + q
                    sc.op("pe", lambda e, o_=pv[:, q * 128:(q + 1) * 128], i=xs[:, kc * 128:(kc + 1) * 128]: e.transpose(out=o_, in_=i, identity=ident),
                          reads=[(Bxs, 0), KI], writes=[("ps", pb)], signal=(q == 3))
                for q in range(4):
                    kc = g * 4 + q
                    dst = xnT[:, kc, j * 128:(j + 1) * 128]
                    srcp = pv[:, q * 128:(q + 1) * 128]
                    if g % 2 == 0:
                        sc.op("dve", lambda e, o_=dst, i=srcp, s_=gT[:, kc:kc + 1]: e.tensor_scalar(out=o_, in0=i, scalar1=s_, scalar2=None, op0=ALU.mult),
                              reads=[("ps", pb), (Bg, 0)], writes=[(Bx, j)])
                    else:
                        sc.op("act", lambda e, o_=dst, i=srcp, s_=gT[:, kc:kc + 1]: e.activation(out=o_, in_=i, func=AF.Copy, scale=s_),
                              reads=[("ps", pb), (Bg, 0)], writes=[(Bx, j)])
        for tt in range(8):
            if tt == 0:
                for j in range(4):
                    pro_norm(0, j)
                    pro_tr(0, j)
            Bx = Bxx[tt % 2]
            xnT = view(Bx, BF16, shape=(32, 512))
            xkeys = [(Bx, j) for j in range(4)]
            for cb in range(INW // 512):
                info = cb_info(cb)
                gi = tt * 44 + cb
                while wl[0] <= min(gi + 1, 8 * 44 - 1):
                    g_ = wl[0]; wl[0] += 1
                    ws_ = Wb[g_ % 2]
                    sc.dma("sp", view(ws_, BF16), wb["w_in"][l][g_ % 44], reads=wkeys("w_in", l, g_ % 44), writes=[(ws_, 0)])
                wslot = Wb[gi % 2]
                wv = view(wslot, BF16, shape=(32, 512))
                bst = Bst[sti[0] % 3]; sti[0] += 1
                stv = view(bst, BF16, shape=(4, 512))
                for sub in range(4):
                    pb = psrot[pi[0] % 6]; pi[0] += 1
                    ps = psf(pb)
                    for kc in range(32):
                        if info[0] == "F":
                            lhsT, rhs = wv[:, kc, sub * 128:(sub + 1) * 128], xnT[:, kc, :]
                        else:
                            lhsT, rhs = xnT[:, kc, sub * 128:(sub + 1) * 128], wv[:, kc, :]
                        sc.op("pe", lambda e, o_=ps[:], a=lhsT, b=rhs, st=(kc == 0), sp_=(kc == 31): e.matmul(o_, a, b, start=st, stop=sp_),
                              reads=[(wslot, 0)] + xkeys, writes=[("ps", pb)], signal=(kc == 31))
                    dst = stv[:, sub, :]
                    if info[0] == "F" or info[3] == "copy":
                        fn = info[2] if info[0] == "F" else None
                        if fn is None:
                            if evi[0] % 2 == 0:
                                sc.op("dve", lambda e, o_=dst, i=ps[:]: e.tensor_copy(out=o_, in_=i), reads=[("ps", pb)], writes=[(bst, sub)])
                            else:
                                sc.op("act", lambda e, o_=dst, i=ps[:]: e.copy(out=o_, in_=i), reads=[("ps", pb)], writes=[(bst, sub)])
                            evi[0] += 1
                        else:
                            sc.op("act", lambda e, o_=dst, i=ps[:], f=fn: e.activation(out=o_, in_=i, func=f), reads=[("ps", pb)], writes=[(bst, sub)])
                    else:
                        ti = tt * 4 + sub
                        p3 = ps[:].rearrange("p (h d) -> p h d", h=4, d=128)
                        d3 = dst.rearrange("p (h d) -> p h d", h=4, d=128)
                        cosb = ropev[:, 0, ti, :].unsqueeze(1).to_broadcast([128, 4, 64])
                        sinb = ropev[:, 1, ti, :].unsqueeze(1).to_broadcast([128, 4, 64])
                        tA = view(Btmp[0], F32, shape=(4, 64)); tB = view(Btmp[1], F32, shape=(4, 64))
                        t1, t2 = p3[:, :, 0:64], p3[:, :, 64:128]
                        rk_ = [(Brope, 0), (Brope, 1)]
                        sc.op("dve", lambda e, o_=tA, a=t1, b=cosb: e.tensor_tensor(out=o_, in0=a, in1=b, op=ALU.mult), reads=[("ps", pb)] + rk_, writes=[(Btmp[0], 0)])
                        sc.op("dve", lambda e, o_=tB, a=t2, b=sinb: e.tensor_tensor(out=o_, in0=a, in1=b, op=ALU.mult), reads=[("ps", pb)] + rk_, writes=[(Btmp[1], 0)])
                        sc.op("dve", lambda e, o_=d3[:, :, 0:64], a=tA, b=tB: e.tensor_tensor(out=o_, in0=a, in1=b, op=ALU.subtract),
                              reads=[(Btmp[0], 0), (Btmp[1], 0)], writes=[(bst, sub)])
                        sc.op("dve", lambda e, o_=tA, a=t1, b=sinb: e.tensor_tensor(out=o_, in0=a, in1=b, op=ALU.mult), reads=[("ps", pb)] + rk_, writes=[(Btmp[0], 0)])
                        sc.op("dve", lambda e, o_=tB, a=t2, b=cosb: e.tensor_tensor(out=o_, in0=a, in1=b, op=ALU.mult), reads=[("ps", pb)] + rk_, writes=[(Btmp[1], 0)])
                        sc.op("dve", lambda e, o_=d3[:, :, 64:128], a=tA, b=tB: e.tensor_tensor(out=o_, in0=a, in1=b, op=ALU.add),
                              reads=[(Btmp[0], 0), (Btmp[1], 0)], writes=[(bst, sub)])
                skeys = [(bst, s_) for s_ in range(4)]
                if info[0] == "F":
                    r0 = info[1]
                    dstd = sF[r0:r0 + 512, tt * 512:(tt + 1) * 512].rearrange("(s p) t -> p s t", p=128)
                    wk = [("sF", r0 // 128 + s_, tt) for s_ in range(4)]
                else:
                    c0 = info[2]
                    dstd = info[1][tt * 512:(tt + 1) * 512, c0:c0 + 512].rearrange("(j p) c -> p j c", p=128)
                    wk = [("TM", cb, tt)]
                sc.dma("sp", dstd, stv, reads=skeys, writes=wk)
                if tt + 1 < 8:
                    if cb in (4, 12, 20, 28):
                        pro_norm(tt + 1, (cb - 4) // 8)
                    if cb in (8, 16, 24, 32):
                        pro_tr(tt + 1, (cb - 8) // 8)

    def sF_keys(rowblk):
        return [("sF", rowblk, tt) for tt in range(8)]

    def tm_keys(cbs):
        return [("TM", cb, tt) for cb in cbs for tt in range(8)]

    def phaseB1(l):
        o = 0
        sets = []
        for i in range(2):
            d = {}
            for nm, sz in [("kT", 8192), ("qT", 8192), ("v", 8192), ("vsh", 8192), ("gT", 8192), ("aT", 8192), ("E", 4096), ("Eb", 2048)]:
                d[nm] = sc.newbuf("B1_%s%d" % (nm, i), o, sz); o += sz
            sets.append(d)
        Bpe = [sc.newbuf("B1_pexp%d" % i, o + i * 1024, 1024) for i in range(2)]; o += 2048
        Bp = [sc.newbuf("B1_p%d" % i, o + i * 512, 512) for i in range(2)]; o += 1024
        Brc = [sc.newbuf("B1_rc%d" % i, o + i * 256, 256) for i in range(2)]; o += 512
        assert o <= top[0]
        NIT = 16 * 64
        hv = {}

        def head_setup(h):
            d = sets[h % 2]
            kT = view(d["kT"], BF16); qT = view(d["qT"], BF16); gT = view(d["gT"], BF16); aTh = view(d["aT"], BF16)
            v = view(d["v"], BF16, shape=(32, 128)); vsh = view(d["vsh"], BF16, shape=(32, 128))
            E = view(d["E"], F32)
            E3 = view(d["E"], F32, shape=(2, 512))
            sc.dma("sp", qT, sF[h * 128:(h + 1) * 128, :], reads=sF_keys(h), writes=[(d["qT"], 0)])
            sc.dma("sp", kT, sF[2048 + h * 128:2048 + (h + 1) * 128, :], reads=sF_keys(16 + h), writes=[(d["kT"], 0)])
            sc.dma("sp", gT, sF[4096 + h * 128:4096 + (h + 1) * 128, :], reads=sF_keys(32 + h), writes=[(d["gT"], 0)])
            vk = tm_keys([8 + h // 4])
            sc.dma("sp", v, nav[:, h * 128:(h + 1) * 128].rearrange("(c p) d -> p c d", p=128), reads=vk, writes=[(d["v"], 0)])
            sc.dma("sp", vsh[:, 0:31, :], nav[64:64 + 31 * 128, h * 128:(h + 1) * 128].rearrange("(c p) d -> p c d", p=128), reads=vk, writes=[(d["vsh"], 0)])
            sc.dma("sp", E, bias_tab[l, h], writes=[(d["E"], 0)])
            Eb = view(d["Eb"], BF16)
            Eb3 = view(d["Eb"], BF16, shape=(2, 512))
            sc.op("dve", lambda e, o_=E: e.tensor_scalar(out=o_, in0=o_, scalar1=1.0 / NA_SCALE, scalar2=None, op0=ALU.mult), reads=[(d["E"], 0)], writes=[(d["E"], 0)])
            sc.op("dve", lambda e, o_=Eb, i=E: e.tensor_tensor(out=o_, in0=i, in1=maskv, op=ALU.add), reads=[(d["E"], 0), KM], writes=[(d["Eb"], 0)])
            hv[h] = (d, kT, qT, gT, aTh, v, vsh, Eb3)

        def geo(k):
            h, r = k // 64, k % 64
            rs = min(max(r - 4, 0), 56)
            base = rs - r + 7
            return h, r, rs, base % 2, base // 2

        def emit_S(k):
            h, r, rs, tab, i0_ = geo(k)
            if r == 0 and h == 0:
                head_setup(0)
            if r == 16 and h + 1 < 16:
                head_setup(h + 1)
            d, kT, qT, Eb3 = hv[h][0], hv[h][1], hv[h][2], hv[h][7]
            sb_ = k % 2
            Sps = psf(sb_)
            for m in range(4):
                kr = rs + 2 * m
                sc.op("pe", lambda e, o_=Sps[:, m * 64:(m + 1) * 64], a=kT[:, kr * 64:kr * 64 + 128], b=qT[:, r * 64:(r + 1) * 64]:
                      e.matmul(o_, a, b, start=True, stop=False),
                      reads=[(d["kT"], 0), (d["qT"], 0)], writes=[("ps", sb_)], signal=False)
                sc.op("pe", lambda e, o_=Sps[:, m * 64:(m + 1) * 64], b=Eb3[:, tab, (i0_ + m) * 64:(i0_ + m + 1) * 64]:
                      e.matmul(o_, ident, b, start=False, stop=True),
                      reads=[KI, (d["Eb"], 0)], writes=[("ps", sb_)], signal=(m == 3))

        def emit_mid(k):
            h, r, rs, tab, i0_ = geo(k)
            sb_ = k % 2
            Sps = psf(sb_)
            pp = view(Bp[k % 2], BF16)
            sc.op("act", lambda e, o_=pp, i=Sps[:, 0:256]: e.activation(out=o_, in_=i, func=AF.Exp, scale=NA_SCALE),
                  reads=[("ps", sb_)], writes=[(Bp[k % 2], 0)])

        def emit_O(k):
            h, r, rs, tab, i0_ = geo(k)
            d, kT, qT, gT, aTh, v, vsh, E3 = hv[h]
            ob_ = 2 + k % 2
            Ops = psf(ob_)
            pp = view(Bp[k % 2], BF16)
            for m in range(4):
                kr = rs + 2 * m
                vc = v[:, kr // 2, :] if kr % 2 == 0 else vsh[:, (kr - 1) // 2, :]
                sc.op("pe", lambda e, o_=Ops[:, 0:64], a=vc, b=pp[:, m * 64:(m + 1) * 64], st=(m == 0), sp_=(m == 3): e.matmul(o_, a, b, start=st, stop=sp_),
                      reads=[(d["v"], 0), (d["vsh"], 0), (Bp[k % 2], 0)], writes=[("ps", ob_)], signal=False)
            for m in range(4):
                sc.op("pe", lambda e, o_=Ops[:, 64:128], b=pp[:, m * 64:(m + 1) * 64], st=(m == 0), sp_=(m == 3): e.matmul(o_, ones, b, start=st, stop=sp_),
                      reads=[KO, (Bp[k % 2], 0)], writes=[("ps", ob_)], signal=(m == 3))
            rc = view(Brc[k % 2], F32)
            sc.op("dve", lambda e, o_=rc, i=Ops[:, 64:128]: e.reciprocal(out=o_, in_=i), reads=[("ps", ob_)], writes=[(Brc[k % 2], 0)])
            sc.op("dve", lambda e, o_=rc, a=rc, b=gT[:, r * 64:(r + 1) * 64]: e.tensor_tensor(out=o_, in0=a, in1=b, op=ALU.mult),
                  reads=[(Brc[k % 2], 0), (d["gT"], 0)], writes=[(Brc[k % 2], 0)])
            sc.op("dve", lambda e, o_=aTh[:, r * 64:(r + 1) * 64], a=Ops[:, 0:64], b=rc: e.tensor_tensor(out=o_, in0=a, in1=b, op=ALU.mult),
                  reads=[("ps", ob_), (Brc[k % 2], 0)], writes=[(d["aT"], 0)])
            if r == 63:
                sc.dma("pool", aT[h * 128:(h + 1) * 128, :], aTh, reads=[(d["aT"], 0)], writes=[("aT", h)])

        emit_S(0); emit_S(1); emit_mid(0)
        for k in range(NIT):
            if k + 2 < NIT:
                emit_S(k + 2)
            if k + 1 < NIT:
                emit_mid(k + 1)
            emit_O(k)

    def phaseB2(l):
        o = 0
        def nb(nm, sz):
            nonlocal o
            b = sc.newbuf("B2_" + nm, o, sz); o += sz
            return b
        Brq, Brk, Brv, Brg, Bbt = nb("rq", 8192), nb("rk", 8192), nb("rv", 16384), nb("rg", 16384), nb("bt", 16384)
        BqT, BkT, BKf, BKb, BQf, BQb = [nb(n_, 8192) for n_ in ("qT", "kT", "Kf", "Kb", "Qf", "Qb")]
        BSf, BSb = nb("Sf", 16384), nb("Sb", 16384)
        Bqdf, Bqdb, BDT, Bgn = nb("qdf", 4096), nb("qdb", 4096), nb("DT", 4096), nb("gn", 8192)
        Btb = nb("tb", 512)
        Bstate = [nb("st%d" % i, 1024) for i in range(2)]
        Bt1, Bt2 = nb("t1", 512), nb("t2", 512)
        By = [nb("y%d" % i, 512) for i in range(2)]
        Bpt = [nb("pt%d" % i, 256) for i in range(2)]
        Bjk = nb("junk", 512)
        assert o <= top[0], (o, top[0])
        tb = view(Btb, F32)
        nldf, nldb, kdf, kdb, gcf, gcb = [tb[:, i * 8:(i + 1) * 8] for i in range(6)]
        qdf = view(Bqdf, F32, shape=(8, 128)); qdb = view(Bqdb, F32, shape=(8, 128)); DT = view(BDT, F32, shape=(8, 128))
        gnv = view(Bgn, F32)
        t1 = view(Bt1, F32); t2 = view(Bt2, F32)
        sc.dma("sp", nldf, ldf[l:l + 1, :].partition_broadcast(128), writes=[(Btb, "f")])
        sc.dma("sp", nldb, ldb[l:l + 1, :].partition_broadcast(128), writes=[(Btb, "b")])
        sc.dma("sp", gnv, gn[l:l + 1, :].partition_broadcast(128), writes=[(Bgn, 0)])
        for nm, ap_ in (("f", nldf), ("b", nldb)):
            sc.op("act", lambda e, o_=ap_: e.activation(out=o_, in_=o_, func=AF.Abs), reads=[(Btb, nm)], writes=[(Btb, nm)])
            sc.op("dve", lambda e, o_=ap_: e.tensor_scalar(out=o_, in0=o_, scalar1=-1.0, scalar2=None, op0=ALU.mult),
                  reads=[(Btb, nm)], writes=[(Btb, nm)])
        sc.op("act", lambda e: e.activation(out=kdf, in_=nldf, func=AF.Exp, scale=colA, bias=LN_RSCALE), reads=[(Btb, "f"), KC_], writes=[(Btb, "kdf")])
        sc.op("act", lambda e: e.activation(out=kdb, in_=nldb, func=AF.Exp, scale=colB, bias=LN_RSCALE), reads=[(Btb, "b"), KC_], writes=[(Btb, "kdb")])
        sc.op("act", lambda e: e.activation(out=gcf, in_=nldf, func=AF.Exp, scale=128.0), reads=[(Btb, "f")], writes=[(Btb, "gcf")])
        sc.op("act", lambda e: e.activation(out=gcb, in_=nldb, func=AF.Exp, scale=128.0), reads=[(Btb, "b")], writes=[(Btb, "gcb")])
        for h in range(8):
            sc.op("act", lambda e, o_=qdf[:, h, :], s_=nldf[:, h:h + 1]: e.activation(out=o_, in_=rowA, func=AF.Exp, scale=s_), reads=[(Btb, "f"), KC_], writes=[(Bqdf, h)])
            sc.op("act", lambda e, o_=qdb[:, h, :], s_=nldb[:, h:h + 1]: e.activation(out=o_, in_=rowB, func=AF.Exp, scale=s_), reads=[(Btb, "b"), KC_], writes=[(Bqdb, h)])
            sc.op("act", lambda e, s_=nldf[:, h:h + 1]: e.activation(out=t1, in_=M1, func=AF.Exp, scale=s_, bias=LN_RSCALE), reads=[(Btb, "f"), KC_], writes=[(Bt1, 0)])
            sc.op("dve", lambda e: e.tensor_tensor(out=t1, in0=t1, in1=mge, op=ALU.mult), reads=[(Bt1, 0), KC_], writes=[(Bt1, 0)])
            sc.op("act", lambda e, s_=nldb[:, h:h + 1]: e.activation(out=t2, in_=M2, func=AF.Exp, scale=s_, bias=LN_RSCALE), reads=[(Btb, "b"), KC_], writes=[(Bt2, 0)])
            sc.op("dve", lambda e: e.tensor_tensor(out=t2, in0=t2, in1=mle, op=ALU.mult), reads=[(Bt2, 0), KC_], writes=[(Bt2, 0)])
            sc.op("dve", lambda e, o_=DT[:, h, :]: e.tensor_tensor(out=o_, in0=t1, in1=t2, op=ALU.add), reads=[(Bt1, 0), (Bt2, 0)], writes=[(BDT, h)])
        pk = [0]
        for h in range(8):
            rqv = view(Brq, BF16, shape=(32, 128)); rkv = view(Brk, BF16, shape=(32, 128)); rvv = view(Brv, BF16, shape=(32, 256))
            rgv = view(Brg, BF16, shape=(2, S)); btv = view(Bbt, BF16, shape=(2, S))
            qT = view(BqT, BF16); kT = view(BkT, BF16)
            Kf = view(BKf, BF16, shape=(32, 128)); Kb = view(BKb, BF16, shape=(32, 128))
            Qf = view(BQf, BF16, shape=(32, 128)); Qb = view(BQb, BF16, shape=(32, 128))
            Sf = view(BSf, BF16, shape=(32, 256)); Sb = view(BSb, BF16, shape=(32, 256))
            sc.dma("sp", rqv, rq[:, h * 128:(h + 1) * 128].rearrange("(c p) d -> p c d", p=128), reads=tm_keys([16 + h // 4]), writes=[(Brq, 0)])
            sc.dma("sp", rkv, rk[:, h * 128:(h + 1) * 128].rearrange("(c p) d -> p c d", p=128), reads=tm_keys([18 + h // 4]), writes=[(Brk, 0)])
            sc.dma("sp", rvv, rv[:, h * 256:(h + 1) * 256].rearrange("(c p) d -> p c d", p=128), reads=tm_keys([20 + h // 2]), writes=[(Brv, 0)])
            sc.dma("sp", rgv, sF[6144 + h * 256:6144 + (h + 1) * 256, :].rearrange("(j p) t -> p j t", p=128),
                   reads=sF_keys(48 + 2 * h) + sF_keys(49 + 2 * h), writes=[(Brg, 0)])
            for (src3, srck, dstv, dstk) in ((rqv, Brq, qT, BqT), (rkv, Brk, kT, BkT)):
                for g in range(8):
                    pb = pk[0] % 2; pk[0] += 1
                    pv = psb(pb)
                    for q_ in range(4):
                        c = g * 4 + q_
                        sc.op("pe", lambda e, o_=pv[:, q_ * 128:(q_ + 1) * 128], i=src3[:, c, :]: e.transpose(out=o_, in_=i, identity=ident),
                              reads=[(srck, 0), KI], writes=[("ps", pb)], signal=(q_ == 3))
                    if g % 2 == 0:
                        sc.op("dve", lambda e, o_=dstv[:, g * 512:(g + 1) * 512], i=pv[:, 0:512]: e.tensor_copy(out=o_, in_=i), reads=[("ps", pb)], writes=[(dstk, g)])
                    else:
                        sc.op("act", lambda e, o_=dstv[:, g * 512:(g + 1) * 512], i=pv[:, 0:512]: e.copy(out=o_, in_=i), reads=[("ps", pb)], writes=[(dstk, g)])
            qTk = [(BqT, g) for g in range(8)]; kTk = [(BkT, g) for g in range(8)]
            rk2 = view(Brk, BF16)
            sc.op("dve", lambda e, o_=view(BKf, BF16), s_=kdf[:, h:h + 1]: e.tensor_scalar(out=o_, in0=rk2, scalar1=s_, scalar2=None, op0=ALU.mult),
                  reads=[(Brk, 0), (Btb, "kdf")], writes=[(BKf, 0)])
            sc.op("dve", lambda e, o_=view(BKb, BF16), s_=kdb[:, h:h + 1]: e.tensor_scalar(out=o_, in0=rk2, scalar1=s_, scalar2=None, op0=ALU.mult),
                  reads=[(Brk, 0), (Btb, "kdb")], writes=[(BKb, 0)])
            qT3 = view(BqT, BF16, shape=(32, 128))
            sc.op("dve", lambda e, b=qdf[:, h, :].unsqueeze(1).to_broadcast([128, 32, 128]): e.tensor_tensor(out=Qf, in0=qT3, in1=b, op=ALU.mult),
                  reads=qTk + [(Bqdf, h)], writes=[(BQf, 0)])
            sc.op("dve", lambda e, b=qdb[:, h, :].unsqueeze(1).to_broadcast([128, 32, 128]): e.tensor_tensor(out=Qb, in0=qT3, in1=b, op=ALU.mult),
                  reads=qTk + [(Bqdb, h)], writes=[(BQb, 0)])
            for (Kd, Kdk, Sd, Sdk, gc, order) in ((Kf, BKf, Sf, BSf, gcf, list(range(32))), (Kb, BKb, Sb, BSb, gcb, list(range(31, -1, -1)))):
                first = order[0]
                sc.op("dve", lambda e, o_=Sd[:, first, :]: e.memset(o_, 0.0), writes=[(Sdk, first)])
                prev_state = None
                for idx in range(31):
                    n = order[idx]; nxt = order[idx + 1]
                    pb = 4 + pk[0] % 2; pk[0] += 1
                    ps = psf(pb)
                    sc.op("pe", lambda e, o_=ps[:, 0:256], a=Kd[:, n, :], b=rvv[:, n, :]: e.matmul(o_, a, b, start=True, stop=True),
                          reads=[(Kdk, 0), (Brv, 0)], writes=[("ps", pb)])
                    bs = Bstate[idx % 2]
                    stv_ = view(bs, F32)
                    if prev_state is None:
                        sc.op("dve", lambda e, o_=stv_, i=ps[:, 0:256]: e.tensor_copy(out=o_, in_=i), reads=[("ps", pb)], writes=[(bs, 0)])
                    else:
                        pst = view(prev_state, F32)
                        sc.op("dve", lambda e, o_=stv_, a=pst, s_=gc[:, h:h + 1], b=ps[:, 0:256]: e.scalar_tensor_tensor(out=o_, in0=a, scalar=s_, in1=b, op0=ALU.mult, op1=ALU.add),
                              reads=[(prev_state, 0), ("ps", pb), (Btb, "gcf"), (Btb, "gcb")], writes=[(bs, 0)])
                    sc.op("act", lambda e, o_=Sd[:, nxt, :], i=stv_: e.copy(out=o_, in_=i), reads=[(bs, 0)], writes=[(Sdk, nxt)])
                    prev_state = bs
            ss_o = 16
            ssv = smallv[:, ss_o:ss_o + 1]; rsv = smallv[:, ss_o + 1:ss_o + 2]; rstd = smallv[:, ss_o + 2:ss_o + 3]
            jk = view(Bjk, BF16)

            def c_ST(n):
                pb = 2 + n % 2
                ps = psf(pb)
                sc.op("pe", lambda e, o_=ps[:, 0:128], a=kT[:, n * 128:(n + 1) * 128], b=qT[:, n * 128:(n + 1) * 128]: e.matmul(o_, a, b, start=True, stop=True),
                      reads=kTk + qTk, writes=[("ps", pb)])

            def c_pt(n):
                pb = 2 + n % 2
                ps = psf(pb)
                pt = view(Bpt[n % 2], BF16)
                sc.op("dve", lambda e, o_=pt, a=ps[:, 0:128], b=DT[:, h, :]: e.tensor_tensor(out=o_, in0=a, in1=b, op=ALU.mult),
                      reads=[("ps", pb), (BDT, h)], writes=[(Bpt[n % 2], 0)])

            def c_O(n):
                pt = view(Bpt[n % 2], BF16)
                ob = 6 + n % 2
                po = psf(ob)
                sc.op("pe", lambda e, o_=po[:, 0:256], a=pt, b=rvv[:, n, :]: e.matmul(o_, a, b, start=True, stop=False),
                      reads=[(Bpt[n % 2], 0), (Brv, 0)], writes=[("ps", ob)], signal=False)
                sc.op("pe", lambda e, o_=po[:, 0:256], a=Qf[:, n, :], b=Sf[:, n, :]: e.matmul(o_, a, b, start=False, stop=False),
                      reads=[(BQf, 0), (BSf, n)], writes=[("ps", ob)], signal=False)
                sc.op("pe", lambda e, o_=po[:, 0:256], a=Qb[:, n, :], b=Sb[:, n, :]: e.matmul(o_, a, b, start=False, stop=True),
                      reads=[(BQb, 0), (BSb, n)], writes=[("ps", ob)])
                sc.op("act", lambda e, i=po[:, 0:256]: e.activation(out=jk, in_=i, func=AF.Square, accum_out=ssv), reads=[("ps", ob)], writes=[(Bjk, 0), (B_small, "ss2")])
                sc.op("act", lambda e: e.activation(out=rsv, in_=ssv, func=AF.Sqrt, scale=1.0 / 256, bias=EPS), reads=[(B_small, "ss2")], writes=[(B_small, "rs2")])
                sc.op("dve", lambda e: e.reciprocal(out=rstd, in_=rsv), reads=[(B_small, "rs2")], writes=[(B_small, "rstd2")])
                yb = By[n % 2]
                yv = view(yb, BF16)
                sc.op("dve", lambda e, o_=yv, a=po[:, 0:256], b=gnv[:, h * 256:(h + 1) * 256]: e.scalar_tensor_tensor(out=o_, in0=a, scalar=rstd, in1=b, op0=ALU.mult, op1=ALU.mult),
                      reads=[("ps", ob), (B_small, "rstd2"), (Bgn, 0)], writes=[(yb, 0)])

            def c_TR(n):
                yb = By[n % 2]
                yv = view(yb, BF16)
                tb_ = n % 2
                pv = psb(tb_)
                for j in range(2):
                    sc.op("pe", lambda e, o_=pv[:, j * 128:(j + 1) * 128], i=yv[:, j * 128:(j + 1) * 128]: e.transpose(out=o_, in_=i, identity=ident),
                          reads=[(yb, 0), KI], writes=[("ps", tb_)], signal=(j == 1))
                sc.op("dve", lambda e, o_=btv[:, :, n * 128:(n + 1) * 128], a=pv[:, 0:256].rearrange("p (j t) -> p j t", j=2, t=128), b=rgv[:, :, n * 128:(n + 1) * 128]:
                      e.tensor_tensor(out=o_, in0=a, in1=b, op=ALU.mult),
                      reads=[("ps", tb_), (Brg, 0)], writes=[(Bbt, 0)])

            c_ST(0); c_ST(1); c_pt(0)
            for n in range(32):
                if n + 2 < 32:
                    c_ST(n + 2)
                if n + 1 < 32:
                    c_pt(n + 1)
                c_O(n)
                if n >= 1:
                    c_TR(n - 1)
            c_TR(31)
            sc.dma("pool", bT[h * 256:(h + 1) * 256, :].rearrange("(j p) t -> p j t", p=128), btv, reads=[(Bbt, 0)], writes=[("bT", h)])

    def phaseC(l, xsrc, xsrc_keys, ydst, ykey):
        T = 256
        o = 0
        def nb(nm, sz):
            nonlocal o
            b = sc.newbuf("C_" + nm, o, sz); o += sz
            return b
        Wh = [nb("wh%d" % i, 16384) for i in range(4)]
        Bwp = [nb("wple%d" % i, 2048) for i in range(2)]
        Bmo = nb("mo", 32768)
        Bxin = nb("xin", 16384)
        Blp = nb("lnpost", 16384)
        Bmg = nb("merged", 16384)
        Bab = nb("ab", 16384)
        Bx1b = nb("x1b", 8192)
        Bgt = [nb("gt%d" % i, 4096) for i in range(2)]
        Bm1 = [nb("m1_%d" % i, 1024) for i in range(2)]
        Bm2 = [nb("m2_%d" % i, 1024) for i in range(2)]
        Bsg = [nb("sg%d" % i, 2048) for i in range(2)]
        Bpl = nb("pl", 1024)
        Bplb = nb("plb", 512)
        BpT = nb("pT", 1024)
        Bjk = nb("junk", 1024)
        assert o <= top[0], (o, top[0])
        lpv = view(Blp, F32)
        for hh in range(2):
            sc.dma("sp", lpv[:, hh * 2048:(hh + 1) * 2048], lnpost[l:l + 1, hh * 2048:(hh + 1) * 2048].partition_broadcast(128), writes=[(Blp, hh)])
        mo = view(Bmo, F32, shape=(2, D))
        mg = view(Bmg, BF16, shape=(32, T))
        wi = [0]
        pi = [0]
        ssq_o = 32
        hi_ = [0]
        fi_ = [0]
        pli = [0]

        def wload(nm, cb):
            KC = dict((a, b // 128) for a, b, c in W_SPECS)[nm]
            if KC == 2:
                b = Bwp[pli[0] % 2]; pli[0] += 1
                sc.dma("sp", view(b, BF16, n=1024), wb[nm][l][cb], reads=wkeys(nm, l, cb), writes=[(b, 0)])
                return [(b, 0)], view(b, BF16, shape=(2, 512), n=1024)
            if KC == 16:
                b = Wh[hi_[0] % 4]; hi_[0] += 1
                sc.dma("sp", view(b, BF16, n=KC * 512), wb[nm][l][cb], reads=wkeys(nm, l, cb), writes=[(b, 0)])
                return [(b, 0)], view(b, BF16, shape=(KC, 512), n=KC * 512)
            pair = fi_[0] % 2; fi_[0] += 1
            b0, b1 = Wh[2 * pair], Wh[2 * pair + 1]
            esz = 2
            v_ = arena[:, b0.lo:b1.hi].bitcast(BF16)
            sc.dma("sp", v_, wb[nm][l][cb], reads=wkeys(nm, l, cb), writes=[(b0, 0), (b1, 0)])
            return [(b0, 0), (b1, 0)], v_.rearrange("p (a b) -> p a b", a=32, b=512)
        for tt in range(S // T):
            t0 = tt * T
            aTv = view(Bab, BF16, shape=(16, T), n=16 * T)
            bTv = view(Bab, BF16, shape=(16, T), off=16 * T, n=16 * T)
            sc.dma("sp", aTv, aT[:, t0:t0 + T].rearrange("(k p) t -> p k t", p=128), reads=[("aT", h) for h in range(16)], writes=[(Bab, 0)])
            sc.dma("sp", bTv, bT[:, t0:t0 + T].rearrange("(k p) t -> p k t", p=128), reads=[("bT", h) for h in range(8)], writes=[(Bab, 1)])
            if CSTOP < 1:
                continue
            for cb in range(8):
                sa, wa = wload("w_pa", cb)
                sb2, wbv = wload("w_pb", cb)
                gslot = Bgt[cb % 2]
                gv = view(gslot, BF16, shape=(2, 4, T))
                r0 = 8192 + cb * 512
                sc.dma("sp", gv[:, 0], sF[r0:r0 + 512, t0:t0 + T].rearrange("(s p) t -> p s t", p=128), reads=[("sF", r0 // 128 + s_, t0 // 512) for s_ in range(4)], writes=[(gslot, 0)])
                r1 = 12288 + cb * 512
                sc.dma("sp", gv[:, 1], sF[r1:r1 + 512, t0:t0 + T].rearrange("(s p) t -> p s t", p=128), reads=[("sF", r1 // 128 + s_, t0 // 512) for s_ in range(4)], writes=[(gslot, 1)])
                for sub in range(4):
                    fb = cb * 4 + sub
                    pa = 2 + pi[0] % 3; pbk = 5 + pi[0] % 3; pi[0] += 1
                    for kc in range(16):
                        sc.op("pe", lambda e, o_=psf(pa)[:, 0:T], a=wa[:, kc, sub * 128:(sub + 1) * 128], b=aTv[:, kc, :], st=(kc == 0), sp_=(kc == 15): e.matmul(o_, a, b, start=st, stop=sp_),
                              reads=sa + [(Bab, 0)], writes=[("ps", pa)], signal=(kc == 15))
                    for kc in range(16):
                        sc.op("pe", lambda e, o_=psf(pbk)[:, 0:T], a=wbv[:, kc, sub * 128:(sub + 1) * 128], b=bTv[:, kc, :], st=(kc == 0), sp_=(kc == 15): e.matmul(o_, a, b, start=st, stop=sp_),
                              reads=sb2 + [(Bab, 1)], writes=[("ps", pbk)], signal=(kc == 15))
                    m1 = view(Bm1[fb % 2], F32); m2 = view(Bm2[fb % 2], F32)
                    sc.op("dve", lambda e, o_=m1, a=psf(pa)[:, 0:T], b=gv[:, 0, sub, :]: e.tensor_tensor(out=o_, in0=a, in1=b, op=ALU.mult),
                          reads=[("ps", pa), (gslot, 0)], writes=[(Bm1[fb % 2], 0)])
                    sc.op("dve", lambda e, o_=m2, a=psf(pbk)[:, 0:T], b=gv[:, 1, sub, :]: e.tensor_tensor(out=o_, in0=a, in1=b, op=ALU.mult),
                          reads=[("ps", pbk), (gslot, 1)], writes=[(Bm2[fb % 2], 0)])
                    sc.op("dve", lambda e, o_=mg[:, fb, :], a=m1, b=m2: e.tensor_tensor(out=o_, in0=a, in1=b, op=ALU.add),
                          reads=[(Bm1[fb % 2], 0), (Bm2[fb % 2], 0)], writes=[(Bmg, fb)])
            mgk = [(Bmg, fb) for fb in range(32)]
            if CSTOP < 2:
                continue
            for cb in range(8):
                sw, wv = wload("w_out", cb)
                for j in range(2):
                    pb = 2 + pi[0] % 6; pi[0] += 1
                    ps = psf(pb)
                    for kc in range(32):
                        sc.op("pe", lambda e, o_=ps[:], a=mg[:, kc, j * 128:(j + 1) * 128], b=wv[:, kc, :], st=(kc == 0), sp_=(kc == 31): e.matmul(o_, a, b, start=st, stop=sp_),
                              reads=sw + mgk, writes=[("ps", pb)], signal=(kc == 31))
                    sc.op("dve", lambda e, o_=mo[:, j, cb * 512:(cb + 1) * 512], i=ps[:]: e.tensor_copy(out=o_, in_=i), reads=[("ps", pb)], writes=[(Bmo, (j, cb))])
                    sq = smallv[:, ssq_o + j * 8 + cb:ssq_o + j * 8 + cb + 1]
                    sc.op("act", lambda e, i=mo[:, j, cb * 512:(cb + 1) * 512], a=sq: e.activation(out=view(Bjk, BF16), in_=i, func=AF.Square, accum_out=a),
                          reads=[(Bmo, (j, cb))], writes=[(Bjk, 0), (B_small, ("ssq", j, cb))])
            if CSTOP < 3:
                continue
            for j in range(2):
                tok0 = t0 + j * 128
                tot = smallv[:, 48 + j:49 + j]; rs_ = smallv[:, 50 + j:51 + j]; rstd = smallv[:, 52 + j:53 + j]
                sc.op("dve", lambda e, o_=tot, i=smallv[:, ssq_o + j * 8:ssq_o + j * 8 + 8]: e.tensor_reduce(out=o_, in_=i, axis=mybir.AxisListType.X, op=ALU.add),
                      reads=[(B_small, ("ssq", j, cb)) for cb in range(8)], writes=[(B_small, ("tot", j))])
                sc.op("act", lambda e, o_=rs_, i=tot: e.activation(out=o_, in_=i, func=AF.Sqrt, scale=1.0 / D, bias=EPS), reads=[(B_small, ("tot", j))], writes=[(B_small, ("rs", j))])
                sc.op("dve", lambda e, o_=rstd, i=rs_: e.reciprocal(out=o_, in_=i), reads=[(B_small, ("rs", j))], writes=[(B_small, ("rstd", j))])
                xin = view(Bxin, F32)
                sc.dma("sp", xin, xsrc[tok0:tok0 + 128, :], reads=xsrc_keys(tok0), writes=[(Bxin, 0)])
                mok = [(Bmo, (j, cb)) for cb in range(8)]
                sc.op("dve", lambda e, o_=mo[:, j, :], s_=rstd: e.scalar_tensor_tensor(out=o_, in0=o_, scalar=s_, in1=lpv, op0=ALU.mult, op1=ALU.mult),
                      reads=mok + [(B_small, ("rstd", j)), (Blp, 0), (Blp, 1)], writes=mok)
                sc.op("dve", lambda e, o_=mo[:, j, :], b=xin: e.tensor_tensor(out=o_, in0=o_, in1=b, op=ALU.add), reads=mok + [(Bxin, 0)], writes=mok)
                x1b = view(Bx1b, BF16)
                sc.op("act", lambda e, i=mo[:, j, :]: e.copy(out=x1b, in_=i), reads=mok, writes=[(Bx1b, 0)])
                x1T = view(Bab, BF16, shape=(32, T))
                for g in range(8):
                    pb = g % 2
                    pv = psb(pb)
                    for q in range(4):
                        kc = g * 4 + q
                        sc.op("pe", lambda e, o_=pv[:, q * 128:(q + 1) * 128], i=x1b[:, kc * 128:(kc + 1) * 128]: e.transpose(out=o_, in_=i, identity=ident),
                              reads=[(Bx1b, 0), KI], writes=[("ps", pb)], signal=(q == 3))
                    dst = x1T[:, g * 4:(g + 1) * 4, j * 128:(j + 1) * 128]
                    srcp = pv[:, 0:512].rearrange("p (q t) -> p q t", q=4, t=128)
                    wkk = [(Bab, 0), (Bab, 1)]
                    if g % 2 == 0:
                        sc.op("dve", lambda e, o_=dst, i=srcp: e.tensor_copy(out=o_, in_=i), reads=[("ps", pb)], writes=wkk)
                    else:
                        sc.op("act", lambda e, o_=dst, i=srcp: e.copy(out=o_, in_=i), reads=[("ps", pb)], writes=wkk)
                pl = view(Bpl, F32, n=256)
                plb = view(Bplb, BF16, n=256)
                pT = view(BpT, BF16, shape=(2, T))
                sc.dma("sp", pl, p_in[l, tok0:tok0 + 128, :], writes=[(Bpl, 0)])
                sc.op("act", lambda e: e.copy(out=plb, in_=pl), reads=[(Bpl, 0)], writes=[(Bplb, 0)])
                pv = psb(1)
                for q in range(2):
                    sc.op("pe", lambda e, o_=pv[:, q * 128:(q + 1) * 128], i=plb[:, q * 128:(q + 1) * 128]: e.transpose(out=o_, in_=i, identity=ident),
                          reads=[(Bplb, 0), KI], writes=[("ps", 1)], signal=(q == 1))
                sc.op("dve", lambda e, o_=pT[:, :, j * 128:(j + 1) * 128], i=pv[:, 0:256].rearrange("p (q t) -> p q t", q=2, t=128): e.tensor_copy(out=o_, in_=i),
                      reads=[("ps", 1)], writes=[(BpT, j)])
            x1T = view(Bab, BF16, shape=(32, T))
            x1k = [(Bab, 0), (Bab, 1)]
            if CSTOP < 4:
                continue
            for cb in range(8):
                sg_, wg = wload("w_pg", cb)
                sp2, wp = wload("w_ple", cb)
                for j in range(2):
                    pb = 2 + pi[0] % 6; pi[0] += 1
                    ps = psf(pb)
                    for kc in range(32):
                        sc.op("pe", lambda e, o_=ps[:], a=x1T[:, kc, j * 128:(j + 1) * 128], b=wg[:, kc, :], st=(kc == 0), sp_=(kc == 31): e.matmul(o_, a, b, start=st, stop=sp_),
                              reads=sg_ + x1k, writes=[("ps", pb)], signal=(kc == 31))
                    pb2 = 2 + pi[0] % 6; pi[0] += 1
                    ps2 = psf(pb2)
                    pT = view(BpT, BF16, shape=(2, T))
                    for kc in range(2):
                        sc.op("pe", lambda e, o_=ps2[:], a=pT[:, kc, j * 128:(j + 1) * 128], b=wp[:, kc, :], st=(kc == 0), sp_=(kc == 1): e.matmul(o_, a, b, start=st, stop=sp_),
                              reads=sp2 + [(BpT, j)], writes=[("ps", pb2)], signal=(kc == 1))
                    sgb = Bsg[(cb * 2 + j) % 2]
                    sgv = view(sgb, F32)
                    sc.op("act", lambda e, o_=sgv, i=ps[:]: e.activation(out=o_, in_=i, func=AF.Sigmoid), reads=[("ps", pb)], writes=[(sgb, 0)])
                    sc.op("dve", lambda e, o_=sgv, b=ps2[:]: e.tensor_tensor(out=o_, in0=o_, in1=b, op=ALU.mult), reads=[(sgb, 0), ("ps", pb2)], writes=[(sgb, 0)])
                    xs_ = mo[:, j, cb * 512:(cb + 1) * 512]
                    sc.op("dve", lambda e, o_=xs_, b=sgv: e.tensor_tensor(out=o_, in0=o_, in1=b, op=ALU.add), reads=[(sgb, 0), (Bmo, (j, cb))], writes=[(Bmo, (j, cb))])
            for j in range(2):
                tok0 = t0 + j * 128
                sc.dma("pool", ydst[tok0:tok0 + 128, :], mo[:, j, :], reads=[(Bmo, (j, cb)) for cb in range(8)], writes=[(ykey, tok0 // 128)])

    layers = list(range(nlayers))
    if "0" in phases:
        phase0(layers)
    for l in layers:
        if l == 0:
            xs_ap, xs_keys = x_in, (lambda tok0: [])
        else:
            xs_ap, xs_keys = xmid, (lambda tok0: [("xmid", tok0 // 128)])
        if "A" in phases:
            phaseA(l, xs_ap, xs_keys)
        if "B" in phases:
            phaseB1(l)
            phaseB2(l)
        if "C" in phases:
            if l == nlayers - 1:
                phaseC(l, xs_ap, xs_keys, y_out, "y")
            else:
                phaseC(l, xs_ap, xs_keys, xmid, "xmid")
    sc.finish()
    block = es.enter_context(nc.Block())
    sc.emit(block)
    es.close()
    return nc, sc


def _const_tables():
    c = np.arange(128, dtype=np.float32)
    cst = np.zeros((128, CST_W), np.float32)
    cst[:, 0] = 127.0 - c
    cst[:, 1] = c
    t = np.arange(128, dtype=np.float32)
    cst[:, 2:130] = (t + 1.0)[None, :]
    cst[:, 130:258] = (128.0 - t)[None, :]
    s_ = c[:, None]
    tt = t[None, :]
    cst[:, 258:386] = np.maximum(tt - s_, 0.0)
    cst[:, 386:514] = np.maximum(s_ - tt, 0.0)
    cst[:, 514:642] = (tt >= s_).astype(np.float32)
    cst[:, 642:770] = (s_ >= tt).astype(np.float32)
    half = 64
    inv = (10000.0 ** (-np.arange(half, dtype=np.float32) / half)).astype(np.float32)
    ang = np.arange(S, dtype=np.float32)[:, None] * inv[None, :]
    rope = np.stack([np.cos(ang), np.sin(ang)]).astype(np.float32)
    cols = np.arange(64)
    cs = np.clip(cols - 8, 0, 48)
    valid = (cols[None, :] >= cs[:, None]) & (cols[None, :] < cs[:, None] + 16)
    m = valid.T.astype(np.float32)
    mask = np.tile(np.tile(m, (2, 1))[:, None, :], (1, 16, 1)).reshape(128, 1024)
    return cst, rope, np.ascontiguousarray(mask)


def _bias_table(rpb):
    cols = np.arange(64)
    cidx = np.clip(cols[:, None] - cols[None, :] + 15, 0, 30)
    out = np.zeros((NL, 16, 128, 2, 8, 64), np.float32)
    for tab in range(2):
        for i in range(8):
            for jp in range(2):
                ri = 2 * i + tab + jp
                if ri > 14:
                    continue
                out[:, :, jp * 64:(jp + 1) * 64, tab, i, :] = rpb[:, :, ri][:, :, cidx]
    return out.reshape(NL, 16, 128, 1024)


_CACHE = {}


def kernel(x_prompt, x_sample, p_prompt, p_sample, w_in, ln_pre, ln_post, na_rpb,
           ret_log_decay_fwd, ret_log_decay_bwd, ret_gn_gain, w_proj_a, w_proj_b,
           w_out, w_ple, w_ple_gate):
    if "nc" not in _CACHE:
        _CACHE["nc"] = build_program()[0]
    nc = _CACHE["nc"]
    f = lambda a: np.ascontiguousarray(np.asarray(a, dtype=np.float32))
    cst, rope, mask = _const_tables()
    shared = {
        "w_in": f(w_in), "w_pa": f(w_proj_a), "w_pb": f(w_proj_b), "w_out": f(w_out),
        "w_pg": f(w_ple_gate), "w_ple": f(w_ple),
        "ln_preT": np.ascontiguousarray(f(ln_pre).reshape(NL, 32, 128).transpose(0, 2, 1)),
        "ln_post": f(ln_post), "gn": f(ret_gn_gain), "ldf": f(ret_log_decay_fwd), "ldb": f(ret_log_decay_bwd),
        "bias_tab": _bias_table(f(na_rpb)), "mask_tab": mask, "cst": cst, "rope": rope,
        "ident": np.eye(128, dtype=np.float32).astype(ml_dtypes.bfloat16),
        "ones": np.ones((128, 128), np.float32).astype(ml_dtypes.bfloat16),
    }
    xp, xs_, pp, ps_ = f(x_prompt), f(x_sample), f(p_prompt), f(p_sample)
    seqs = [(xp[i], pp[:, i]) for i in range(4)] + [(xs_[i], ps_[:, i]) for i in range(2)]
    seqs = seqs + [seqs[0], seqs[1]]
    in_maps = []
    for c in range(8):
        d = dict(shared)
        d["x"] = np.ascontiguousarray(seqs[c][0])
        d["p"] = np.ascontiguousarray(seqs[c][1])
        in_maps.append(d)
    res = run_bass_kernel_spmd(nc, in_maps, core_ids=list(range(8)))
    ys = [np.asarray(res.results[c]["y"], dtype=np.float32) for c in range(6)]
    y_prompt = np.stack(ys[0:4])
    y_sample = np.stack(ys[4:6])
    return (y_prompt, y_sample)
```
